# Optimizing a Trainium2 kernel written in Bass

```python
import math
import jax, jax.numpy as jnp
from jax import lax
import numpy as np

D_MODEL = 1024
BATCH = 8
SEQ = 2048
DEPTH = 1
DEC_BATCH = 128
DEC_SEQ = 1
PAST_LEN = 16384
PAGE_SIZE = 128

D_MIX = D_MODEL
D_S5 = D_MIX // 2
S5_CH = 16
S5_GROUPS = D_S5 // S5_CH
S5_STATE = 64
D_POOL = D_MIX - D_S5
POOL_WINDOWS = (2, 4, 8, 16)
N_POOL = len(POOL_WINDOWS)
POOL_CH = D_POOL // N_POOL
POOL_BUF = max(POOL_WINDOWS) - 1
D_FF = ((8 * D_MODEL // 3 + 127) // 128) * 128
CONV_W = 3
EPS = 1e-6
DT_MIN = 1e-3
DT_MAX = 1e-1

kernel_name = "hymba_s5_pool_convffn_step"


def _rmsnorm(x, g):
    xf = x.astype(jnp.float32)
    y = xf * lax.rsqrt(jnp.mean(xf * xf, axis=-1, keepdims=True) + EPS)
    return (y * g.astype(jnp.float32)).astype(x.dtype)


def _lin_rec_op(e1, e2):
    a1, b1 = e1
    a2, b2 = e2
    return a1 * a2, a2 * b1 + b2


def _s5_mixer(u, h0_re, h0_im, a_re, a_im, log_dt, b_re, b_im, c_re, c_im, d_skip, w_glu):
    f32 = jnp.float32
    n, l, _ = u.shape
    uf = u.astype(f32).reshape(n, l, S5_GROUPS, S5_CH)
    a = lax.complex(a_re.astype(f32), a_im.astype(f32))
    dt_a = jnp.exp(log_dt.astype(f32))[:, None] * a
    a_bar = jnp.exp(dt_a)
    b = lax.complex(b_re.astype(f32), b_im.astype(f32))
    b_bar = ((a_bar - 1.0) / a)[:, :, None] * b
    bu = jnp.einsum('gph,nlgh->nlgp', b_bar, uf.astype(jnp.complex64))
    a_seq = jnp.broadcast_to(a_bar, bu.shape)
    _, h = lax.associative_scan(_lin_rec_op, (a_seq, bu), axis=1)
    steps = jnp.arange(1, l + 1, dtype=f32)[:, None, None]
    h0 = lax.complex(h0_re.astype(f32), h0_im.astype(f32))
    h = h + jnp.exp(steps * dt_a)[None] * h0[:, None]
    c = lax.complex(c_re.astype(f32), c_im.astype(f32))
    y = jnp.real(jnp.einsum('ghp,nlgp->nlgh', c, h)) + d_skip.astype(f32).reshape(S5_GROUPS, S5_CH) * uf
    y = jax.nn.gelu(y.reshape(n, l, D_S5), approximate=False)
    y = y * jax.nn.sigmoid(y @ w_glu.astype(f32))
    h_last = h[:, -1]
    return y.astype(u.dtype), jnp.real(h_last), jnp.imag(h_last)


def _pool_mixer(v, buf, n_past, w_pool, pool_scale):
    f32 = jnp.float32
    n, l, _ = v.shape
    seq = jnp.concatenate([buf.astype(f32), v.astype(f32)], axis=1)
    cs = jnp.concatenate([jnp.zeros((n, 1, D_POOL), f32), jnp.cumsum(seq, axis=1)], axis=1)
    end = cs[:, POOL_BUF + 1:]
    pos = n_past + jnp.arange(l, dtype=jnp.int32) + 1
    means = []
    for gi, w in enumerate(POOL_WINDOWS):
        sl = slice(gi * POOL_CH, (gi + 1) * POOL_CH)
        start = cs[:, POOL_BUF + 1 - w:POOL_BUF + 1 - w + l, sl]
        cnt = jnp.minimum(pos, w).astype(f32)[None, :, None]
        means.append((end[..., sl] - start) / cnt)
    pooled = jnp.concatenate(means, axis=-1) - seq[:, POOL_BUF:]
    z = jnp.einsum('nlgc,gcd->nlgd', pooled.reshape(n, l, N_POOL, POOL_CH), w_pool.astype(f32))
    out = z.reshape(n, l, D_POOL) * pool_scale.astype(f32)
    new_buf = seq[:, -POOL_BUF:]
    return out.astype(v.dtype), new_buf.astype(v.dtype)


def _conv_ffn(x, buf, w_up, conv_w, conv_b, w_down):
    l = x.shape[1]
    hup = x @ w_up
    seq = jnp.concatenate([buf.astype(hup.dtype), hup], axis=1)
    conv = conv_b + sum(conv_w[k] * seq[:, k:k + l] for k in range(CONV_W))
    gate, val = conv[..., :D_FF], conv[..., D_FF:]
    out = (jax.nn.gelu(gate, approximate=False) * val) @ w_down
    return out, seq[:, -(CONV_W - 1):]


def _layer(x, h0_re, h0_im, pool_buf, conv_buf, n_past,
           norm_mix_g, w_in, s5_a_re, s5_a_im, s5_log_dt, s5_b_re, s5_b_im,
           s5_c_re, s5_c_im, s5_d, s5_w_glu, pool_w, pool_scale, w_out,
           norm_ffn_g, ffn_w_up, ffn_conv_w, ffn_conv_b, ffn_w_down):
    h = _rmsnorm(x, norm_mix_g)
    proj = h @ w_in
    u, v = proj[..., :D_S5], proj[..., D_S5:]
    y_s5, new_re, new_im = _s5_mixer(u, h0_re, h0_im, s5_a_re, s5_a_im, s5_log_dt,
                                     s5_b_re, s5_b_im, s5_c_re, s5_c_im, s5_d, s5_w_glu)
    y_pool, new_pool = _pool_mixer(v, pool_buf, n_past, pool_w, pool_scale)
    x = x + jnp.concatenate([y_s5, y_pool], axis=-1) @ w_out
    h = _rmsnorm(x, norm_ffn_g)
    y_ffn, new_conv = _conv_ffn(h, conv_buf, ffn_w_up, ffn_conv_w, ffn_conv_b, ffn_w_down)
    x = x + y_ffn
    return x, new_re, new_im, new_pool, new_conv


def setup_inputs(seed: int = 0) -> dict:
    key = jax.random.key(seed)
    ks = jax.random.split(key, 32)
    f32 = jnp.float32
    nrm = lambda k, s, sc: jax.random.normal(k, s, f32) * sc
    n_idx = jnp.arange(S5_STATE, dtype=f32)
    return {
        "x_prompt": nrm(ks[0], (BATCH, SEQ, D_MODEL), 1.0),
        "x_sample": nrm(ks[1], (DEC_BATCH, DEC_SEQ, D_MODEL), 1.0),
        "state_s5_re": nrm(ks[2], (DEPTH, DEC_BATCH, S5_GROUPS, S5_STATE), 0.3),
        "state_s5_im": nrm(ks[3], (DEPTH, DEC_BATCH, S5_GROUPS, S5_STATE), 0.3),
        "state_pool": nrm(ks[4], (DEPTH, DEC_BATCH, POOL_BUF, D_POOL), 1.0),
        "state_ffn_conv": nrm(ks[5], (DEPTH, DEC_BATCH, CONV_W - 1, 2 * D_FF), 1.0),
        "norm_mix_g": 1.0 + nrm(ks[6], (DEPTH, D_MODEL), 0.02),
        "w_in": nrm(ks[7], (DEPTH, D_MODEL, D_MIX), D_MODEL ** -0.5),
        "s5_a_re": -0.5 + nrm(ks[8], (DEPTH, S5_GROUPS, S5_STATE), 0.01),
        "s5_a_im": math.pi * n_idx + nrm(ks[9], (DEPTH, S5_GROUPS, S5_STATE), 0.01),
        "s5_log_dt": jax.random.uniform(ks[10], (DEPTH, S5_GROUPS), f32,
                                         math.log(DT_MIN), math.log(DT_MAX)),
        "s5_b_re": nrm(ks[11], (DEPTH, S5_GROUPS, S5_STATE, S5_CH), (2 * S5_CH) ** -0.5),
        "s5_b_im": nrm(ks[12], (DEPTH, S5_GROUPS, S5_STATE, S5_CH), (2 * S5_CH) ** -0.5),
        "s5_c_re": nrm(ks[13], (DEPTH, S5_GROUPS, S5_CH, S5_STATE), (2 * S5_STATE) ** -0.5),
        "s5_c_im": nrm(ks[14], (DEPTH, S5_GROUPS, S5_CH, S5_STATE), (2 * S5_STATE) ** -0.5),
        "s5_d": nrm(ks[15], (DEPTH, D_S5), 1.0),
        "s5_w_glu": nrm(ks[16], (DEPTH, D_S5, D_S5), D_S5 ** -0.5),
        "pool_w": nrm(ks[17], (DEPTH, N_POOL, POOL_CH, POOL_CH), POOL_CH ** -0.5),
        "pool_scale": 1.0 + nrm(ks[18], (DEPTH, D_POOL), 0.02),
        "w_out": nrm(ks[19], (DEPTH, D_MIX, D_MODEL), D_MIX ** -0.5),
        "norm_ffn_g": 1.0 + nrm(ks[20], (DEPTH, D_MODEL), 0.02),
        "ffn_w_up": nrm(ks[21], (DEPTH, D_MODEL, 2 * D_FF), D_MODEL ** -0.5),
        "ffn_conv_w": nrm(ks[22], (DEPTH, CONV_W, 2 * D_FF), CONV_W ** -0.5),
        "ffn_conv_b": nrm(ks[23], (DEPTH, 2 * D_FF), 0.02),
        "ffn_w_down": nrm(ks[24], (DEPTH, D_FF, D_MODEL), D_FF ** -0.5),
        "norm_final_g": 1.0 + nrm(ks[25], (D_MODEL,), 0.02),
    }


def reference(x_prompt, x_sample, state_s5_re, state_s5_im, state_pool, state_ffn_conv,
              norm_mix_g, w_in, s5_a_re, s5_a_im, s5_log_dt, s5_b_re, s5_b_im,
              s5_c_re, s5_c_im, s5_d, s5_w_glu, pool_w, pool_scale, w_out,
              norm_ffn_g, ffn_w_up, ffn_conv_w, ffn_conv_b, ffn_w_down, norm_final_g):
    f32 = jnp.float32
    xp, xs = x_prompt, x_sample
    p_re, p_im, p_pool, p_conv = [], [], [], []
    s_re, s_im, s_pool, s_conv = [], [], [], []
    for i in range(DEPTH):
        w = (norm_mix_g[i], w_in[i], s5_a_re[i], s5_a_im[i], s5_log_dt[i], s5_b_re[i],
             s5_b_im[i], s5_c_re[i], s5_c_im[i], s5_d[i], s5_w_glu[i], pool_w[i],
             pool_scale[i], w_out[i], norm_ffn_g[i], ffn_w_up[i], ffn_conv_w[i],
             ffn_conv_b[i], ffn_w_down[i])
        zs = jnp.zeros((BATCH, S5_GROUPS, S5_STATE), f32)
        zp = jnp.zeros((BATCH, POOL_BUF, D_POOL), xp.dtype)
        zc = jnp.zeros((BATCH, CONV_W - 1, 2 * D_FF), xp.dtype)
        xp, a, b, c, d = _layer(xp, zs, zs, zp, zc, 0, *w)
        p_re.append(a); p_im.append(b); p_pool.append(c); p_conv.append(d)
        xs, a, b, c, d = _layer(xs, state_s5_re[i], state_s5_im[i], state_pool[i],
                                state_ffn_conv[i], PAST_LEN, *w)
        s_re.append(a); s_im.append(b); s_pool.append(c); s_conv.append(d)
    y_prompt = _rmsnorm(xp, norm_final_g)
    y_sample = _rmsnorm(xs, norm_final_g)
    return (y_prompt, y_sample,
            jnp.stack(p_re), jnp.stack(p_im), jnp.stack(p_pool), jnp.stack(p_conv),
            jnp.stack(s_re), jnp.stack(s_im), jnp.stack(s_pool), jnp.stack(s_conv))
```

```python
import math
from contextlib import ExitStack

import numpy as np
import concourse.bass as bass
import concourse.mybir as mybir
from concourse.bass_utils import run_bass_kernel_spmd

F32 = mybir.dt.float32
BF16 = mybir.dt.bfloat16
AF = mybir.ActivationFunctionType
ALU = mybir.AluOpType
AX = mybir.AxisListType


class Buf:
    __slots__ = ("name", "writer", "readers")

    def __init__(self, name):
        self.name = name
        self.writer = None
        self.readers = []


class Op:
    __slots__ = ("eng", "fn", "kind", "deps", "signal", "tok", "waits", "clock", "ndma", "idx")


class Sched:
    ENGS = ("pe", "act", "dve", "pool", "sp")
    NRING = 40

    def __init__(self, nc):
        self.nc = nc
        self.ops = []
        self.ring_n = 0
        self.ring_cum = [0] * self.NRING
        self.ring_last = [None] * self.NRING
        self.out_dmas = []
        self.n_sw = 0

    def buf(self, name):
        return Buf(name)

    def _add(self, eng, fn, reads, writes, kind, ndma=0):
        op = Op()
        op.eng, op.fn, op.kind, op.ndma = eng, fn, kind, ndma
        op.signal = False
        op.idx = len(self.ops)
        deps = []
        for b in reads:
            if b.writer is not None:
                deps.append(b.writer)
        for b in writes:
            if b.writer is not None:
                deps.append(b.writer)
            deps.extend(b.readers)
        for b in reads:
            b.readers.append(op)
        for b in writes:
            b.writer = op
            b.readers = []
        if kind == "dma" and eng == "pool":
            op.tok = (("sw", self.n_sw), 16 * ndma)
            self.n_sw += 1
        elif kind == "dma":
            r = self.ring_n % self.NRING
            self.ring_n += 1
            if self.ring_last[r] is not None:
                deps.append(self.ring_last[r])
            self.ring_cum[r] += 16 * ndma
            op.tok = (("dma", r), self.ring_cum[r])
            self.ring_last[r] = op
        else:
            op.tok = None
        op.deps = [d for d in set(deps) if not (d.eng == "pe" and eng == "pe" and d.kind == "c" and kind == "c") and d is not op]
        self.ops.append(op)
        return op

    def c(self, eng, fn, reads=(), writes=()):
        return self._add(eng, fn, list(reads), list(writes), "c")

    def dma(self, eng, fn, reads=(), writes=(), ndma=1, out=False):
        op = self._add(eng, fn, list(reads), list(writes), "dma", ndma)
        if out:
            self.out_dmas.append(op)
        return op

    def emit(self, stack):
        nc = self.nc
        fin = self._add("sp", None, [], [], "c")
        fin.deps = list(self.out_dmas)
        for op in self.ops:
            for d in op.deps:
                d.signal = True
        cnt = {e: 0 for e in self.ENGS}
        for op in self.ops:
            if op.kind == "c" and op.signal:
                cnt[op.eng] += 1
                op.tok = (op.eng, cnt[op.eng])
        known = {e: {} for e in self.ENGS}
        for op in self.ops:
            kn = known[op.eng]
            waits = []
            for d in sorted(op.deps, key=lambda o: o.idx):
                k, v = d.tok
                if kn.get(k, 0) >= v:
                    continue
                waits.append((k, v))
                for kk, vv in d.clock.items():
                    if kn.get(kk, 0) < vv:
                        kn[kk] = vv
            best = {}
            for k, v in waits:
                best[k] = max(best.get(k, 0), v)
            op.waits = list(best.items())
            op.clock = dict(kn)
            if op.tok is not None:
                op.clock[op.tok[0]] = max(op.clock.get(op.tok[0], 0), op.tok[1])
        sems = {}
        for e in ("pe", "act", "dve", "pool", "sp"):
            sems[e] = stack.enter_context(nc.semaphore("s_" + e))
        for r in range(min(self.NRING, max(self.ring_n, 1))):
            sems[("dma", r)] = stack.enter_context(nc.semaphore("s_dma%d" % r))
        for r in range(self.n_sw):
            sems[("sw", r)] = stack.enter_context(nc.semaphore("s_sw%d" % r))
        block = stack.enter_context(nc.Block())
        per = {e: [o for o in self.ops if o.eng == e] for e in self.ENGS}
        self.stats = {e: (len(per[e]), sum(len(o.waits) for o in per[e])) for e in self.ENGS}

        def run(eng_handle, lst):
            for op in lst:
                for k, v in op.waits:
                    eng_handle.wait_ge(sems[k], v)
                if op.fn is None:
                    continue
                res = op.fn(eng_handle)
                if op.kind == "dma":
                    if not isinstance(res, (list, tuple)):
                        res = [res]
                    assert len(res) == op.ndma, (len(res), op.ndma)
                    for ins in res:
                        ins.then_inc(sems[op.tok[0]], 16)
                elif op.signal:
                    if isinstance(res, (list, tuple)):
                        res = res[-1]
                    res.then_inc(sems[op.tok[0]], 1)

        @block.tensor
        def _(e):
            run(e, per["pe"])

        @block.scalar
        def _(e):
            run(e, per["act"])

        @block.vector
        def _(e):
            run(e, per["dve"])

        @block.gpsimd
        def _(e):
            run(e, per["pool"])

        @block.sync
        def _(e):
            run(e, per["sp"])


NCORES = 8
D = 1024
L = 2048
BT = 512
NBLK = L // BT
NS = 16
NG, NP_, NH = 32, 64, 16
NQ = 16
T8 = 8
CH = BT // T8
DFF = 2816
NF = DFF // 128
EPS = 1e-6
MAGIC = 12582912.0
TWO_PI = 2.0 * math.pi
PI_LO = 3.1415925

V_G1, V_G2, V_PS, V_DS, V_CW, V_CB, V_IC, V_IO = 0, 8, 16, 20, 24, 156, 200, 216
NV = 280
SP_ARE, SP_AIM, SP_LDT, SP_BRE, SP_BIM, SP_CRE, SP_CIM = 0, 16, 32, 48, 560, 1072, 1584
NSP = 2096


def build_program(do_s5=True, do_sample=True, nblk=NBLK, stage=99):
    nc = bass.Bass("TRN2", target_bir_lowering=False)
    din = lambda name, shape: nc.dram_tensor(name, list(shape), F32, kind="ExternalInput").ap()
    dout = lambda name, shape: nc.dram_tensor(name, list(shape), F32, kind="ExternalOutput").ap()
    x_d = din("x", [L, D])
    xs_d = din("xs", [NS, D])
    w_in_d = din("w_in", [D, D])
    w_glu_d = din("w_glu", [512, 512])
    pool_w_d = din("pool_w", [512, 128])
    w_out_d = din("w_out", [D, D])
    w_up_d = din("w_up", [D, 2 * DFF])
    w_down_d = din("w_down", [DFF, D])
    vecs_d = din("vecs", [128, NV])
    g3b_d = din("g3b", [128, D])
    ident_d = din("ident", [128, 128])
    s5p_d = din("s5p", [128, NSP])
    s5re_d = din("s5re", [NS, 2048])
    s5im_d = din("s5im", [NS, 2048])
    pst_d = din("pst", [NS, 15, 512])
    cst_d = din("cst", [NS, 2, 2 * DFF])
    y_d = dout("y", [L, D])
    ys_d = dout("ys", [NS, D])
    pre_d = dout("p_re", [NQ, 128])
    pim_d = dout("p_im", [NQ, 128])
    ppool_d = dout("p_pool", [15, 512])
    pconv_d = dout("p_conv", [2, 2 * DFF])
    sre_d = dout("s_re", [NS, 2048])
    sim_d = dout("s_im", [NS, 2048])
    spool_d = dout("s_pool", [NS, 15, 512])
    sconv_d = dout("s_conv", [NS, 2, 2 * DFF])

    wupb = nc.dram_tensor("wupb", [NF, 128, 8, 256], BF16).ap()
    wdnb = nc.dram_tensor("wdnb", [8, 128, NF, 128], BF16).ap()
    winb = nc.dram_tensor("winb", [8, 128, 8, 128], BF16).ap()
    woutb = nc.dram_tensor("woutb", [128, 8, D], BF16).ap()
    st = ExitStack()
    with st:
        S = Sched(nc)
        sbt = lambda name, shape, dt: st.enter_context(nc.sbuf_tensor("sb_" + name, list(shape), dt))
        w_glu_sb = sbt("w_glu_sb", [128, 4, 512], BF16)
        pool_w_sb = sbt("pool_w_sb", [128, 4, 128], BF16)
        NWI, NWU, NWD = 2, 3, 2
        w_in_ring = [sbt("w_in_r%d" % i, [128, 8, 128], BF16) for i in range(NWI)]
        w_up_ring = [sbt("w_up_r%d" % i, [128, 8, 256], BF16) for i in range(NWU)]
        w_dn_ring = [sbt("w_dn_r%d" % i, [128, NF, 128], BF16) for i in range(NWD)]
        vecs = sbt("vecs", [128, NV], F32)
        g3b = sbt("g3b", [128, D], F32)
        ident = sbt("ident", [128, 128], F32)
        identb = sbt("identb", [128, 128], BF16)
        Bst = sbt("Bst", [128, 4, T8, 2, 128], BF16)
        Cst = sbt("Cst", [128, NQ, T8 + 1, 2, 32], BF16)
        Kblk = sbt("Kblk", [128, 4, T8, 128], BF16)
        tabc = sbt("tabc", [128, NQ, CH], F32)
        tabs = sbt("tabs", [128, NQ, CH], F32)
        rdec = sbt("rdec", [128, NQ, CH + 1], F32)
        sp_small = sbt("sp_small", [128, 40, NQ], F32)
        LR, LI, PR0, PI0, CW, SW, GCR, GCI, HFR, HFI, TMPA, TMPB, TMPC, TMPD, ANG0 = 0, 1, 2, 11, 20, 21, 22, 23, 24, 25, 26, 27, 28, 29, 30
        stat = sbt("stat", [128, 16], F32)
        x_sb = sbt("x_sb", [128, 4, D], F32)
        x_sb2 = sbt("x_sb2", [128, 4, D], F32)
        scr = sbt("scr", [128, 2, D], F32)
        hT = sbt("hT", [128, 8, BT], BF16)
        v_sb = sbt("v_sb", [128, 4, 16 + BT], F32)
        Hb = sbt("Hb", [128, NQ, 2, CH + 1], BF16)
        hhalo = sbt("hhalo", [128, 2 * NF, 2], F32)
        hfix = sbt("hfix", [128, 2 * NF, 2], F32)
        hfix2 = sbt("hfix2", [128, 2 * NF], F32)
        ARENA_W = 8712
        arena = sbt("arena", [128, ARENA_W], F32)
        fence_scr = sbt("fence_scr", [128, 2], F32)

        def av(off, nwords, dt=F32, **kw):
            a = arena[:, off:off + nwords]
            if dt != F32:
                a = a.bitcast(dt)
            return a

        actT = av(0, 5632, BF16).rearrange("p (k n) -> p k n", k=NF)
        hup = av(5632, 2056).rearrange("p (s g n) -> p s g n", s=2, g=2)
        cvt = av(7688, 1024).rearrange("p (g n) -> p g n", g=2)
        cvt3 = av(5632, 3072).rearrange("p (s g n) -> p s g n", s=3, g=2)
        w_out_sb = av(0, 4096, BF16).rearrange("p (k n) -> p k n", k=8)
        tA = av(0, 1040).rearrange("p (q c) -> p q c", q=NQ)
        tB = av(1040, 1040).rearrange("p (q c) -> p q c", q=NQ)
        tC = av(2080, 1040).rearrange("p (q c) -> p q c", q=NQ)
        tD = av(3120, 1040).rearrange("p (q c) -> p q c", q=NQ)
        uT = av(4160, 1024, BF16).rearrange("p (k n) -> p k n", k=4)
        yg = av(5184, 1024, BF16).rearrange("p (k n) -> p k n", k=4)
        ptmp = av(6208, 1056).rearrange("p (s n) -> p s n", s=2)
        pooled = av(7264, 256, BF16)
        ytmp = av(7520, 1024).rearrange("p (s n) -> p s n", s=2)
        s5p = av(0, NSP)
        G0r = av(2096, 512).rearrange("p (q h) -> p q h", q=NQ)
        G0i = av(2608, 512).rearrange("p (q h) -> p q h", q=NQ)
        Gnr = av(3120, 512).rearrange("p (q h) -> p q h", q=NQ)
        Gni = av(3632, 512).rearrange("p (q h) -> p q h", q=NQ)
        Gnb = av(4144, 512, BF16).rearrange("p (r q h) -> p r q h", r=2, q=NQ)
        Ccb = av(4656, 512, BF16).rearrange("p (r q h) -> p r q h", r=2, q=NQ)
        pt1 = av(5168, 512).rearrange("p (q h) -> p q h", q=NQ)
        pt2 = av(5680, 512).rearrange("p (q h) -> p q h", q=NQ)
        pang = av(6192, 1024).rearrange("p (q c) -> p q c", q=NQ)
        pang2 = av(7216, 1024).rearrange("p (q c) -> p q c", q=NQ)

        PA = st.enter_context(nc.psum_tensor("psA", [128, 1024], F32))
        PB = st.enter_context(nc.psum_tensor("psB", [128, 1024], F32))
        PCD = st.enter_context(nc.psum_tensor("psCD", [128, 2048], F32))
        PC, PD = PCD[:, 0:1024], PCD[:, 1024:2048]
        PSn = [PA, PB, PC, PD]
        bank = [[S.buf("ps%d_%d" % (i, h)) for h in range(2)] for i in range(4)]

        def pbank(i, h):
            return PSn[i][:, h * 512:(h + 1) * 512]

        b_wout, b_wglu, b_poolw = S.buf("wout"), S.buf("wglu"), S.buf("poolw")
        b_win = [S.buf("win%d" % i) for i in range(NWI)]
        b_wup = [S.buf("wup%d" % i) for i in range(NWU)]
        b_wdn = [S.buf("wdn%d" % i) for i in range(NWD)]
        b_vecs, b_g3b, b_ident, b_identb = S.buf("vecs"), S.buf("g3b"), S.buf("ident"), S.buf("identb")
        b_Bst, b_Cst, b_Kblk, b_tab, b_rdec, b_sps = S.buf("Bst"), S.buf("Cst"), S.buf("Kblk"), S.buf("tab"), S.buf("rdec"), S.buf("sps")
        b_gc, b_hfin = S.buf("gc"), S.buf("hfin")
        b_stat = S.buf("stat")
        b_x = [S.buf("x%d" % t) for t in range(4)]
        b_x2 = [S.buf("x2_%d" % t) for t in range(4)]
        b_scr = [S.buf("scr%d" % i) for i in range(2)]
        b_h = [[S.buf("h%d_%d" % (k, t)) for t in range(4)] for k in range(8)]
        b_v = [S.buf("v%d" % g) for g in range(4)]
        b_Hb, b_Hb0 = S.buf("Hb"), S.buf("Hb0")
        b_hhalo = S.buf("hhalo")
        b_hfix, b_hfix2 = S.buf("hfix"), S.buf("hfix2")
        b_act = [S.buf("act%d" % f) for f in range(NF)]
        b_hup = [S.buf("hup%d" % s) for s in range(2)]
        b_cvt = S.buf("cvt")
        b_cvt3 = [[S.buf("cvt3_%d_%d" % (i, g)) for g in range(2)] for i in range(3)]
        b_tA, b_tB, b_tC, b_tD = S.buf("tA"), S.buf("tB"), S.buf("tC"), S.buf("tD")
        b_uT = [S.buf("uT%d" % k) for k in range(4)]
        b_yg = [S.buf("yg%d" % k) for k in range(4)]
        b_ptmp = [S.buf("ptmp%d" % i) for i in range(2)]
        b_pooled = S.buf("pooled")
        b_ytmp = [S.buf("ytmp%d" % i) for i in range(2)]
        b_prep = S.buf("prep")
        flat_ = lambda L_: [b for l in L_ for b in l]
        arena_bufs = b_act + b_hup + flat_(b_cvt3) + [b_cvt, b_tA, b_tB, b_tC, b_tD] + b_uT + b_yg + b_ptmp + [b_pooled] + b_ytmp + [b_prep]
        b_fence = S.buf("fence")

        def fence():
            S.c("dve", lambda e: e.memset(fence_scr[:, 0:1], 0.0), writes=arena_bufs + [b_fence])

        flat = lambda L_: [b for l in L_ for b in l]
        V = lambda c0, n: vecs[:, c0:c0 + n]

        S.dma("sp", lambda e: e.dma_start(out=vecs[:], in_=vecs_d), writes=[b_vecs])
        S.dma("sp", lambda e: e.dma_start(out=ident[:], in_=ident_d), writes=[b_ident])
        S.dma("sp", lambda e: e.dma_start(out=g3b[:], in_=g3b_d), writes=[b_g3b])
        b_s5p = S.buf("s5p")
        S.dma("sp", lambda e: e.dma_start(out=s5p, in_=s5p_d), writes=[b_prep, b_s5p])
        S.c("dve", lambda e: e.tensor_copy(out=identb[:], in_=ident[:]), reads=[b_ident], writes=[b_identb])
        S.dma("pool", lambda e: e.dma_start(out=w_glu_sb[:], in_=w_glu_d.rearrange("(k p) n -> p k n", p=128)), writes=[b_wglu])
        S.dma("pool", lambda e: e.dma_start(out=pool_w_sb[:], in_=pool_w_d.rearrange("(k p) n -> p k n", p=128)), writes=[b_poolw])

        b_winb = [S.buf("winb%d" % m) for m in range(8)]
        b_wupb = [S.buf("wupb%d" % f) for f in range(NF)]
        b_wdnb = [S.buf("wdnb%d" % m) for m in range(8)]
        for m in range(8):
            S.dma("pool", lambda e, m=m: e.dma_start(out=winb[m], in_=w_in_d[:, m * 128:(m + 1) * 128].rearrange("(k p) n -> p k n", p=128)),
                  writes=[b_winb[m]])
        for f in range(NF):
            S.dma("pool", lambda e, f=f: [e.dma_start(out=wupb[f][:, :, 0:128], in_=w_up_d[:, f * 128:(f + 1) * 128].rearrange("(k p) n -> p k n", p=128)),
                                           e.dma_start(out=wupb[f][:, :, 128:256], in_=w_up_d[:, DFF + f * 128:DFF + (f + 1) * 128].rearrange("(k p) n -> p k n", p=128))],
                  writes=[b_wupb[f]], ndma=2)
        for m in range(8):
            S.dma("pool", lambda e, m=m: e.dma_start(out=wdnb[m], in_=w_down_d[:, m * 128:(m + 1) * 128].rearrange("(k p) n -> p k n", p=128)),
                  writes=[b_wdnb[m]])
        spv = lambda i, n=1: sp_small[:, i:i + n, :]

        def sincos(ang_ap, out_s, out_c, shape, tmp1, tmp2, reads, writes):
            S.c("dve", lambda e: e.tensor_scalar(out=tmp1, in0=ang_ap, scalar1=1.0 / TWO_PI, scalar2=MAGIC, op0=ALU.mult, op1=ALU.add),
                reads=reads, writes=[b_prep])
            S.c("dve", lambda e: e.tensor_scalar(out=tmp1, in0=tmp1, scalar1=MAGIC, scalar2=-TWO_PI, op0=ALU.subtract, op1=ALU.mult),
                reads=[b_prep], writes=[b_prep])
            S.c("dve", lambda e: e.tensor_tensor(out=tmp1, in0=tmp1, in1=ang_ap, op=ALU.add), reads=[b_prep] + reads, writes=[b_prep])
            S.c("dve", lambda e: e.tensor_scalar(out=tmp1, in0=tmp1, scalar1=PI_LO, scalar2=-PI_LO, op0=ALU.min, op1=ALU.max), reads=[b_prep], writes=[b_prep])
            S.c("act", lambda e: e.activation(out=out_s, in_=tmp1, func=AF.Sin), reads=[b_prep], writes=writes)
            S.c("dve", lambda e: e.tensor_scalar(out=tmp2, in0=ang_ap, scalar1=1.0 / TWO_PI, scalar2=0.25, op0=ALU.mult, op1=ALU.add),
                reads=reads, writes=[b_prep])
            S.c("dve", lambda e: e.tensor_scalar(out=tmp2, in0=tmp2, scalar1=MAGIC, scalar2=None, op0=ALU.add), reads=[b_prep], writes=[b_prep])
            S.c("dve", lambda e: e.tensor_scalar(out=tmp2, in0=tmp2, scalar1=MAGIC, scalar2=-TWO_PI, op0=ALU.subtract, op1=ALU.mult),
                reads=[b_prep], writes=[b_prep])
            S.c("dve", lambda e: e.scalar_tensor_tensor(out=tmp2, in0=ang_ap, scalar=0.5 * math.pi, in1=tmp2, op0=ALU.add, op1=ALU.add),
                reads=[b_prep] + reads, writes=[b_prep])
            S.c("dve", lambda e: e.tensor_scalar(out=tmp2, in0=tmp2, scalar1=PI_LO, scalar2=-PI_LO, op0=ALU.min, op1=ALU.max), reads=[b_prep], writes=[b_prep])
            S.c("act", lambda e: e.activation(out=out_c, in_=tmp2, func=AF.Sin), reads=[b_prep], writes=writes)

        if do_s5:
            are = s5p[:, SP_ARE:SP_ARE + 16]
            aim = s5p[:, SP_AIM:SP_AIM + 16]
            ldt = s5p[:, SP_LDT:SP_LDT + 16]
            Bre = s5p[:, SP_BRE:SP_BRE + 512].rearrange("p (q h) -> p q h", q=NQ)
            Bim = s5p[:, SP_BIM:SP_BIM + 512].rearrange("p (q h) -> p q h", q=NQ)
            Cre = s5p[:, SP_CRE:SP_CRE + 512].rearrange("p (q h) -> p q h", q=NQ)
            Cim = s5p[:, SP_CIM:SP_CIM + 512].rearrange("p (q h) -> p q h", q=NQ)
            sp2 = lambda i: sp_small[:, i, :]
            S.c("act", lambda e: e.activation(out=sp2(TMPA), in_=ldt, func=AF.Exp), reads=[b_prep], writes=[b_sps])
            S.c("dve", lambda e: e.tensor_tensor(out=sp2(LR), in0=sp2(TMPA), in1=are, op=ALU.mult), reads=[b_prep, b_sps], writes=[b_sps])
            S.c("dve", lambda e: e.tensor_tensor(out=sp2(LI), in0=sp2(TMPA), in1=aim, op=ALU.mult), reads=[b_prep, b_sps], writes=[b_sps])
            for n in range(9):
                S.c("act", lambda e, n=n: e.activation(out=sp2(PR0 + n), in_=sp2(LR), func=AF.Exp, scale=float(n)), reads=[b_sps], writes=[b_sps])
                S.c("dve", lambda e, n=n: e.tensor_scalar(out=sp2(ANG0 + n), in0=sp2(LI), scalar1=float(n), scalar2=None, op0=ALU.mult),
                    reads=[b_sps], writes=[b_sps])
            S.c("dve", lambda e: e.tensor_scalar(out=sp2(ANG0 + 9), in0=sp2(LI), scalar1=float(BT), scalar2=None, op0=ALU.mult),
                reads=[b_sps], writes=[b_sps])
            S.c("dve", lambda e: e.memset(rdec[:], 0.0), writes=[b_rdec])
            S.c("dve", lambda e: e.tensor_copy(out=rdec[:, :, 1:CH + 1], in_=sp_small[:, PR0 + 8, :].unsqueeze(2).broadcast_to([128, NQ, CH])),
                reads=[b_sps], writes=[b_rdec])
            angv = sp_small[:, ANG0:ANG0 + 10, :]
            sin10 = pang[:, 0:10, 0:16]
            cos10 = pang[:, 0:10, 16:32]
            sincos(angv, sin10, cos10, None, pang2[:, 0:10, 0:16], pang2[:, 0:10, 16:32], [b_sps], [b_prep])
            S.c("dve", lambda e: e.tensor_tensor(out=sp_small[:, PI0:PI0 + 9, :], in0=sp_small[:, PR0:PR0 + 9, :], in1=sin10[:, 0:9, :], op=ALU.mult),
                reads=[b_prep, b_sps], writes=[b_sps])
            S.c("dve", lambda e: e.tensor_tensor(out=sp_small[:, PR0:PR0 + 9, :], in0=sp_small[:, PR0:PR0 + 9, :], in1=cos10[:, 0:9, :], op=ALU.mult),
                reads=[b_prep, b_sps], writes=[b_sps])
            S.c("dve", lambda e: e.tensor_copy(out=sp2(SW), in_=sin10[:, 9, :]), reads=[b_prep], writes=[b_sps])
            S.c("dve", lambda e: e.tensor_copy(out=sp2(CW), in_=cos10[:, 9, :]), reads=[b_prep], writes=[b_sps])
            S.c("dve", lambda e: e.tensor_scalar(out=sp2(TMPB), in0=sp2(LI), scalar1=float(T8), scalar2=None, op0=ALU.mult), reads=[b_sps], writes=[b_sps])
            S.c("dve", lambda e: e.tensor_tensor(out=pang, in0=sp_small[:, TMPB, :].unsqueeze(2).broadcast_to([128, NQ, CH]),
                                                  in1=V(V_IO, 64).unsqueeze(1).broadcast_to([128, NQ, CH]), op=ALU.mult),
                reads=[b_sps, b_vecs, b_prep], writes=[b_prep])
            tmp_a = av(2096, 1024).rearrange("p (q c) -> p q c", q=NQ)
            tmp_b = av(3120, 1024).rearrange("p (q c) -> p q c", q=NQ)
            sincos(pang, tabs[:], tabc[:], None, tmp_a, tmp_b, [b_prep], [b_tab])
            S.c("dve", lambda e: e.tensor_scalar(out=sp2(TMPA), in0=sp2(PR0 + 1), scalar1=-1.0, scalar2=None, op0=ALU.add), reads=[b_sps], writes=[b_sps])
            S.c("dve", lambda e: e.tensor_tensor(out=sp2(TMPB), in0=are, in1=are, op=ALU.mult), reads=[b_prep], writes=[b_sps])
            S.c("dve", lambda e: e.tensor_tensor(out=sp2(TMPC), in0=aim, in1=aim, op=ALU.mult), reads=[b_prep], writes=[b_sps])
            S.c("dve", lambda e: e.tensor_tensor(out=sp2(TMPB), in0=sp2(TMPB), in1=sp2(TMPC), op=ALU.add), reads=[b_sps], writes=[b_sps])
            S.c("dve", lambda e: e.reciprocal(out=sp2(TMPB), in_=sp2(TMPB)), reads=[b_sps], writes=[b_sps])
            S.c("dve", lambda e: e.tensor_tensor(out=sp2(TMPC), in0=sp2(TMPA), in1=are, op=ALU.mult), reads=[b_sps, b_prep], writes=[b_sps])
            S.c("dve", lambda e: e.tensor_tensor(out=sp2(TMPD), in0=sp2(PI0 + 1), in1=aim, op=ALU.mult), reads=[b_sps, b_prep], writes=[b_sps])
            S.c("dve", lambda e: e.tensor_tensor(out=sp2(TMPC), in0=sp2(TMPC), in1=sp2(TMPD), op=ALU.add), reads=[b_sps], writes=[b_sps])
            S.c("dve", lambda e: e.tensor_tensor(out=sp2(TMPC), in0=sp2(TMPC), in1=sp2(TMPB), op=ALU.mult), reads=[b_sps], writes=[b_sps])
            S.c("dve", lambda e: e.tensor_tensor(out=sp2(TMPD), in0=sp2(PI0 + 1), in1=are, op=ALU.mult), reads=[b_sps, b_prep], writes=[b_sps])
            S.c("dve", lambda e: e.tensor_tensor(out=sp2(TMPA), in0=sp2(TMPA), in1=aim, op=ALU.mult), reads=[b_sps, b_prep], writes=[b_sps])
            S.c("dve", lambda e: e.tensor_tensor(out=sp2(TMPD), in0=sp2(TMPD), in1=sp2(TMPA), op=ALU.subtract), reads=[b_sps], writes=[b_sps])
            S.c("dve", lambda e: e.tensor_tensor(out=sp2(TMPD), in0=sp2(TMPD), in1=sp2(TMPB), op=ALU.mult), reads=[b_sps], writes=[b_sps])
            bq = lambda i: sp_small[:, i, :].unsqueeze(2).broadcast_to([128, NQ, 32])

            pt3 = av(6192, 512).rearrange("p (q h) -> p q h", q=NQ)
            pt4 = av(6704, 512).rearrange("p (q h) -> p q h", q=NQ)
            Gnb2 = av(7216, 512, BF16).rearrange("p (r q h) -> p r q h", r=2, q=NQ)
            pts = [pt1, pt2, pt3, pt4]
            b_pt = [S.buf("pt%d" % i) for i in range(4)]
            b_G0, b_Ccb, b_Gn2 = S.buf("G0"), S.buf("Ccb"), [S.buf("Gnb0"), S.buf("Gnb1")]
            arena_bufs.extend(b_pt + [b_G0, b_Ccb] + b_Gn2)
            S.c("dve", lambda e: e.memset(fence_scr[:, 1:2], 0.0), reads=[b_sps], writes=[b_prep] + b_pt + [b_G0, b_Ccb] + b_Gn2)

            def cmul(out_r, out_i, ar, ai, br_, bi_, in_bufs, out_bufs, neg_i=False):
                t1, t2, t3, t4 = pts
                S.c("dve", lambda e: e.tensor_tensor(out=t1, in0=ar, in1=br_, op=ALU.mult), reads=in_bufs, writes=[b_pt[0]])
                S.c("dve", lambda e: e.tensor_tensor(out=t2, in0=ai, in1=bi_, op=ALU.mult), reads=in_bufs, writes=[b_pt[1]])
                S.c("dve", lambda e: e.tensor_tensor(out=t3, in0=ar, in1=bi_, op=ALU.mult), reads=in_bufs, writes=[b_pt[2]])
                S.c("dve", lambda e: e.tensor_tensor(out=t4, in0=ai, in1=br_, op=ALU.mult), reads=in_bufs, writes=[b_pt[3]])
                S.c("dve", lambda e: e.tensor_tensor(out=out_r, in0=t1, in1=t2, op=ALU.subtract), reads=[b_pt[0], b_pt[1]], writes=out_bufs)
                if neg_i:
                    S.c("dve", lambda e: e.scalar_tensor_tensor(out=out_i, in0=t3, scalar=-1.0, in1=t4, op0=ALU.mult, op1=ALU.subtract),
                        reads=[b_pt[2], b_pt[3]], writes=out_bufs)
                else:
                    S.c("dve", lambda e: e.tensor_tensor(out=out_i, in0=t3, in1=t4, op=ALU.add), reads=[b_pt[2], b_pt[3]], writes=out_bufs)

            cmul(G0r, G0i, bq(TMPC), bq(TMPD), Bre, Bim, [b_sps, b_s5p], [b_G0])
            S.c("dve", lambda e: e.tensor_copy(out=Ccb[:, 0], in_=Cre), reads=[b_s5p], writes=[b_Ccb])
            S.c("dve", lambda e: e.tensor_scalar(out=Ccb[:, 1], in0=Cim, scalar1=-1.0, scalar2=None, op0=ALU.mult), reads=[b_s5p], writes=[b_Ccb])
            S.c("dve", lambda e: e.memset(Kblk[:], 0.0), writes=[b_Kblk])
            for n in range(T8):
                gb, bg = (Gnb, b_Gn2[0]) if n % 2 == 0 else (Gnb2, b_Gn2[1])
                cmul(gb[:, 0], gb[:, 1], bq(PR0 + n), bq(PI0 + n), G0r, G0i, [b_sps, b_G0], [bg])
                def tr_fn(e, gb=gb):
                    last = None
                    for ri in range(2):
                        for q in range(NQ):
                            q4, kc = q // 4, q % 4
                            col = (kc * 2 + ri) * 128
                            last = e.matmul(PSn[0][32 * q4:32 * q4 + 32, col:col + 128], lhsT=gb[:, ri, q, :], rhs=identb[:],
                                            start=True, stop=True, tile_position=(0, 32 * q4))
                    return last
                S.c("pe", tr_fn, reads=[bg, b_identb], writes=[bank[0][0], bank[0][1]])
                S.c("act", lambda e, n=n: e.activation(out=Bst[:, :, T8 - 1 - n, :, :],
                                                        in_=PSn[0][:, :].rearrange("p (k r m) -> p k r m", k=4, r=2), func=AF.Copy),
                    reads=[bank[0][0], bank[0][1]], writes=[b_Bst])
                def kb_fn(e, gb=gb):
                    last = None
                    for q in range(NQ):
                        q4, kc = q // 4, q % 4
                        o = PSn[1][32 * q4:32 * q4 + 32, kc * 128 + 32 * q4:kc * 128 + 32 * q4 + 32]
                        e.matmul(o, lhsT=gb[:, 0, q, :], rhs=Ccb[:, 0, q, :], start=True, stop=False, tile_position=(0, 32 * q4))
                        last = e.matmul(o, lhsT=gb[:, 1, q, :], rhs=Ccb[:, 1, q, :], start=False, stop=True, tile_position=(0, 32 * q4))
                    return last
                S.c("pe", kb_fn, reads=[bg, b_Ccb], writes=[bank[1][0]])
                for q4 in range(4):
                    S.c("act", lambda e, n=n, q4=q4: e.activation(
                        out=Kblk[32 * q4:32 * q4 + 32, :, n, 32 * q4:32 * q4 + 32],
                        in_=PSn[1][32 * q4:32 * q4 + 32, 0:512].rearrange("p (k m) -> p k m", k=4)[:, :, 32 * q4:32 * q4 + 32], func=AF.Copy),
                        reads=[bank[1][0]], writes=[b_Kblk])
            for n in range(T8 + 1):
                cmul(Cst[:, :, n, 0, :], Cst[:, :, n, 1, :], Cre, Cim, bq(PR0 + n), bq(PI0 + n), [b_sps, b_s5p], [b_Cst], neg_i=True)
            S.c("dve", lambda e: e.memset(sp_small[:, GCR:GCR + 2, :], 0.0), writes=[b_gc])
            S.c("dve", lambda e: e.memset(Hb[:], 0.0), writes=[b_Hb, b_Hb0])
        S.c("dve", lambda e: e.memset(hhalo[:], 0.0), writes=[b_hhalo])
        S.c("dve", lambda e: e.memset(v_sb[:], 0.0), writes=b_v)
        fence()
        b_woutb = S.buf("woutb")
        S.dma("pool", lambda e: e.dma_start(out=woutb, in_=w_out_d.rearrange("(k p) n -> p k n", p=128)), writes=[b_woutb])
        arena_bufs.append(b_wout)

        def load_wout():
            S.dma("sp", lambda e: e.dma_start(out=w_out_sb, in_=woutb), reads=[b_woutb], writes=[b_tA, b_tB, b_tC, b_tD, b_wout])
        cnt = {"win": 0, "wup": 0, "wdn": 0, "bk": 0}

        def rms_stats(nt, xt_aps, xbufs, premem=False):
            if not premem:
                S.c("dve", lambda e: e.memset(stat[:, 0:8], 0.0), writes=[b_stat])
            for i, xa in enumerate(xt_aps):
                np_ = xa.shape[0]
                S.c("act", lambda e, i=i, xa=xa, np_=np_: e.activation(out=scr[0:np_, 1, :], in_=xa, func=AF.Square, accum_out=stat[0:np_, i:i + 1]),
                    reads=[xbufs[i], b_stat], writes=[b_scr[1], b_stat])
            S.c("dve", lambda e: e.tensor_scalar(out=stat[:, 0:nt], in0=stat[:, 0:nt], scalar1=1.0 / D, scalar2=EPS, op0=ALU.mult, op1=ALU.add),
                reads=[b_stat], writes=[b_stat])
            S.c("act", lambda e: e.activation(out=stat[:, 0:nt], in_=stat[:, 0:nt], func=AF.Sqrt), reads=[b_stat], writes=[b_stat])
            S.c("dve", lambda e: e.reciprocal(out=stat[:, 8:8 + nt], in_=stat[:, 0:nt]), reads=[b_stat], writes=[b_stat])

        def norm_to_hT(tiles, gcol, dst=None, dbufs=None, premem=False):
            rms_stats(len(tiles), [t_[0] for t_ in tiles], [t_[1] for t_ in tiles], premem=premem)
            for i, (xa, xb, c0, np_) in enumerate(tiles):
                s = i % 2
                S.c("dve", lambda e, xa=xa, i=i, np_=np_, s=s: e.tensor_scalar(out=scr[0:np_, s, :], in0=xa, scalar1=stat[0:np_, 8 + i:9 + i],
                                                                         scalar2=None, op0=ALU.mult),
                    reads=[xb, b_stat], writes=[b_scr[s]])

                def tr(e, np_=np_, s=s):
                    last = None
                    for k in range(8):
                        last = e.transpose(out=PA[:, k * 128:k * 128 + np_], in_=scr[0:np_, s, k * 128:(k + 1) * 128], identity=ident[0:np_, 0:np_])
                    return last
                S.c("pe", tr, reads=[b_scr[s], b_ident], writes=[bank[0][0], bank[0][1]])
                t_idx = c0 // 128
                dst_ = hT if dst is None else dst
                S.c("dve", lambda e, np_=np_, c0=c0, dst_=dst_: e.tensor_tensor(out=dst_[:, :, c0:c0 + np_],
                                                                 in0=PA[:, :].rearrange("p (k n) -> p k n", k=8)[:, :, 0:np_],
                                                                 in1=V(gcol, 8).unsqueeze(2).broadcast_to([128, 8, np_]), op=ALU.mult),
                    reads=[bank[0][0], bank[0][1], b_vecs], writes=([b_h[k][t_idx] for k in range(8)] if dbufs is None else dbufs))

        def load_win(m):
            slot = cnt["win"] % NWI
            cnt["win"] += 1
            S.dma("sp", lambda e: e.dma_start(out=w_in_ring[slot][:], in_=winb[m]), reads=[b_winb[m]], writes=[b_win[slot]])
            return slot

        def load_wup(f):
            slot = cnt["wup"] % NWU
            cnt["wup"] += 1
            S.dma("sp", lambda e: e.dma_start(out=w_up_ring[slot][:], in_=wupb[f]), reads=[b_wupb[f]], writes=[b_wup[slot]])
            return slot

        def load_wdn(m):
            slot = cnt["wdn"] % NWD
            cnt["wdn"] += 1
            S.dma("sp", lambda e: e.dma_start(out=w_dn_ring[slot][:], in_=wdnb[m]), reads=[b_wdnb[m]], writes=[b_wdn[slot]])
            return slot

        def next_bank():
            i = cnt["bk"] % 2
            cnt["bk"] += 1
            return pbank(1, i), bank[1][i]

        sel_d = din("sel", [128, 32])
        selt = sbt("selt", [128, 32], F32)
        xst = sbt("xst", [128, 2, 128], F32)
        xs_sb = sbt("xs_sb", [128, D], F32)
        hTs = sbt("hTs", [128, 8, NS], BF16)
        HbS = sbt("HbS", [128, NQ, 2, NS], BF16)
        cbufT = sbt("cbufT", [128, 2 * NF, 2 * NS], F32)
        hupS = sbt("hupS", [128, 2 * NF, NS], F32)
        actS = sbt("actS", [128, NF, NS], BF16)
        cvS = sbt("cvS", [128, 3, 2, NS], F32)
        b_sel, b_xst = S.buf("sel"), [S.buf("xst0"), S.buf("xst1")]
        b_xs, b_hs, b_HbS = S.buf("xs"), [S.buf("hs%d" % k) for k in range(8)], S.buf("HbS")
        b_cst, b_cbufT, b_hupS, b_actS = S.buf("cst"), S.buf("cbufT"), S.buf("hupS"), S.buf("actS")
        b_cvS = [[S.buf("cvS%d_%d" % (i, g)) for g in range(2)] for i in range(3)]
        b_pss = [S.buf("pss%d" % i) for i in range(4)]
        b_psd = [S.buf("psd%d" % i) for i in range(2)]
        b_pst = [S.buf("pst%d" % i) for i in range(2)]
        arena_bufs.append(b_cst)
        N = NS
        def sample_phase_A():
            S.dma("sp", lambda e: e.dma_start(out=selt[:], in_=sel_d), writes=[b_sel])
            S.dma("sp", lambda e: e.dma_start(out=xs_sb[0:N, :], in_=xs_d), writes=[b_xs])
            S.dma("sp", lambda e: e.dma_start(out=spool_d[:, 0:14, :], in_=pst_d[:, 1:15, :]), out=True)
            S.dma("sp", lambda e: e.dma_start(out=sconv_d[:, 0, :], in_=cst_d[:, 1, :]), out=True)
            norm_to_hT([(xs_sb[0:N, :], b_xs, 0, N)], V_G1, dst=hTs, dbufs=b_hs)
            hcol = b_hs
            for m in range(8):
                slot = load_win(m)
                po, pbuf = next_bank()

                def proj(e, slot=slot, po=po):
                    last = None
                    for k in range(8):
                        last = e.matmul(po[:, 0:N], lhsT=w_in_ring[slot][:, k, :], rhs=hTs[:, k, :], start=(k == 0), stop=(k == 7))
                    return last
                S.c("pe", proj, reads=[b_win[slot]] + hcol, writes=[pbuf])
                if m < 4:
                    S.c("act", lambda e, m=m, po=po: e.activation(out=uT[:, m, 0:N], in_=po[:, 0:N], func=AF.Copy), reads=[pbuf], writes=[b_uT[m]])
                else:
                    S.c("act", lambda e, m=m, po=po: e.activation(out=v_sb[:, m - 4, 16:16 + N], in_=po[:, 0:N], func=AF.Copy), reads=[pbuf], writes=[b_v[m - 4]])
            if do_s5:
                h0r_tm = scr[0:N, :, :].rearrange("p s n -> p (s n)")
                h0i_tm = x_sb[0:N, 2:4, :].rearrange("p s n -> p (s n)")
                S.dma("sp", lambda e: e.dma_start(out=h0r_tm, in_=s5re_d), writes=b_scr)
                S.dma("sp", lambda e: e.dma_start(out=h0i_tm, in_=s5im_d), writes=[b_x[2], b_x[3]])

                def trh(e):
                    last = None
                    for ri, src in ((0, h0r_tm), (1, h0i_tm)):
                        for q in range(NQ):
                            qo = (q % 4) * 4 + q // 4
                            last = e.transpose(out=PB[:, ri * 256 + q * N:ri * 256 + (q + 1) * N], in_=src[:, qo * 128:(qo + 1) * 128], identity=ident[0:N, 0:N])
                    return last
                S.c("pe", trh, reads=b_scr + [b_x[2], b_x[3], b_ident], writes=[bank[1][0]])
                h0r, h0i, hnr, hni = tA[:, :, 0:N], tB[:, :, 0:N], tC[:, :, 0:N], tD[:, :, 0:N]
                S.c("act", lambda e: e.activation(out=h0r, in_=PB[:, 0:256].rearrange("p (q n) -> p q n", q=NQ), func=AF.Copy), reads=[bank[1][0]], writes=[b_tA])
                S.c("act", lambda e: e.activation(out=h0i, in_=PB[:, 256:512].rearrange("p (q n) -> p q n", q=NQ), func=AF.Copy), reads=[bank[1][0]], writes=[b_tB])

                def bu(e):
                    last = None
                    for ri in range(2):
                        for q in range(NQ):
                            q4, kc = q // 4, q % 4
                            c0_ = q4 * 512 + (kc * 2 + ri) * N
                            last = e.matmul(PCD[:, c0_:c0_ + N], lhsT=Bst[32 * q4:32 * q4 + 32, kc, T8 - 1, ri, :],
                                            rhs=uT[32 * q4:32 * q4 + 32, kc, 0:N], start=True, stop=True, tile_position=(32 * q4, 0))
                    return last
                S.c("pe", bu, reads=[b_Bst] + b_uT, writes=[bank[2][0], bank[2][1], bank[3][0], bank[3][1]])
                al = sp_small[:, PR0 + 1, :].unsqueeze(2).broadcast_to([128, NQ, N])
                be = sp_small[:, PI0 + 1, :].unsqueeze(2).broadcast_to([128, NQ, N])
                t1 = ytmp[:, 0, 0:256].rearrange("p (q n) -> p q n", q=NQ)
                t2 = ytmp[:, 1, 0:256].rearrange("p (q n) -> p q n", q=NQ)
                TT = lambda eng, o, a, b_, op, rd, wr: S.c(eng, lambda e: e.tensor_tensor(out=o, in0=a, in1=b_, op=op), reads=rd, writes=wr)
                PSv = PCD[:, :].rearrange("p (a x) -> p a x", a=4)[:, :, 0:8 * N].rearrange("p a (k r n) -> p a k r n", k=4, r=2)
                Sre, Sim = PSv[:, :, :, 0, :], PSv[:, :, :, 1, :]
                v4 = lambda a: a.rearrange("p (a k) c -> p a k c", a=4)
                TT("dve", t1, h0r, al, ALU.mult, [b_tA, b_sps], [b_ytmp[0]])
                TT("dve", t2, h0i, be, ALU.mult, [b_tB, b_sps], [b_ytmp[1]])
                TT("dve", hnr, t1, t2, ALU.subtract, b_ytmp, [b_tC])
                TT("dve", v4(hnr), v4(hnr), Sre, ALU.add, [b_tC, bank[2][0], bank[2][1], bank[3][0], bank[3][1]], [b_tC])
                TT("dve", t1, h0i, al, ALU.mult, [b_tB, b_sps], [b_ytmp[0]])
                TT("dve", t2, h0r, be, ALU.mult, [b_tA, b_sps], [b_ytmp[1]])
                TT("dve", hni, t1, t2, ALU.add, b_ytmp, [b_tD])
                TT("dve", v4(hni), v4(hni), Sim, ALU.add, [b_tD, bank[2][0], bank[2][1], bank[3][0], bank[3][1]], [b_tD])
                S.c("dve", lambda e: e.tensor_copy(out=HbS[:, :, 0, :], in_=hnr), reads=[b_tC], writes=[b_HbS])
                S.c("dve", lambda e: e.tensor_copy(out=HbS[:, :, 1, :], in_=hni), reads=[b_tD], writes=[b_HbS])
                for ri, (src, dst_tm, dbufs, dd) in enumerate(((hnr, h0r_tm, b_scr, sre_d), (hni, h0i_tm, [b_x[2], b_x[3]], sim_d))):
                    for half in range(2):
                        def trb(e, src=src, half=half):
                            last = None
                            for qq in range(8):
                                qo = half * 8 + qq
                                qp = (qo % 4) * 4 + qo // 4
                                last = e.transpose(out=PA[0:N, qq * 128:(qq + 1) * 128], in_=src[:, qp, :], identity=ident[:])
                            return last
                        S.c("pe", trb, reads=[b_tC, b_tD, b_ident], writes=[bank[0][0], bank[0][1]])
                        S.c("act", lambda e, dst_tm=dst_tm, half=half: e.activation(out=dst_tm[:, half * 1024:(half + 1) * 1024], in_=PA[0:N, :], func=AF.Copy),
                            reads=[bank[0][0], bank[0][1]], writes=dbufs)
                    S.dma("sp", lambda e, dd=dd, dst_tm=dst_tm: e.dma_start(out=dd, in_=dst_tm), reads=dbufs, out=True)
                for kc in range(4):
                    Y2 = pbank(3, 1)

                    def ys(e, kc=kc, Y2=Y2):
                        last = None
                        for q4 in range(4):
                            q = q4 * 4 + kc
                            o = Y2[32 * q4:32 * q4 + 32, 0:N]
                            e.matmul(o, lhsT=Cst[:, q, 0, 0, :], rhs=HbS[:, q, 0, :], start=True, stop=False, tile_position=(0, 32 * q4))
                            last = e.matmul(o, lhsT=Cst[:, q, 0, 1, :], rhs=HbS[:, q, 1, :], start=False, stop=True, tile_position=(0, 32 * q4))
                        return last
                    S.c("pe", ys, reads=[b_Cst, b_HbS], writes=[bank[3][1]])
                    S.c("dve", lambda e, kc=kc, Y2=Y2: e.scalar_tensor_tensor(out=ytmp[:, 0, 0:N], in0=uT[:, kc, 0:N], scalar=vecs[:, V_DS + kc:V_DS + kc + 1],
                                                                          in1=Y2[:, 0:N], op0=ALU.mult, op1=ALU.add),
                        reads=[bank[3][1], b_uT[kc], b_vecs], writes=[b_ytmp[0]])
                    S.c("act", lambda e, kc=kc: e.activation(out=yg[:, kc, 0:N], in_=ytmp[:, 0, 0:N], func=AF.Gelu), reads=[b_ytmp[0]], writes=[b_yg[kc]])
                for m in range(4):
                    po, pbuf = next_bank()

                    def glu(e, m=m, po=po):
                        last = None
                        for k in range(4):
                            last = e.matmul(po[:, 0:N], lhsT=w_glu_sb[:, k, m * 128:(m + 1) * 128], rhs=yg[:, k, 0:N], start=(k == 0), stop=(k == 3))
                        return last
                    S.c("pe", glu, reads=[b_wglu] + b_yg, writes=[pbuf])
                    S.c("act", lambda e, po=po: e.activation(out=ytmp[:, 1, 0:N], in_=po[:, 0:N], func=AF.Sigmoid), reads=[pbuf], writes=[b_ytmp[1]])
                    S.c("dve", lambda e, m=m: e.tensor_tensor(out=hTs[:, m, :], in0=yg[:, m, 0:N], in1=ytmp[:, 1, 0:N], op=ALU.mult),
                        reads=[b_ytmp[1], b_yg[m]], writes=[b_hs[m]])
            else:
                for m in range(4):
                    S.c("dve", lambda e, m=m: e.memset(hTs[:, m, :], 0.0), writes=[b_hs[m]])
            for g, w in enumerate((2, 4, 8, 16)):
                po, pbuf = next_bank()
                for half in range(2):
                    s = half
                    S.dma("sp", lambda e, g=g, half=half, s=s: e.dma_start(
                        out=xst[0:120, s, :], in_=pst_d[half * 8:(half + 1) * 8, :, g * 128:(g + 1) * 128].rearrange("t r c -> (t r) c")), writes=[b_xst[s]])
                    S.c("pe", lambda e, g=g, half=half, s=s, po=po: e.matmul(po[:, half * 8:(half + 1) * 8], lhsT=xst[0:120, s, :], rhs=selt[0:120, g * 8:(g + 1) * 8],
                                                                        start=True, stop=True), reads=[b_xst[s], b_sel], writes=[pbuf])
                S.c("dve", lambda e, g=g, w=w: e.tensor_scalar(out=ptmp[:, 0, 0:N], in0=v_sb[:, g, 16:16 + N], scalar1=1.0 / w - 1.0, scalar2=None, op0=ALU.mult),
                    reads=[b_v[g]], writes=[b_ptmp[0]])
                S.c("dve", lambda e, w=w, po=po: e.scalar_tensor_tensor(out=pooled[:, 0:N], in0=po[:, 0:N], scalar=1.0 / w, in1=ptmp[:, 0, 0:N], op0=ALU.mult, op1=ALU.add),
                    reads=[pbuf, b_ptmp[0]], writes=[b_pooled])
                po2, pbuf2 = next_bank()
                S.c("pe", lambda e, g=g, po2=po2: e.matmul(po2[:, 0:N], lhsT=pool_w_sb[:, g, :], rhs=pooled[:, 0:N], start=True, stop=True),
                    reads=[b_poolw, b_pooled], writes=[pbuf2])
                S.c("act", lambda e, g=g, po2=po2: e.activation(out=hTs[:, 4 + g, :], in_=po2[:, 0:N], func=AF.Copy, scale=vecs[:, V_PS + g:V_PS + g + 1]),
                    reads=[pbuf2, b_vecs], writes=[b_hs[4 + g]])

            def trv(e):
                last = None
                for g in range(4):
                    last = e.transpose(out=PA[0:N, g * 128:(g + 1) * 128], in_=v_sb[:, g, 16:16 + N], identity=ident[:])
                return last
            S.c("pe", trv, reads=b_v + [b_ident], writes=[bank[0][0]])
            S.c("act", lambda e: e.activation(out=x_sb[0:N, 1, 0:512], in_=PA[0:N, 0:512], func=AF.Copy), reads=[bank[0][0]], writes=[b_x[1]])
            S.dma("sp", lambda e: e.dma_start(out=spool_d[:, 14, :], in_=x_sb[0:N, 1, 0:512]), reads=[b_x[1]], out=True)

            load_wout()

            def oproj_s(e):
                last = None
                for h in range(2):
                    for k in range(8):
                        last = e.matmul(pbank(2, h)[0:N, :], lhsT=hTs[:, k, :], rhs=w_out_sb[:, k, h * 512:(h + 1) * 512], start=(k == 0), stop=(k == 7))
                return last
            S.c("pe", oproj_s, reads=[b_wout] + hcol, writes=[bank[2][0], bank[2][1]])
            S.c("dve", lambda e: e.tensor_tensor(out=xs_sb[0:N, :], in0=xs_sb[0:N, :], in1=PC[0:N, :], op=ALU.add),
                reads=[b_xs, bank[2][0], bank[2][1]], writes=[b_xs])
            norm_to_hT([(xs_sb[0:N, :], b_xs, 0, N)], V_G2, dst=hTs, dbufs=b_hs)
            fence()
            cst_tm = av(0, 5632)
            S.dma("sp", lambda e: e.dma_start(out=cst_tm[0:32, :], in_=cst_d.rearrange("t r f -> (t r) f")), reads=[b_fence], writes=[b_cst])
            for rnd in range(3):
                c0 = rnd * 16
                ncx = min(16, 2 * NF - c0)

                def trc(e, c0=c0, ncx=ncx):
                    last = None
                    for ci in range(ncx):
                        last = e.transpose(out=PA[:, ci * 32:(ci + 1) * 32], in_=cst_tm[0:32, (c0 + ci) * 128:(c0 + ci + 1) * 128], identity=ident[0:32, 0:32])
                    return last
                S.c("pe", trc, reads=[b_cst, b_ident], writes=[bank[0][0]])
                S.c("act", lambda e, c0=c0, ncx=ncx: e.activation(out=cbufT[:, c0:c0 + ncx, :], in_=PA[:, 0:ncx * 32].rearrange("p (c n) -> p c n", c=ncx), func=AF.Copy),
                    reads=[bank[0][0]], writes=[b_cbufT])
            fence()

        def sample_up(f, slot):
            pr = f % 2
            cs = f % 3
            psu = [PA[:, pr * 512 + gv * NS:pr * 512 + (gv + 1) * NS] for gv in range(2)]

            def up(e):
                last = None
                for gv in range(2):
                    for k in range(8):
                        last = e.matmul(psu[gv], lhsT=w_up_ring[slot][:, k, gv * 128:(gv + 1) * 128], rhs=hTs[:, k, :], start=(k == 0), stop=(k == 7))
                return last
            S.c("pe", up, reads=[b_wup[slot]] + b_hs, writes=[bank[0][pr]])
            for gv in range(2):
                ch = f + gv * NF
                cw = lambda k, ch=ch: vecs[:, V_CW + ch * 3 + k:V_CW + ch * 3 + k + 1]
                cb2 = cbufT[:, ch, :].rearrange("p (t r) -> p t r", r=2)
                cc = cvS[:, cs, gv, :]
                bc = b_cvS[cs][gv]
                S.c("act", lambda e, gv=gv, ch=ch: e.activation(out=hupS[:, ch, :], in_=psu[gv], func=AF.Copy), reads=[bank[0][pr]], writes=[b_hupS])
                S.c("act", lambda e, gv=gv, ch=ch, cw=cw, cc=cc: e.activation(out=cc, in_=psu[gv], func=AF.Identity, scale=cw(2),
                                                                       bias=vecs[:, V_CB + ch:V_CB + ch + 1]), reads=[bank[0][pr], b_vecs], writes=[bc])
                S.c("dve", lambda e, cw=cw, cb2=cb2, cc=cc: e.scalar_tensor_tensor(out=cc, in0=cb2[:, :, 1], scalar=cw(1), in1=cc, op0=ALU.mult, op1=ALU.add),
                    reads=[b_cbufT, b_vecs, bc], writes=[bc])
                S.c("dve", lambda e, cw=cw, cb2=cb2, cc=cc: e.scalar_tensor_tensor(out=cc, in0=cb2[:, :, 0], scalar=cw(0), in1=cc, op0=ALU.mult, op1=ALU.add),
                    reads=[b_cbufT, b_vecs, bc], writes=[bc])
            S.c("act", lambda e: e.activation(out=cvS[:, cs, 0, :], in_=cvS[:, cs, 0, :], func=AF.Gelu), reads=[b_cvS[cs][0]], writes=[b_cvS[cs][0]])
            S.c("dve", lambda e: e.tensor_tensor(out=actS[:, f, :], in0=cvS[:, cs, 0, :], in1=cvS[:, cs, 1, :], op=ALU.mult), reads=b_cvS[cs], writes=[b_actS])

        def sample_down(m, slot):
            s_ = m % 2
            psd = pbank(2 + s_, 0)[:, 0:NS]

            def down(e):
                last = None
                for k in range(NF):
                    last = e.matmul(psd, lhsT=w_dn_ring[slot][:, k, :], rhs=actS[:, k, :], start=(k == 0), stop=(k == NF - 1))
                return last
            S.c("pe", down, reads=[b_wdn[slot], b_actS], writes=[bank[2 + s_][0]])
            S.c("act", lambda e: e.activation(out=xst[:, s_, 0:N], in_=psd, func=AF.Copy), reads=[bank[2 + s_][0]], writes=[b_xst[s_]])

        def sample_btr(m):
            s_ = m % 2
            pst_ = pbank(2 + s_, 1)[0:N, 0:128]
            S.c("pe", lambda e: e.transpose(out=pst_, in_=xst[:, s_, 0:N], identity=ident[:]),
                reads=[b_xst[s_], b_ident], writes=[bank[2 + s_][1]])
            S.c("dve", lambda e: e.tensor_tensor(out=xs_sb[0:N, m * 128:(m + 1) * 128], in0=xs_sb[0:N, m * 128:(m + 1) * 128], in1=pst_, op=ALU.add),
                reads=[b_xs, bank[2 + s_][1]], writes=[b_xs])

        def sample_phase_C():
            for rnd in range(6):
                c0 = rnd * 8
                ncx = min(8, 2 * NF - c0)
                s = rnd % 2

                def trhup(e, c0=c0, ncx=ncx):
                    last = None
                    for ci in range(ncx):
                        last = e.transpose(out=PA[0:N, ci * 128:(ci + 1) * 128], in_=hupS[:, c0 + ci, :], identity=ident[:])
                    return last
                S.c("pe", trhup, reads=[b_hupS, b_ident], writes=[bank[0][0], bank[0][1]])
                S.c("act", lambda e, s=s, ncx=ncx: e.activation(out=scr[0:N, s, 0:ncx * 128], in_=PA[0:N, 0:ncx * 128], func=AF.Copy),
                    reads=[bank[0][0], bank[0][1]], writes=[b_scr[s]])
                S.dma("sp", lambda e, s=s, c0=c0, ncx=ncx: e.dma_start(out=sconv_d[:, 1, c0 * 128:(c0 + ncx) * 128], in_=scr[0:N, s, 0:ncx * 128]),
                      reads=[b_scr[s]], out=True)
            rms_stats(1, [xs_sb[0:N, :]], [b_xs])
            S.c("dve", lambda e: e.scalar_tensor_tensor(out=scr[0:N, 0, :], in0=xs_sb[0:N, :], scalar=stat[0:N, 8:9], in1=g3b[0:N, :],
                                                         op0=ALU.mult, op1=ALU.mult), reads=[b_xs, b_stat, b_g3b], writes=[b_scr[0]])
            S.dma("sp", lambda e: e.dma_start(out=ys_d, in_=scr[0:N, 0, :]), reads=[b_scr[0]], out=True)

        def block_early(blk):
            last_blk = (blk == NBLK - 1)
            if blk > 0:
                S.c("pool", lambda e: e.tensor_copy(out=v_sb[:, :, 1:16], in_=v_sb[:, :, BT + 1:BT + 16]), reads=b_v, writes=b_v)
            all_h = flat(b_h)
            for m in range(8):
                slot = load_win(m)
                po, pbuf = next_bank()

                def proj(e, slot=slot, po=po):
                    last = None
                    for k in range(8):
                        last = e.matmul(po, lhsT=w_in_ring[slot][:, k, :], rhs=hT[:, k, :], start=(k == 0), stop=(k == 7))
                    return last
                S.c("pe", proj, reads=[b_win[slot]] + all_h, writes=[pbuf])
                if m < 4:
                    S.c("act", lambda e, m=m, po=po: e.activation(out=uT[:, m, :], in_=po, func=AF.Copy), reads=[pbuf], writes=[b_uT[m]])
                else:
                    S.c("act", lambda e, m=m, po=po: e.activation(out=v_sb[:, m - 4, 16:16 + BT], in_=po, func=AF.Copy), reads=[pbuf], writes=[b_v[m - 4]])

            for g, w in enumerate((2, 4, 8, 16)):
                vg = v_sb[:, g, :]
                nsteps = g + 1
                cur = vg
                curb = [b_v[g]]
                sh = 1
                for si in range(nsteps):
                    dst = ptmp[:, si % 2, :]
                    lo = 2 * sh
                    S.c("dve", lambda e, dst=dst, cur=cur, sh=sh: e.tensor_tensor(out=dst[:, 2 * sh - 1:16 + BT], in0=cur[:, 2 * sh - 1:16 + BT],
                                                                          in1=cur[:, sh - 1:16 + BT - sh], op=ALU.add),
                        reads=curb, writes=[b_ptmp[si % 2]])
                    cur = dst
                    curb = [b_ptmp[si % 2]]
                    sh *= 2
                S.c("dve", lambda e, cur=cur, w=w, vg=vg: e.scalar_tensor_tensor(out=pooled, in0=cur[:, 16:16 + BT], scalar=1.0 / w, in1=vg[:, 16:16 + BT],
                                                                             op0=ALU.mult, op1=ALU.subtract),
                    reads=curb + [b_v[g]], writes=[b_pooled])
                if blk == 0:
                    o2 = ptmp[:, (nsteps) % 2, :]
                    S.c("dve", lambda e, cur=cur, w=w, o2=o2: e.tensor_tensor(out=o2[:, 0:w - 1], in0=cur[:, 16:16 + w - 1], in1=V(V_IC, w - 1), op=ALU.mult),
                        reads=curb + [b_vecs], writes=[b_ptmp[nsteps % 2]])
                    S.c("dve", lambda e, w=w, o2=o2, vg=vg: e.tensor_tensor(out=pooled[:, 0:w - 1], in0=o2[:, 0:w - 1], in1=vg[:, 16:16 + w - 1], op=ALU.subtract),
                        reads=[b_ptmp[nsteps % 2], b_v[g]], writes=[b_pooled])
                po, pbuf = next_bank()
                S.c("pe", lambda e, g=g, po=po: e.matmul(po, lhsT=pool_w_sb[:, g, :], rhs=pooled, start=True, stop=True),
                    reads=[b_poolw, b_pooled], writes=[pbuf])
                S.c("act", lambda e, g=g, po=po: e.activation(out=hT[:, 4 + g, :], in_=po, func=AF.Copy, scale=vecs[:, V_PS + g:V_PS + g + 1]),
                    reads=[pbuf, b_vecs], writes=b_h[4 + g])
            if last_blk:
                S.dma("sp", lambda e: [e.dma_start(out=ppool_d[:, g * 128:(g + 1) * 128].rearrange("t c -> c t"), in_=v_sb[:, g, BT + 1:BT + 16],
                                                    allow_slow_non_contiguous=True) for g in range(4)], reads=b_v, out=True, ndma=4)


        for blk in range(nblk):
            r0 = blk * BT
            last_blk = (blk == NBLK - 1)
            xb, bx = (x_sb, b_x) if blk % 2 == 0 else (x_sb2, b_x2)
            xn_, bxn = (x_sb2, b_x2) if blk % 2 == 0 else (x_sb, b_x)
            mrg = do_sample and blk == nblk - 1
            if mrg:
                sample_phase_A()
            if blk == 0:
                for t in range(4):
                    S.dma("sp", lambda e, t=t, r0=r0, xb=xb: e.dma_start(out=xb[:, t, :], in_=x_d[r0 + t * 128:r0 + (t + 1) * 128, :]), writes=[bx[t]])
            if blk == 0:
                norm_to_hT([(xb[:, t, :], bx[t], t * 128, 128) for t in range(4)], V_G1)
            if blk == 0 or mrg:
                block_early(blk)
            if do_s5 and stage >= 1:
                PCDv = PCD[:, :].rearrange("p (a k r c) -> p a k r c", a=4, k=4, r=2)
                PCv, PDv = PCDv[:, :, :, 0, :], PCDv[:, :, :, 1, :]
                v4 = lambda a: a.rearrange("p (a k) c -> p a k c", a=4)

                def lvl0(e):
                    last = None
                    for ri in range(2):
                        for kc in range(4):
                            for j in range(T8):
                                for q4 in range(4):
                                    c0_ = q4 * 512 + (kc * 2 + ri) * CH
                                    last = e.matmul(PCD[:, c0_:c0_ + CH], lhsT=Bst[32 * q4:32 * q4 + 32, kc, j, ri, :],
                                                    rhs=uT[32 * q4:32 * q4 + 32, kc, j:BT:T8], start=(j == 0), stop=(j == T8 - 1),
                                                    tile_position=(32 * q4, 0))
                    return last
                S.c("pe", lvl0, reads=[b_Bst] + b_uT, writes=[bank[2][0], bank[2][1], bank[3][0], bank[3][1]])
                if stage >= 2:
                    bC, bD = [bank[2][0], bank[2][1]], [bank[3][0], bank[3][1]]
                    A1, B1, C1, D1 = tA[:, :, 1:CH + 1], tB[:, :, 1:CH + 1], tC[:, :, 1:CH + 1], tD[:, :, 1:CH + 1]
                    TT = lambda eng, o, a, b_, op, rd, wr: S.c(eng, lambda e: e.tensor_tensor(out=o, in0=a, in1=b_, op=op), reads=rd, writes=wr)
                    TT("dve", v4(A1), PCv, v4(tabc[:]), ALU.mult, bC + bD + [b_tab], [b_tA])
                    TT("dve", v4(B1), PDv, v4(tabs[:]), ALU.mult, bC + bD + [b_tab], [b_tB])
                    TT("dve", A1, A1, B1, ALU.add, [b_tA, b_tB], [b_tA])
                    TT("dve", v4(C1), PDv, v4(tabc[:]), ALU.mult, bC + bD + [b_tab], [b_tC])
                    TT("dve", v4(D1), PCv, v4(tabs[:]), ALU.mult, bC + bD + [b_tab], [b_tD])
                    TT("dve", C1, C1, D1, ALU.subtract, [b_tC, b_tD], [b_tC])
                    S.c("pool", lambda e: e.tensor_copy(out=tA[:, :, 0], in_=sp_small[:, GCR, :]), reads=[b_gc], writes=[b_tA])
                    S.c("pool", lambda e: e.tensor_copy(out=tC[:, :, 0], in_=sp_small[:, GCI, :]), reads=[b_gc], writes=[b_tC])
                    fl = lambda a: a.rearrange("p q c -> p (q c)")
                    S.c("dve", lambda e: e.tensor_tensor_scan(out=fl(tB), data0=fl(rdec[:]), data1=fl(tA), initial=0.0, op0=ALU.mult, op1=ALU.add),
                        reads=[b_rdec, b_tA], writes=[b_tB])
                    S.c("dve", lambda e: e.tensor_tensor_scan(out=fl(tD), data0=fl(rdec[:]), data1=fl(tC), initial=0.0, op0=ALU.mult, op1=ALU.add),
                        reads=[b_rdec, b_tC], writes=[b_tD])
                    S.c("pool", lambda e: e.tensor_copy(out=sp_small[:, GCR, :], in_=tB[:, :, CH]), reads=[b_tB], writes=[b_gc])
                    S.c("pool", lambda e: e.tensor_copy(out=sp_small[:, GCI, :], in_=tD[:, :, CH]), reads=[b_tD], writes=[b_gc])
                    TT("dve", A1, B1, tabc[:], ALU.mult, [b_tB, b_tab], [b_tA])
                    TT("dve", C1, D1, tabs[:], ALU.mult, [b_tD, b_tab], [b_tC])
                    TT("dve", Hb[:, :, 0, 1:CH + 1], A1, C1, ALU.subtract, [b_tA, b_tC], [b_Hb])
                    if last_blk:
                        TT("pool", sp_small[:, HFR, :], tA[:, :, CH], tC[:, :, CH], ALU.subtract, [b_tA, b_tC], [b_hfin])
                    TT("dve", A1, B1, tabs[:], ALU.mult, [b_tB, b_tab], [b_tA])
                    TT("dve", C1, D1, tabc[:], ALU.mult, [b_tD, b_tab], [b_tC])
                    TT("dve", Hb[:, :, 1, 1:CH + 1], A1, C1, ALU.add, [b_tA, b_tC], [b_Hb])
                    if last_blk:
                        TT("pool", sp_small[:, HFI, :], tA[:, :, CH], tC[:, :, CH], ALU.add, [b_tA, b_tC], [b_hfin])
                        S.dma("sp", lambda e: e.dma_start(out=pre_d.rearrange("q p -> p q"), in_=sp_small[:, HFR, :], allow_slow_non_contiguous=True),
                              reads=[b_hfin], out=True)
                        S.dma("sp", lambda e: e.dma_start(out=pim_d.rearrange("q p -> p q"), in_=sp_small[:, HFI, :], allow_slow_non_contiguous=True),
                              reads=[b_hfin], out=True)
                if stage >= 4:
                    for kc in range(4):
                        pi = 2 + (kc % 2)
                        Y1, Y2 = pbank(pi, 0), pbank(pi, 1)

                        def ystage(e, kc=kc, Y1=Y1, Y2=Y2):
                            last = None
                            for jp in range(T8):
                                for j in range(jp + 1):
                                    last = e.matmul(Y1[:, jp:BT:T8], lhsT=Kblk[:, kc, jp - j, :], rhs=uT[:, kc, j:BT:T8], start=(j == 0), stop=(j == jp))
                            for jp in range(T8):
                                for q4 in range(4):
                                    q = q4 * 4 + kc
                                    o = Y2[32 * q4:32 * q4 + 32, jp:BT:T8]
                                    e.matmul(o, lhsT=Cst[:, q, jp + 1, 0, :], rhs=Hb[:, q, 0, 0:CH], start=True, stop=False, tile_position=(0, 32 * q4))
                                    last = e.matmul(o, lhsT=Cst[:, q, jp + 1, 1, :], rhs=Hb[:, q, 1, 0:CH], start=False, stop=True, tile_position=(0, 32 * q4))
                            return last
                        S.c("pe", ystage, reads=[b_Kblk, b_Cst, b_uT[kc], b_Hb, b_Hb0], writes=[bank[pi][0], bank[pi][1]])
                        s = kc % 2
                        S.c("act", lambda e, s=s, Y1=Y1: e.activation(out=ytmp[:, s, :], in_=Y1, func=AF.Copy), reads=[bank[pi][0]], writes=[b_ytmp[s]])
                        S.c("dve", lambda e, s=s, Y2=Y2: e.tensor_tensor(out=ytmp[:, s, :], in0=ytmp[:, s, :], in1=Y2, op=ALU.add),
                            reads=[b_ytmp[s], bank[pi][1]], writes=[b_ytmp[s]])
                        S.c("dve", lambda e, s=s, kc=kc: e.scalar_tensor_tensor(out=ytmp[:, s, :], in0=uT[:, kc, :], scalar=vecs[:, V_DS + kc:V_DS + kc + 1],
                                                                            in1=ytmp[:, s, :], op0=ALU.mult, op1=ALU.add),
                            reads=[b_ytmp[s], b_uT[kc], b_vecs], writes=[b_ytmp[s]])
                        S.c("act", lambda e, s=s, kc=kc: e.activation(out=yg[:, kc, :], in_=ytmp[:, s, :], func=AF.Gelu), reads=[b_ytmp[s]], writes=[b_yg[kc]])
                if stage >= 5:
                    S.c("pool", lambda e: e.tensor_copy(out=Hb[:, :, :, 0], in_=Hb[:, :, :, CH]), reads=[b_Hb], writes=[b_Hb0])
                    for m in range(4):
                        po, pbuf = next_bank()

                        def glu(e, m=m, po=po):
                            last = None
                            for k in range(4):
                                last = e.matmul(po, lhsT=w_glu_sb[:, k, m * 128:(m + 1) * 128], rhs=yg[:, k, :], start=(k == 0), stop=(k == 3))
                            return last
                        S.c("pe", glu, reads=[b_wglu] + b_yg, writes=[pbuf])
                        s = m % 2
                        S.c("act", lambda e, s=s, po=po: e.activation(out=ytmp[:, s, :], in_=po, func=AF.Sigmoid), reads=[pbuf], writes=[b_ytmp[s]])
                        S.c("dve", lambda e, s=s, m=m: e.tensor_tensor(out=hT[:, m, :], in0=yg[:, m, :], in1=ytmp[:, s, :], op=ALU.mult),
                            reads=[b_ytmp[s], b_yg[m]], writes=b_h[m])
                if stage < 5:
                    for m in range(4):
                        S.c("dve", lambda e, m=m: e.memset(hT[:, m, :], 0.0), writes=b_h[m])
            else:
                for m in range(4):
                    S.c("dve", lambda e, m=m: e.memset(hT[:, m, :], 0.0), writes=b_h[m])

            load_wout()
            S.c("dve", lambda e: e.memset(stat[:, 0:8], 0.0), writes=[b_stat])
            for t in range(4):
                pi = 2 + (t % 2)

                def oproj(e, t=t, pi=pi):
                    last = None
                    for h in range(2):
                        for k in range(8):
                            last = e.matmul(pbank(pi, h), lhsT=hT[:, k, t * 128:(t + 1) * 128], rhs=w_out_sb[:, k, h * 512:(h + 1) * 512],
                                            start=(k == 0), stop=(k == 7))
                    return last
                S.c("pe", oproj, reads=[b_wout] + [b_h[k][t] for k in range(8)], writes=[bank[pi][0], bank[pi][1]])
                S.c("dve", lambda e, t=t, pi=pi, xb=xb: e.tensor_tensor(out=xb[:, t, :], in0=xb[:, t, :], in1=PSn[pi][:, :], op=ALU.add),
                    reads=[bx[t], bank[pi][0], bank[pi][1]], writes=[bx[t]])
            norm_to_hT([(xb[:, t, :], bx[t], t * 128, 128) for t in range(4)], V_G2, premem=True)
            fence()
            if blk + 1 < nblk:
                for t in range(4):
                    S.dma("sp", lambda e, t=t, r1=r0 + BT, xn_=xn_: e.dma_start(out=xn_[:, t, :], in_=x_d[r1 + t * 128:r1 + (t + 1) * 128, :]), writes=[bxn[t]])
            all_h = flat(b_h)
            if blk > 0:
                cwv = vecs[:, V_CW:V_CW + 132].rearrange("p (c k) -> p c k", k=3)
                S.c("pool", lambda e: e.tensor_tensor(out=hfix[:], in0=hhalo[:], in1=cwv[:, :, 0:1].broadcast_to([128, 2 * NF, 2]), op=ALU.mult),
                    reads=[b_hhalo, b_vecs], writes=[b_hfix])
                S.c("pool", lambda e: e.tensor_tensor(out=hfix2[:], in0=hhalo[:, :, 1], in1=cwv[:, :, 1], op=ALU.mult),
                    reads=[b_hhalo, b_vecs], writes=[b_hfix2])
                S.c("pool", lambda e: e.tensor_tensor(out=hfix[:, :, 0], in0=hfix[:, :, 0], in1=hfix2[:], op=ALU.add),
                    reads=[b_hfix, b_hfix2], writes=[b_hfix])
            cs_of = lambda f_: f_ % 3
            pi_of = lambda f_: 1 + (f_ % 3)
            for f in range(NF):
                slot = load_wup(f)
                pi = 1 + (f % 3)
                cs = f % 3

                def up(e, slot=slot, pi=pi):
                    last = None
                    for gv in range(2):
                        for k in range(8):
                            last = e.matmul(pbank(pi, gv), lhsT=w_up_ring[slot][:, k, gv * 128:(gv + 1) * 128], rhs=hT[:, k, :], start=(k == 0), stop=(k == 7))
                    return last
                S.c("pe", up, reads=[b_wup[slot]] + all_h, writes=[bank[pi][0], bank[pi][1]])
                if mrg:
                    sample_up(f, slot)
                for gv in range(2):
                    ch = f + gv * NF
                    cw2 = vecs[:, V_CW + ch * 3 + 2:V_CW + ch * 3 + 3]
                    S.c("act", lambda e, gv=gv, ch=ch, cw2=cw2, f=f: e.activation(out=cvt3[:, cs_of(f), gv, :], in_=pbank(pi_of(f), gv), func=AF.Identity, scale=cw2,
                                                                         bias=vecs[:, V_CB + ch:V_CB + ch + 1]),
                        reads=[bank[pi][gv], b_vecs], writes=[b_cvt3[cs][gv]])
                for k_, sh in ((1, 1), (0, 2)):
                    for gv in range(2):
                        ch = f + gv * NF
                        cwk = vecs[:, V_CW + ch * 3 + k_:V_CW + ch * 3 + k_ + 1]
                        S.c("dve", lambda e, gv=gv, cwk=cwk, sh=sh, cs=cs, pi=pi: e.scalar_tensor_tensor(
                            out=cvt3[:, cs, gv, sh:BT], in0=pbank(pi, gv)[:, 0:BT - sh], scalar=cwk, in1=cvt3[:, cs, gv, sh:BT], op0=ALU.mult, op1=ALU.add),
                            reads=[bank[pi][gv], b_vecs, b_cvt3[cs][gv]], writes=[b_cvt3[cs][gv]])
                if blk > 0:
                    S.c("dve", lambda e, f=f, cs=cs: e.tensor_tensor(out=cvt3[:, cs, :, 0:2], in0=cvt3[:, cs, :, 0:2], in1=hfix[:, f:2 * NF:NF, :], op=ALU.add),
                        reads=[b_hfix] + b_cvt3[cs], writes=b_cvt3[cs])
                S.c("dve", lambda e, f=f, pi=pi: e.tensor_copy(out=hhalo[:, f:2 * NF:NF, :], in_=PSn[pi][:, :].rearrange("p (g n) -> p g n", g=2)[:, :, BT - 2:BT]),
                    reads=[bank[pi][0], bank[pi][1]], writes=[b_hhalo])
                S.c("act", lambda e, cs=cs: e.activation(out=cvt3[:, cs, 0, :], in_=cvt3[:, cs, 0, :], func=AF.Gelu), reads=[b_cvt3[cs][0]], writes=[b_cvt3[cs][0]])
                S.c("pool", lambda e, f=f, cs=cs: e.tensor_tensor(out=actT[:, f, :], in0=cvt3[:, cs, 0, :], in1=cvt3[:, cs, 1, :], op=ALU.mult),
                    reads=b_cvt3[cs], writes=[b_act[f]])
            if last_blk:
                S.dma("sp", lambda e: [e.dma_start(out=pconv_d[r, :].rearrange("(c p) -> p c", p=128), in_=hhalo[:, :, r], allow_slow_non_contiguous=True)
                                        for r in range(2)], reads=[b_hhalo], out=True, ndma=2)
            def emit_down(m):
                slot = load_wdn(m)
                po, pbuf = next_bank()
                if mrg:
                    sample_down(m, slot)

                def down(e, slot=slot, po=po):
                    last = None
                    for k in range(NF):
                        last = e.matmul(po, lhsT=w_dn_ring[slot][:, k, :], rhs=actT[:, k, :], start=(k == 0), stop=(k == NF - 1))
                    return last
                S.c("pe", down, reads=[b_wdn[slot]] + b_act, writes=[pbuf])
                s = m % 2
                S.c("act", lambda e, s=s, po=po: e.activation(out=scr[:, s, 0:BT], in_=po, func=AF.Copy), reads=[pbuf], writes=[b_scr[s]])

            def emit_btr(m):
                s = m % 2
                hb = m % 2

                def btr(e, s=s, hb=hb):
                    last = None
                    for t in range(4):
                        last = e.transpose(out=PA[:, hb * 512 + t * 128:hb * 512 + (t + 1) * 128], in_=scr[:, s, t * 128:(t + 1) * 128], identity=ident[:])
                    return last
                S.c("pe", btr, reads=[b_scr[s], b_ident], writes=[bank[0][hb]])
                if mrg:
                    sample_btr(m)
                S.c("dve", lambda e, m=m, hb=hb, xb=xb: e.tensor_tensor(out=xb[:, :, m * 128:(m + 1) * 128], in0=xb[:, :, m * 128:(m + 1) * 128],
                                                             in1=PA[:, hb * 512:(hb + 1) * 512].rearrange("p (t n) -> p t n", t=4), op=ALU.add),
                    reads=bx + [bank[0][hb]], writes=bx)
            emit_down(0)
            if do_s5 and not last_blk:
                cWb = sp_small[:, CW, :].unsqueeze(2).broadcast_to([128, NQ, CH])
                sWb = sp_small[:, SW, :].unsqueeze(2).broadcast_to([128, NQ, CH])
                u1 = cvt3[:, 0, :, :].rearrange("p g n -> p (g n)").rearrange("p (q c) -> p q c", q=NQ)
                u2 = cvt3[:, 1, :, :].rearrange("p g n -> p (g n)").rearrange("p (q c) -> p q c", q=NQ)
                bu1, bu2 = b_cvt3[0], b_cvt3[1]
                TTp = lambda o, a_, b_, op, rd, wr: S.c("pool", lambda e: e.tensor_tensor(out=o, in0=a_, in1=b_, op=op), reads=rd, writes=wr)
                TTp(u1, tabc[:], sWb, ALU.mult, [b_tab, b_sps], bu1)
                TTp(u2, tabs[:], sWb, ALU.mult, [b_tab, b_sps], bu2)
                TTp(tabc[:], tabc[:], cWb, ALU.mult, [b_tab, b_sps], [b_tab])
                TTp(tabc[:], tabc[:], u2, ALU.subtract, [b_tab] + bu2, [b_tab])
                TTp(tabs[:], tabs[:], cWb, ALU.mult, [b_tab, b_sps], [b_tab])
                TTp(tabs[:], tabs[:], u1, ALU.add, [b_tab] + bu1, [b_tab])
            for m in range(8):
                if m + 1 < 8:
                    emit_down(m + 1)
                emit_btr(m)
            fence()
            if blk + 1 < nblk:
                norm_to_hT([(xn_[:, t, :], bxn[t], t * 128, 128) for t in range(4)], V_G1)
                if not (do_sample and blk + 1 == nblk - 1):
                    block_early(blk + 1)
            if mrg:
                sample_phase_C()
            rms_stats(4, [xb[:, t, :] for t in range(4)], bx)
            for t in range(4):
                s = t % 2
                S.c("dve", lambda e, t=t, s=s, xb=xb: e.scalar_tensor_tensor(out=scr[:, s, :], in0=xb[:, t, :], scalar=stat[:, 8 + t:9 + t], in1=g3b[:],
                                                                  op0=ALU.mult, op1=ALU.mult), reads=[bx[t], b_stat, b_g3b], writes=[b_scr[s]])
                S.dma("sp", lambda e, t=t, s=s, r0=r0: e.dma_start(out=y_d[r0 + t * 128:r0 + (t + 1) * 128, :], in_=scr[:, s, :]), reads=[b_scr[s]], out=True)

        S.emit(st)
        build_program.stats = S.stats
    return nc


_CACHE = {}


def _pack_inputs(inp):
    f32 = np.float32
    A = lambda a: np.ascontiguousarray(np.asarray(a, dtype=f32))
    vecs = np.zeros((128, NV), f32)
    vecs[:, V_G1:V_G1 + 8] = A(inp["norm_mix_g"])[0].reshape(8, 128).T
    vecs[:, V_G2:V_G2 + 8] = A(inp["norm_ffn_g"])[0].reshape(8, 128).T
    vecs[:, V_PS:V_PS + 4] = A(inp["pool_scale"])[0].reshape(4, 128).T
    vecs[:, V_DS:V_DS + 4] = A(inp["s5_d"])[0].reshape(4, 128).T
    vecs[:, V_CW:V_CW + 132] = A(inp["ffn_conv_w"])[0].reshape(3, 44, 128).transpose(2, 1, 0).reshape(128, 132)
    vecs[:, V_CB:V_CB + 44] = A(inp["ffn_conv_b"])[0].reshape(44, 128).T
    vecs[:, V_IC:V_IC + 16] = (1.0 / np.arange(1, 17, dtype=np.float64)).astype(f32)[None, :]
    vecs[:, V_IO:V_IO + 64] = np.arange(64, dtype=f32)[None, :]
    s5p = np.zeros((128, NSP), f32)
    perm = np.array([(qp % 4) * 4 + qp // 4 for qp in range(16)])
    lay = lambda a: A(a)[0].reshape(16, 2, 64)[perm].transpose(1, 2, 0).reshape(128, 16)
    s5p[:, SP_ARE:SP_ARE + 16] = lay(inp["s5_a_re"])
    s5p[:, SP_AIM:SP_AIM + 16] = lay(inp["s5_a_im"])
    s5p[:, SP_LDT:SP_LDT + 16] = np.broadcast_to(A(inp["s5_log_dt"])[0].reshape(16, 2, 1)[perm], (16, 2, 64)).transpose(1, 2, 0).reshape(128, 16)
    for off, key in ((SP_BRE, "s5_b_re"), (SP_BIM, "s5_b_im")):
        b = A(inp[key])[0].reshape(16, 2, 64, 16)[perm]
        o = np.zeros((2, 64, 16, 2, 16), f32)
        for g2 in range(2):
            o[g2, :, :, g2, :] = b[:, g2].transpose(1, 0, 2)
        s5p[:, off:off + 512] = o.reshape(128, 512)
    for off, key in ((SP_CRE, "s5_c_re"), (SP_CIM, "s5_c_im")):
        c = A(inp[key])[0].reshape(16, 2, 16, 64)[perm]
        o = np.zeros((2, 64, 16, 2, 16), f32)
        for g2 in range(2):
            o[g2, :, :, g2, :] = c[:, g2].transpose(2, 0, 1)
        s5p[:, off:off + 512] = o.reshape(128, 512)
    sel = np.zeros((128, 32), f32)
    for g, w in enumerate((2, 4, 8, 16)):
        for t in range(8):
            for r in range(16 - w, 15):
                sel[t * 15 + r, g * 8 + t] = 1.0
    shared = {
        "w_in": A(inp["w_in"])[0], "w_glu": A(inp["s5_w_glu"])[0], "pool_w": A(inp["pool_w"])[0].reshape(512, 128),
        "w_out": A(inp["w_out"])[0], "w_up": A(inp["ffn_w_up"])[0], "w_down": A(inp["ffn_w_down"])[0],
        "vecs": vecs, "g3b": np.ascontiguousarray(np.broadcast_to(A(inp["norm_final_g"])[None, :], (128, D))),
        "ident": np.eye(128, dtype=f32), "s5p": s5p, "sel": sel,
    }
    maps = []
    for i in range(NCORES):
        sl = slice(i * NS, (i + 1) * NS)
        m = dict(shared)
        m["x"] = A(inp["x_prompt"])[i]
        m["xs"] = A(inp["x_sample"])[sl, 0, :]
        m["s5re"] = A(inp["state_s5_re"])[0, sl].reshape(NS, 2048)
        m["s5im"] = A(inp["state_s5_im"])[0, sl].reshape(NS, 2048)
        m["pst"] = A(inp["state_pool"])[0, sl]
        m["cst"] = A(inp["state_ffn_conv"])[0, sl]
        maps.append(m)
    return maps


def kernel(**inputs):
    if "nc" not in _CACHE:
        _CACHE["nc"] = build_program()
    nc = _CACHE["nc"]
    maps = _pack_inputs(inputs)
    res = run_bass_kernel_spmd(nc, maps, core_ids=list(range(NCORES)))
    R = res.results
    f32 = np.float32
    y_prompt = np.stack([R[i]["y"] for i in range(NCORES)]).astype(f32)
    y_sample = np.concatenate([R[i]["ys"] for i in range(NCORES)])[:, None, :].astype(f32)
    inv = np.array([(q % 4) * 4 + q // 4 for q in range(16)])
    p_re = np.stack([R[i]["p_re"][inv].reshape(NG, NP_) for i in range(NCORES)])[None].astype(f32)
    p_im = np.stack([R[i]["p_im"][inv].reshape(NG, NP_) for i in range(NCORES)])[None].astype(f32)
    p_pool = np.stack([R[i]["p_pool"] for i in range(NCORES)])[None].astype(f32)
    p_conv = np.stack([R[i]["p_conv"] for i in range(NCORES)])[None].astype(f32)
    s_re = np.concatenate([R[i]["s_re"].reshape(NS, NG, NP_) for i in range(NCORES)])[None].astype(f32)
    s_im = np.concatenate([R[i]["s_im"].reshape(NS, NG, NP_) for i in range(NCORES)])[None].astype(f32)
    s_pool = np.concatenate([R[i]["s_pool"] for i in range(NCORES)])[None].astype(f32)
    s_conv = np.concatenate([R[i]["s_conv"] for i in range(NCORES)])[None].astype(f32)
    return (y_prompt, y_sample, p_re, p_im, p_pool, p_conv, s_re, s_im, s_pool, s_conv)
```

```python
import math
from contextlib import ExitStack

import numpy as np
import concourse.bass as bass
import concourse.mybir as mybir
from concourse.bass_utils import run_bass_kernel_spmd

F32 = mybir.dt.float32
BF16 = mybir.dt.bfloat16
AF = mybir.ActivationFunctionType
ALU = mybir.AluOpType
AX = mybir.AxisListType


class Buf:
    __slots__ = ("name", "writer", "readers")

    def __init__(self, name):
        self.name = name
        self.writer = None
        self.readers = []


class Op:
    __slots__ = ("eng", "fn", "kind", "deps", "signal", "tok", "waits", "clock", "ndma", "idx")


class Sched:
    ENGS = ("pe", "act", "dve", "pool", "sp")
    NRING = 40

    def __init__(self, nc):
        self.nc = nc
        self.ops = []
        self.ring_n = 0
        self.ring_cum = [0] * self.NRING
        self.ring_last = [None] * self.NRING
        self.out_dmas = []
        self.n_sw = 0

    def buf(self, name):
        return Buf(name)

    def _add(self, eng, fn, reads, writes, kind, ndma=0):
        op = Op()
        op.eng, op.fn, op.kind, op.ndma = eng, fn, kind, ndma
        op.signal = False
        op.idx = len(self.ops)
        deps = []
        for b in reads:
            if b.writer is not None:
                deps.append(b.writer)
        for b in writes:
            if b.writer is not None:
                deps.append(b.writer)
            deps.extend(b.readers)
        for b in reads:
            b.readers.append(op)
        for b in writes:
            b.writer = op
            b.readers = []
        if kind == "dma" and eng == "pool":
            op.tok = (("sw", self.n_sw), 16 * ndma)
            self.n_sw += 1
        elif kind == "dma":
            r = self.ring_n % self.NRING
            self.ring_n += 1
            if self.ring_last[r] is not None:
                deps.append(self.ring_last[r])
            self.ring_cum[r] += 16 * ndma
            op.tok = (("dma", r), self.ring_cum[r])
            self.ring_last[r] = op
        else:
            op.tok = None
        op.deps = [d for d in set(deps) if not (d.eng == "pe" and eng == "pe" and d.kind == "c" and kind == "c") and d is not op]
        self.ops.append(op)
        return op

    def c(self, eng, fn, reads=(), writes=()):
        return self._add(eng, fn, list(reads), list(writes), "c")

    def dma(self, eng, fn, reads=(), writes=(), ndma=1, out=False):
        op = self._add(eng, fn, list(reads), list(writes), "dma", ndma)
        if out:
            self.out_dmas.append(op)
        return op

    def emit(self, stack):
        nc = self.nc
        fin = self._add("sp", None, [], [], "c")
        fin.deps = list(self.out_dmas)
        for op in self.ops:
            for d in op.deps:
                d.signal = True
        cnt = {e: 0 for e in self.ENGS}
        for op in self.ops:
            if op.kind == "c" and op.signal:
                cnt[op.eng] += 1
                op.tok = (op.eng, cnt[op.eng])
        known = {e: {} for e in self.ENGS}
        for op in self.ops:
            kn = known[op.eng]
            waits = []
            for d in sorted(op.deps, key=lambda o: o.idx):
                k, v = d.tok
                if kn.get(k, 0) >= v:
                    continue
                waits.append((k, v))
                for kk, vv in d.clock.items():
                    if kn.get(kk, 0) < vv:
                        kn[kk] = vv
            best = {}
            for k, v in waits:
                best[k] = max(best.get(k, 0), v)
            op.waits = list(best.items())
            op.clock = dict(kn)
            if op.tok is not None:
                op.clock[op.tok[0]] = max(op.clock.get(op.tok[0], 0), op.tok[1])
        sems = {}
        for e in ("pe", "act", "dve", "pool", "sp"):
            sems[e] = stack.enter_context(nc.semaphore("s_" + e))
        for r in range(min(self.NRING, max(self.ring_n, 1))):
            sems[("dma", r)] = stack.enter_context(nc.semaphore("s_dma%d" % r))
        for r in range(self.n_sw):
            sems[("sw", r)] = stack.enter_context(nc.semaphore("s_sw%d" % r))
        block = stack.enter_context(nc.Block())
        per = {e: [o for o in self.ops if o.eng == e] for e in self.ENGS}
        self.stats = {e: (len(per[e]), sum(len(o.waits) for o in per[e])) for e in self.ENGS}

        def run(eng_handle, lst):
            for op in lst:
                for k, v in op.waits:
                    eng_handle.wait_ge(sems[k], v)
                if op.fn is None:
                    continue
                res = op.fn(eng_handle)
                if op.kind == "dma":
                    if not isinstance(res, (list, tuple)):
                        res = [res]
                    assert len(res) == op.ndma, (len(res), op.ndma)
                    for ins in res:
                        ins.then_inc(sems[op.tok[0]], 16)
                elif op.signal:
                    if isinstance(res, (list, tuple)):
                        res = res[-1]
                    res.then_inc(sems[op.tok[0]], 1)

        @block.tensor
        def _(e):
            run(e, per["pe"])

        @block.scalar
        def _(e):
            run(e, per["act"])

        @block.vector
        def _(e):
            run(e, per["dve"])

        @block.gpsimd
        def _(e):
            run(e, per["pool"])

        @block.sync
        def _(e):
            run(e, per["sp"])


NCORES = 8
D = 1024
L = 2048
BT = 512
NBLK = L // BT
NS = 16
NG, NP_, NH = 32, 64, 16
NQ = 16
T8 = 8
CH = BT // T8
DFF = 2816
NF = DFF // 128
EPS = 1e-6
MAGIC = 12582912.0
TWO_PI = 2.0 * math.pi
PI_LO = 3.1415925

V_G1, V_G2, V_PS, V_DS, V_CW, V_CB, V_IC, V_IO = 0, 8, 16, 20, 24, 156, 200, 216
NV = 280
SP_ARE, SP_AIM, SP_LDT, SP_BRE, SP_BIM, SP_CRE, SP_CIM = 0, 16, 32, 48, 560, 1072, 1584
NSP = 2096


def build_program(do_s5=True, do_sample=True, nblk=NBLK, stage=99):
    nc = bass.Bass("TRN2", target_bir_lowering=False)
    din = lambda name, shape: nc.dram_tensor(name, list(shape), F32, kind="ExternalInput").ap()
    dout = lambda name, shape: nc.dram_tensor(name, list(shape), F32, kind="ExternalOutput").ap()
    x_d = din("x", [L, D])
    xs_d = din("xs", [NS, D])
    w_in_d = din("w_in", [D, D])
    w_glu_d = din("w_glu", [512, 512])
    pool_w_d = din("pool_w", [512, 128])
    w_out_d = din("w_out", [D, D])
    w_up_d = din("w_up", [D, 2 * DFF])
    w_down_d = din("w_down", [DFF, D])
    vecs_d = din("vecs", [128, NV])
    g3b_d = din("g3b", [128, D])
    ident_d = din("ident", [128, 128])
    s5p_d = din("s5p", [128, NSP])
    s5re_d = din("s5re", [NS, 2048])
    s5im_d = din("s5im", [NS, 2048])
    pst_d = din("pst", [NS, 15, 512])
    cst_d = din("cst", [NS, 2, 2 * DFF])
    y_d = dout("y", [L, D])
    ys_d = dout("ys", [NS, D])
    pre_d = dout("p_re", [NQ, 128])
    pim_d = dout("p_im", [NQ, 128])
    ppool_d = dout("p_pool", [15, 512])
    pconv_d = dout("p_conv", [2, 2 * DFF])
    sre_d = dout("s_re", [NS, 2048])
    sim_d = dout("s_im", [NS, 2048])
    spool_d = dout("s_pool", [NS, 15, 512])
    sconv_d = dout("s_conv", [NS, 2, 2 * DFF])

    wupb = nc.dram_tensor("wupb", [NF, 128, 8, 256], BF16).ap()
    wdnb = nc.dram_tensor("wdnb", [8, 128, NF, 128], BF16).ap()
    winb = nc.dram_tensor("winb", [8, 128, 8, 128], BF16).ap()
    woutb = nc.dram_tensor("woutb", [128, 8, D], BF16).ap()
    st = ExitStack()
    with st:
        S = Sched(nc)
        sbt = lambda name, shape, dt: st.enter_context(nc.sbuf_tensor("sb_" + name, list(shape), dt))
        w_glu_sb = sbt("w_glu_sb", [128, 4, 512], BF16)
        pool_w_sb = sbt("pool_w_sb", [128, 4, 128], BF16)
        NWI, NWU, NWD = 2, 3, 2
        w_in_ring = [sbt("w_in_r%d" % i, [128, 8, 128], BF16) for i in range(NWI)]
        w_up_ring = [sbt("w_up_r%d" % i, [128, 8, 256], BF16) for i in range(NWU)]
        w_dn_ring = [sbt("w_dn_r%d" % i, [128, NF, 128], BF16) for i in range(NWD)]
        vecs = sbt("vecs", [128, NV], F32)
        g3b = sbt("g3b", [128, D], F32)
        ident = sbt("ident", [128, 128], F32)
        identb = sbt("identb", [128, 128], BF16)
        Bst = sbt("Bst", [128, 4, T8, 2, 128], BF16)
        Cst = sbt("Cst", [128, NQ, T8 + 1, 2, 32], BF16)
        Kblk = sbt("Kblk", [128, 4, T8, 128], BF16)
        tabc = sbt("tabc", [128, NQ, CH], F32)
        tabs = sbt("tabs", [128, NQ, CH], F32)
        rdec = sbt("rdec", [128, NQ, CH + 1], F32)
        sp_small = sbt("sp_small", [128, 40, NQ], F32)
        LR, LI, PR0, PI0, CW, SW, GCR, GCI, HFR, HFI, TMPA, TMPB, TMPC, TMPD, ANG0 = 0, 1, 2, 11, 20, 21, 22, 23, 24, 25, 26, 27, 28, 29, 30
        stat = sbt("stat", [128, 16], F32)
        x_sb = sbt("x_sb", [128, 4, D], F32)
        x_sb2 = sbt("x_sb2", [128, 4, D], F32)
        scr = sbt("scr", [128, 2, D], F32)
        hT = sbt("hT", [128, 8, BT], BF16)
        v_sb = sbt("v_sb", [128, 4, 16 + BT], F32)
        Hb = sbt("Hb", [128, NQ, 2, CH + 1], BF16)
        hhalo = sbt("hhalo", [128, 2 * NF, 2], F32)
        hfix = sbt("hfix", [128, 2 * NF, 2], F32)
        hfix2 = sbt("hfix2", [128, 2 * NF], F32)
        ARENA_W = 8712
        arena = sbt("arena", [128, ARENA_W], F32)
        fence_scr = sbt("fence_scr", [128, 2], F32)

        def av(off, nwords, dt=F32, **kw):
            a = arena[:, off:off + nwords]
            if dt != F32:
                a = a.bitcast(dt)
            return a

        actT = av(0, 5632, BF16).rearrange("p (k n) -> p k n", k=NF)
        hup = av(5632, 2056).rearrange("p (s g n) -> p s g n", s=2, g=2)
        cvt = av(7688, 1024).rearrange("p (g n) -> p g n", g=2)
        cvt3 = av(5632, 3072).rearrange("p (s g n) -> p s g n", s=3, g=2)
        w_out_sb = av(0, 4096, BF16).rearrange("p (k n) -> p k n", k=8)
        tA = av(0, 1040).rearrange("p (q c) -> p q c", q=NQ)
        tB = av(1040, 1040).rearrange("p (q c) -> p q c", q=NQ)
        tC = av(2080, 1040).rearrange("p (q c) -> p q c", q=NQ)
        tD = av(3120, 1040).rearrange("p (q c) -> p q c", q=NQ)
        uT = av(4160, 1024, BF16).rearrange("p (k n) -> p k n", k=4)
        yg = av(5184, 1024, BF16).rearrange("p (k n) -> p k n", k=4)
        ptmp = av(6208, 1056).rearrange("p (s n) -> p s n", s=2)
        pooled = av(7264, 256, BF16)
        ytmp = av(7520, 1024).rearrange("p (s n) -> p s n", s=2)
        s5p = av(0, NSP)
        G0r = av(2096, 512).rearrange("p (q h) -> p q h", q=NQ)
        G0i = av(2608, 512).rearrange("p (q h) -> p q h", q=NQ)
        Gnr = av(3120, 512).rearrange("p (q h) -> p q h", q=NQ)
        Gni = av(3632, 512).rearrange("p (q h) -> p q h", q=NQ)
        Gnb = av(4144, 512, BF16).rearrange("p (r q h) -> p r q h", r=2, q=NQ)
        Ccb = av(4656, 512, BF16).rearrange("p (r q h) -> p r q h", r=2, q=NQ)
        pt1 = av(5168, 512).rearrange("p (q h) -> p q h", q=NQ)
        pt2 = av(5680, 512).rearrange("p (q h) -> p q h", q=NQ)
        pang = av(6192, 1024).rearrange("p (q c) -> p q c", q=NQ)
        pang2 = av(7216, 1024).rearrange("p (q c) -> p q c", q=NQ)

        PA = st.enter_context(nc.psum_tensor("psA", [128, 1024], F32))
        PB = st.enter_context(nc.psum_tensor("psB", [128, 1024], F32))
        PCD = st.enter_context(nc.psum_tensor("psCD", [128, 2048], F32))
        PC, PD = PCD[:, 0:1024], PCD[:, 1024:2048]
        PSn = [PA, PB, PC, PD]
        bank = [[S.buf("ps%d_%d" % (i, h)) for h in range(2)] for i in range(4)]

        def pbank(i, h):
            return PSn[i][:, h * 512:(h + 1) * 512]

        b_wout, b_wglu, b_poolw = S.buf("wout"), S.buf("wglu"), S.buf("poolw")
        b_win = [S.buf("win%d" % i) for i in range(NWI)]
        b_wup = [S.buf("wup%d" % i) for i in range(NWU)]
        b_wdn = [S.buf("wdn%d" % i) for i in range(NWD)]
        b_vecs, b_g3b, b_ident, b_identb = S.buf("vecs"), S.buf("g3b"), S.buf("ident"), S.buf("identb")
        b_Bst, b_Cst, b_Kblk, b_tab, b_rdec, b_sps = S.buf("Bst"), S.buf("Cst"), S.buf("Kblk"), S.buf("tab"), S.buf("rdec"), S.buf("sps")
        b_gc, b_hfin = S.buf("gc"), S.buf("hfin")
        b_stat = S.buf("stat")
        b_x = [S.buf("x%d" % t) for t in range(4)]
        b_x2 = [S.buf("x2_%d" % t) for t in range(4)]
        b_scr = [S.buf("scr%d" % i) for i in range(2)]
        b_h = [[S.buf("h%d_%d" % (k, t)) for t in range(4)] for k in range(8)]
        b_v = [S.buf("v%d" % g) for g in range(4)]
        b_Hb, b_Hb0 = S.buf("Hb"), S.buf("Hb0")
        b_hhalo = S.buf("hhalo")
        b_hfix, b_hfix2 = S.buf("hfix"), S.buf("hfix2")
        b_act = [S.buf("act%d" % f) for f in range(NF)]
        b_hup = [S.buf("hup%d" % s) for s in range(2)]
        b_cvt = S.buf("cvt")
        b_cvt3 = [[S.buf("cvt3_%d_%d" % (i, g)) for g in range(2)] for i in range(3)]
        b_tA, b_tB, b_tC, b_tD = S.buf("tA"), S.buf("tB"), S.buf("tC"), S.buf("tD")
        b_uT = [S.buf("uT%d" % k) for k in range(4)]
        b_yg = [S.buf("yg%d" % k) for k in range(4)]
        b_ptmp = [S.buf("ptmp%d" % i) for i in range(2)]
        b_pooled = S.buf("pooled")
        b_ytmp = [S.buf("ytmp%d" % i) for i in range(2)]
        b_prep = S.buf("prep")
        flat_ = lambda L_: [b for l in L_ for b in l]
        arena_bufs = b_act + b_hup + flat_(b_cvt3) + [b_cvt, b_tA, b_tB, b_tC, b_tD] + b_uT + b_yg + b_ptmp + [b_pooled] + b_ytmp + [b_prep]
        b_fence = S.buf("fence")

        def fence():
            S.c("dve", lambda e: e.memset(fence_scr[:, 0:1], 0.0), writes=arena_bufs + [b_fence])

        flat = lambda L_: [b for l in L_ for b in l]
        V = lambda c0, n: vecs[:, c0:c0 + n]

        S.dma("sp", lambda e: e.dma_start(out=vecs[:], in_=vecs_d), writes=[b_vecs])
        S.dma("sp", lambda e: e.dma_start(out=ident[:], in_=ident_d), writes=[b_ident])
        S.dma("sp", lambda e: e.dma_start(out=g3b[:], in_=g3b_d), writes=[b_g3b])
        b_s5p = S.buf("s5p")
        S.dma("sp", lambda e: e.dma_start(out=s5p, in_=s5p_d), writes=[b_prep, b_s5p])
        S.c("dve", lambda e: e.tensor_copy(out=identb[:], in_=ident[:]), reads=[b_ident], writes=[b_identb])
        S.dma("pool", lambda e: e.dma_start(out=w_glu_sb[:], in_=w_glu_d.rearrange("(k p) n -> p k n", p=128)), writes=[b_wglu])
        S.dma("pool", lambda e: e.dma_start(out=pool_w_sb[:], in_=pool_w_d.rearrange("(k p) n -> p k n", p=128)), writes=[b_poolw])

        b_winb = [S.buf("winb%d" % m) for m in range(8)]
        b_wupb = [S.buf("wupb%d" % f) for f in range(NF)]
        b_wdnb = [S.buf("wdnb%d" % m) for m in range(8)]
        for m in range(8):
            S.dma("pool", lambda e, m=m: e.dma_start(out=winb[m], in_=w_in_d[:, m * 128:(m + 1) * 128].rearrange("(k p) n -> p k n", p=128)),
                  writes=[b_winb[m]])
        for f in range(NF):
            S.dma("pool", lambda e, f=f: [e.dma_start(out=wupb[f][:, :, 0:128], in_=w_up_d[:, f * 128:(f + 1) * 128].rearrange("(k p) n -> p k n", p=128)),
                                           e.dma_start(out=wupb[f][:, :, 128:256], in_=w_up_d[:, DFF + f * 128:DFF + (f + 1) * 128].rearrange("(k p) n -> p k n", p=128))],
                  writes=[b_wupb[f]], ndma=2)
        for m in range(8):
            S.dma("pool", lambda e, m=m: e.dma_start(out=wdnb[m], in_=w_down_d[:, m * 128:(m + 1) * 128].rearrange("(k p) n -> p k n", p=128)),
                  writes=[b_wdnb[m]])
        spv = lambda i, n=1: sp_small[:, i:i + n, :]

        def sincos(ang_ap, out_s, out_c, shape, tmp1, tmp2, reads, writes):
            S.c("dve", lambda e: e.tensor_scalar(out=tmp1, in0=ang_ap, scalar1=1.0 / TWO_PI, scalar2=MAGIC, op0=ALU.mult, op1=ALU.add),
                reads=reads, writes=[b_prep])
            S.c("dve", lambda e: e.tensor_scalar(out=tmp1, in0=tmp1, scalar1=MAGIC, scalar2=-TWO_PI, op0=ALU.subtract, op1=ALU.mult),
                reads=[b_prep], writes=[b_prep])
            S.c("dve", lambda e: e.tensor_tensor(out=tmp1, in0=tmp1, in1=ang_ap, op=ALU.add), reads=[b_prep] + reads, writes=[b_prep])
            S.c("dve", lambda e: e.tensor_scalar(out=tmp1, in0=tmp1, scalar1=PI_LO, scalar2=-PI_LO, op0=ALU.min, op1=ALU.max), reads=[b_prep], writes=[b_prep])
            S.c("act", lambda e: e.activation(out=out_s, in_=tmp1, func=AF.Sin), reads=[b_prep], writes=writes)
            S.c("dve", lambda e: e.tensor_scalar(out=tmp2, in0=ang_ap, scalar1=1.0 / TWO_PI, scalar2=0.25, op0=ALU.mult, op1=ALU.add),
                reads=reads, writes=[b_prep])
            S.c("dve", lambda e: e.tensor_scalar(out=tmp2, in0=tmp2, scalar1=MAGIC, scalar2=None, op0=ALU.add), reads=[b_prep], writes=[b_prep])
            S.c("dve", lambda e: e.tensor_scalar(out=tmp2, in0=tmp2, scalar1=MAGIC, scalar2=-TWO_PI, op0=ALU.subtract, op1=ALU.mult),
                reads=[b_prep], writes=[b_prep])
            S.c("dve", lambda e: e.scalar_tensor_tensor(out=tmp2, in0=ang_ap, scalar=0.5 * math.pi, in1=tmp2, op0=ALU.add, op1=ALU.add),
                reads=[b_prep] + reads, writes=[b_prep])
            S.c("dve", lambda e: e.tensor_scalar(out=tmp2, in0=tmp2, scalar1=PI_LO, scalar2=-PI_LO, op0=ALU.min, op1=ALU.max), reads=[b_prep], writes=[b_prep])
            S.c("act", lambda e: e.activation(out=out_c, in_=tmp2, func=AF.Sin), reads=[b_prep], writes=writes)

        if do_s5:
            are = s5p[:, SP_ARE:SP_ARE + 16]
            aim = s5p[:, SP_AIM:SP_AIM + 16]
            ldt = s5p[:, SP_LDT:SP_LDT + 16]
            Bre = s5p[:, SP_BRE:SP_BRE + 512].rearrange("p (q h) -> p q h", q=NQ)
            Bim = s5p[:, SP_BIM:SP_BIM + 512].rearrange("p (q h) -> p q h", q=NQ)
            Cre = s5p[:, SP_CRE:SP_CRE + 512].rearrange("p (q h) -> p q h", q=NQ)
            Cim = s5p[:, SP_CIM:SP_CIM + 512].rearrange("p (q h) -> p q h", q=NQ)
            sp2 = lambda i: sp_small[:, i, :]
            S.c("act", lambda e: e.activation(out=sp2(TMPA), in_=ldt, func=AF.Exp), reads=[b_prep], writes=[b_sps])
            S.c("dve", lambda e: e.tensor_tensor(out=sp2(LR), in0=sp2(TMPA), in1=are, op=ALU.mult), reads=[b_prep, b_sps], writes=[b_sps])
            S.c("dve", lambda e: e.tensor_tensor(out=sp2(LI), in0=sp2(TMPA), in1=aim, op=ALU.mult), reads=[b_prep, b_sps], writes=[b_sps])
            for n in range(9):
                S.c("act", lambda e, n=n: e.activation(out=sp2(PR0 + n), in_=sp2(LR), func=AF.Exp, scale=float(n)), reads=[b_sps], writes=[b_sps])
                S.c("dve", lambda e, n=n: e.tensor_scalar(out=sp2(ANG0 + n), in0=sp2(LI), scalar1=float(n), scalar2=None, op0=ALU.mult),
                    reads=[b_sps], writes=[b_sps])
            S.c("dve", lambda e: e.tensor_scalar(out=sp2(ANG0 + 9), in0=sp2(LI), scalar1=float(BT), scalar2=None, op0=ALU.mult),
                reads=[b_sps], writes=[b_sps])
            S.c("dve", lambda e: e.memset(rdec[:], 0.0), writes=[b_rdec])
            S.c("dve", lambda e: e.tensor_copy(out=rdec[:, :, 1:CH + 1], in_=sp_small[:, PR0 + 8, :].unsqueeze(2).broadcast_to([128, NQ, CH])),
                reads=[b_sps], writes=[b_rdec])
            angv = sp_small[:, ANG0:ANG0 + 10, :]
            sin10 = pang[:, 0:10, 0:16]
            cos10 = pang[:, 0:10, 16:32]
            sincos(angv, sin10, cos10, None, pang2[:, 0:10, 0:16], pang2[:, 0:10, 16:32], [b_sps], [b_prep])
            S.c("dve", lambda e: e.tensor_tensor(out=sp_small[:, PI0:PI0 + 9, :], in0=sp_small[:, PR0:PR0 + 9, :], in1=sin10[:, 0:9, :], op=ALU.mult),
                reads=[b_prep, b_sps], writes=[b_sps])
            S.c("dve", lambda e: e.tensor_tensor(out=sp_small[:, PR0:PR0 + 9, :], in0=sp_small[:, PR0:PR0 + 9, :], in1=cos10[:, 0:9, :], op=ALU.mult),
                reads=[b_prep, b_sps], writes=[b_sps])
            S.c("dve", lambda e: e.tensor_copy(out=sp2(SW), in_=sin10[:, 9, :]), reads=[b_prep], writes=[b_sps])
            S.c("dve", lambda e: e.tensor_copy(out=sp2(CW), in_=cos10[:, 9, :]), reads=[b_prep], writes=[b_sps])
            S.c("dve", lambda e: e.tensor_scalar(out=sp2(TMPB), in0=sp2(LI), scalar1=float(T8), scalar2=None, op0=ALU.mult), reads=[b_sps], writes=[b_sps])
            S.c("dve", lambda e: e.tensor_tensor(out=pang, in0=sp_small[:, TMPB, :].unsqueeze(2).broadcast_to([128, NQ, CH]),
                                                  in1=V(V_IO, 64).unsqueeze(1).broadcast_to([128, NQ, CH]), op=ALU.mult),
                reads=[b_sps, b_vecs, b_prep], writes=[b_prep])
            tmp_a = av(2096, 1024).rearrange("p (q c) -> p q c", q=NQ)
            tmp_b = av(3120, 1024).rearrange("p (q c) -> p q c", q=NQ)
            sincos(pang, tabs[:], tabc[:], None, tmp_a, tmp_b, [b_prep], [b_tab])
            S.c("dve", lambda e: e.tensor_scalar(out=sp2(TMPA), in0=sp2(PR0 + 1), scalar1=-1.0, scalar2=None, op0=ALU.add), reads=[b_sps], writes=[b_sps])
            S.c("dve", lambda e: e.tensor_tensor(out=sp2(TMPB), in0=are, in1=are, op=ALU.mult), reads=[b_prep], writes=[b_sps])
            S.c("dve", lambda e: e.tensor_tensor(out=sp2(TMPC), in0=aim, in1=aim, op=ALU.mult), reads=[b_prep], writes=[b_sps])
            S.c("dve", lambda e: e.tensor_tensor(out=sp2(TMPB), in0=sp2(TMPB), in1=sp2(TMPC), op=ALU.add), reads=[b_sps], writes=[b_sps])
            S.c("dve", lambda e: e.reciprocal(out=sp2(TMPB), in_=sp2(TMPB)), reads=[b_sps], writes=[b_sps])
            S.c("dve", lambda e: e.tensor_tensor(out=sp2(TMPC), in0=sp2(TMPA), in1=are, op=ALU.mult), reads=[b_sps, b_prep], writes=[b_sps])
            S.c("dve", lambda e: e.tensor_tensor(out=sp2(TMPD), in0=sp2(PI0 + 1), in1=aim, op=ALU.mult), reads=[b_sps, b_prep], writes=[b_sps])
            S.c("dve", lambda e: e.tensor_tensor(out=sp2(TMPC), in0=sp2(TMPC), in1=sp2(TMPD), op=ALU.add), reads=[b_sps], writes=[b_sps])
            S.c("dve", lambda e: e.tensor_tensor(out=sp2(TMPC), in0=sp2(TMPC), in1=sp2(TMPB), op=ALU.mult), reads=[b_sps], writes=[b_sps])
            S.c("dve", lambda e: e.tensor_tensor(out=sp2(TMPD), in0=sp2(PI0 + 1), in1=are, op=ALU.mult), reads=[b_sps, b_prep], writes=[b_sps])
            S.c("dve", lambda e: e.tensor_tensor(out=sp2(TMPA), in0=sp2(TMPA), in1=aim, op=ALU.mult), reads=[b_sps, b_prep], writes=[b_sps])
            S.c("dve", lambda e: e.tensor_tensor(out=sp2(TMPD), in0=sp2(TMPD), in1=sp2(TMPA), op=ALU.subtract), reads=[b_sps], writes=[b_sps])
            S.c("dve", lambda e: e.tensor_tensor(out=sp2(TMPD), in0=sp2(TMPD), in1=sp2(TMPB), op=ALU.mult), reads=[b_sps], writes=[b_sps])
            bq = lambda i: sp_small[:, i, :].unsqueeze(2).broadcast_to([128, NQ, 32])

            pt3 = av(6192, 512).rearrange("p (q h) -> p q h", q=NQ)
            pt4 = av(6704, 512).rearrange("p (q h) -> p q h", q=NQ)
            Gnb2 = av(7216, 512, BF16).rearrange("p (r q h) -> p r q h", r=2, q=NQ)
            pts = [pt1, pt2, pt3, pt4]
            b_pt = [S.buf("pt%d" % i) for i in range(4)]
            b_G0, b_Ccb, b_Gn2 = S.buf("G0"), S.buf("Ccb"), [S.buf("Gnb0"), S.buf("Gnb1")]
            arena_bufs.extend(b_pt + [b_G0, b_Ccb] + b_Gn2)
            S.c("dve", lambda e: e.memset(fence_scr[:, 1:2], 0.0), reads=[b_sps], writes=[b_prep] + b_pt + [b_G0, b_Ccb] + b_Gn2)

            def cmul(out_r, out_i, ar, ai, br_, bi_, in_bufs, out_bufs, neg_i=False):
                t1, t2, t3, t4 = pts
                S.c("dve", lambda e: e.tensor_tensor(out=t1, in0=ar, in1=br_, op=ALU.mult), reads=in_bufs, writes=[b_pt[0]])
                S.c("dve", lambda e: e.tensor_tensor(out=t2, in0=ai, in1=bi_, op=ALU.mult), reads=in_bufs, writes=[b_pt[1]])
                S.c("dve", lambda e: e.tensor_tensor(out=t3, in0=ar, in1=bi_, op=ALU.mult), reads=in_bufs, writes=[b_pt[2]])
                S.c("dve", lambda e: e.tensor_tensor(out=t4, in0=ai, in1=br_, op=ALU.mult), reads=in_bufs, writes=[b_pt[3]])
                S.c("dve", lambda e: e.tensor_tensor(out=out_r, in0=t1, in1=t2, op=ALU.subtract), reads=[b_pt[0], b_pt[1]], writes=out_bufs)
                if neg_i:
                    S.c("dve", lambda e: e.scalar_tensor_tensor(out=out_i, in0=t3, scalar=-1.0, in1=t4, op0=ALU.mult, op1=ALU.subtract),
                        reads=[b_pt[2], b_pt[3]], writes=out_bufs)
                else:
                    S.c("dve", lambda e: e.tensor_tensor(out=out_i, in0=t3, in1=t4, op=ALU.add), reads=[b_pt[2], b_pt[3]], writes=out_bufs)

            cmul(G0r, G0i, bq(TMPC), bq(TMPD), Bre, Bim, [b_sps, b_s5p], [b_G0])
            S.c("dve", lambda e: e.tensor_copy(out=Ccb[:, 0], in_=Cre), reads=[b_s5p], writes=[b_Ccb])
            S.c("dve", lambda e: e.tensor_scalar(out=Ccb[:, 1], in0=Cim, scalar1=-1.0, scalar2=None, op0=ALU.mult), reads=[b_s5p], writes=[b_Ccb])
            S.c("dve", lambda e: e.memset(Kblk[:], 0.0), writes=[b_Kblk])
            for n in range(T8):
                gb, bg = (Gnb, b_Gn2[0]) if n % 2 == 0 else (Gnb2, b_Gn2[1])
                cmul(gb[:, 0], gb[:, 1], bq(PR0 + n), bq(PI0 + n), G0r, G0i, [b_sps, b_G0], [bg])
                def tr_fn(e, gb=gb):
                    last = None
                    for ri in range(2):
                        for q in range(NQ):
                            q4, kc = q // 4, q % 4
                            col = (kc * 2 + ri) * 128
                            last = e.matmul(PSn[0][32 * q4:32 * q4 + 32, col:col + 128], lhsT=gb[:, ri, q, :], rhs=identb[:],
                                            start=True, stop=True, tile_position=(0, 32 * q4))
                    return last
                S.c("pe", tr_fn, reads=[bg, b_identb], writes=[bank[0][0], bank[0][1]])
                S.c("act", lambda e, n=n: e.activation(out=Bst[:, :, T8 - 1 - n, :, :],
                                                        in_=PSn[0][:, :].rearrange("p (k r m) -> p k r m", k=4, r=2), func=AF.Copy),
                    reads=[bank[0][0], bank[0][1]], writes=[b_Bst])
                def kb_fn(e, gb=gb):
                    last = None
                    for q in range(NQ):
                        q4, kc = q // 4, q % 4
                        o = PSn[1][32 * q4:32 * q4 + 32, kc * 128 + 32 * q4:kc * 128 + 32 * q4 + 32]
                        e.matmul(o, lhsT=gb[:, 0, q, :], rhs=Ccb[:, 0, q, :], start=True, stop=False, tile_position=(0, 32 * q4))
                        last = e.matmul(o, lhsT=gb[:, 1, q, :], rhs=Ccb[:, 1, q, :], start=False, stop=True, tile_position=(0, 32 * q4))
                    return last
                S.c("pe", kb_fn, reads=[bg, b_Ccb], writes=[bank[1][0]])
                for q4 in range(4):
                    S.c("act", lambda e, n=n, q4=q4: e.activation(
                        out=Kblk[32 * q4:32 * q4 + 32, :, n, 32 * q4:32 * q4 + 32],
                        in_=PSn[1][32 * q4:32 * q4 + 32, 0:512].rearrange("p (k m) -> p k m", k=4)[:, :, 32 * q4:32 * q4 + 32], func=AF.Copy),
                        reads=[bank[1][0]], writes=[b_Kblk])
            for n in range(T8 + 1):
                cmul(Cst[:, :, n, 0, :], Cst[:, :, n, 1, :], Cre, Cim, bq(PR0 + n), bq(PI0 + n), [b_sps, b_s5p], [b_Cst], neg_i=True)
            S.c("dve", lambda e: e.memset(sp_small[:, GCR:GCR + 2, :], 0.0), writes=[b_gc])
            S.c("dve", lambda e: e.memset(Hb[:], 0.0), writes=[b_Hb, b_Hb0])
        S.c("dve", lambda e: e.memset(hhalo[:], 0.0), writes=[b_hhalo])
        S.c("dve", lambda e: e.memset(v_sb[:], 0.0), writes=b_v)
        fence()
        b_woutb = S.buf("woutb")
        S.dma("pool", lambda e: e.dma_start(out=woutb, in_=w_out_d.rearrange("(k p) n -> p k n", p=128)), writes=[b_woutb])
        arena_bufs.append(b_wout)

        def load_wout():
            S.dma("sp", lambda e: e.dma_start(out=w_out_sb, in_=woutb), reads=[b_woutb], writes=[b_tA, b_tB, b_tC, b_tD, b_wout])
        cnt = {"win": 0, "wup": 0, "wdn": 0, "bk": 0}

        def rms_stats(nt, xt_aps, xbufs):
            S.c("dve", lambda e: e.memset(stat[:, 0:8], 0.0), writes=[b_stat])
            for i, xa in enumerate(xt_aps):
                np_ = xa.shape[0]
                S.c("act", lambda e, i=i, xa=xa, np_=np_: e.activation(out=scr[0:np_, 1, :], in_=xa, func=AF.Square, accum_out=stat[0:np_, i:i + 1]),
                    reads=[xbufs[i], b_stat], writes=[b_scr[1], b_stat])
            S.c("dve", lambda e: e.tensor_scalar(out=stat[:, 0:nt], in0=stat[:, 0:nt], scalar1=1.0 / D, scalar2=EPS, op0=ALU.mult, op1=ALU.add),
                reads=[b_stat], writes=[b_stat])
            S.c("act", lambda e: e.activation(out=stat[:, 0:nt], in_=stat[:, 0:nt], func=AF.Sqrt), reads=[b_stat], writes=[b_stat])
            S.c("dve", lambda e: e.reciprocal(out=stat[:, 8:8 + nt], in_=stat[:, 0:nt]), reads=[b_stat], writes=[b_stat])

        def norm_to_hT(tiles, gcol, dst=None, dbufs=None):
            rms_stats(len(tiles), [t_[0] for t_ in tiles], [t_[1] for t_ in tiles])
            for i, (xa, xb, c0, np_) in enumerate(tiles):
                s = i % 2
                S.c("dve", lambda e, xa=xa, i=i, np_=np_, s=s: e.tensor_scalar(out=scr[0:np_, s, :], in0=xa, scalar1=stat[0:np_, 8 + i:9 + i],
                                                                         scalar2=None, op0=ALU.mult),
                    reads=[xb, b_stat], writes=[b_scr[s]])

                def tr(e, np_=np_, s=s):
                    last = None
                    for k in range(8):
                        last = e.transpose(out=PA[:, k * 128:k * 128 + np_], in_=scr[0:np_, s, k * 128:(k + 1) * 128], identity=ident[0:np_, 0:np_])
                    return last
                S.c("pe", tr, reads=[b_scr[s], b_ident], writes=[bank[0][0], bank[0][1]])
                t_idx = c0 // 128
                dst_ = hT if dst is None else dst
                S.c("dve", lambda e, np_=np_, c0=c0, dst_=dst_: e.tensor_tensor(out=dst_[:, :, c0:c0 + np_],
                                                                 in0=PA[:, :].rearrange("p (k n) -> p k n", k=8)[:, :, 0:np_],
                                                                 in1=V(gcol, 8).unsqueeze(2).broadcast_to([128, 8, np_]), op=ALU.mult),
                    reads=[bank[0][0], bank[0][1], b_vecs], writes=([b_h[k][t_idx] for k in range(8)] if dbufs is None else dbufs))

        def load_win(m):
            slot = cnt["win"] % NWI
            cnt["win"] += 1
            S.dma("sp", lambda e: e.dma_start(out=w_in_ring[slot][:], in_=winb[m]), reads=[b_winb[m]], writes=[b_win[slot]])
            return slot

        def load_wup(f):
            slot = cnt["wup"] % NWU
            cnt["wup"] += 1
            S.dma("sp", lambda e: e.dma_start(out=w_up_ring[slot][:], in_=wupb[f]), reads=[b_wupb[f]], writes=[b_wup[slot]])
            return slot

        def load_wdn(m):
            slot = cnt["wdn"] % NWD
            cnt["wdn"] += 1
            S.dma("sp", lambda e: e.dma_start(out=w_dn_ring[slot][:], in_=wdnb[m]), reads=[b_wdnb[m]], writes=[b_wdn[slot]])
            return slot

        def next_bank():
            i = cnt["bk"] % 2
            cnt["bk"] += 1
            return pbank(1, i), bank[1][i]

        sel_d = din("sel", [128, 32])
        selt = sbt("selt", [128, 32], F32)
        xst = sbt("xst", [128, 2, 128], F32)
        xs_sb = sbt("xs_sb", [128, D], F32)
        hTs = sbt("hTs", [128, 8, NS], BF16)
        HbS = sbt("HbS", [128, NQ, 2, NS], BF16)
        cbufT = sbt("cbufT", [128, 2 * NF, 2 * NS], F32)
        hupS = sbt("hupS", [128, 2 * NF, NS], F32)
        actS = sbt("actS", [128, NF, NS], BF16)
        cvS = sbt("cvS", [128, 3, 2, NS], F32)
        b_sel, b_xst = S.buf("sel"), [S.buf("xst0"), S.buf("xst1")]
        b_xs, b_hs, b_HbS = S.buf("xs"), [S.buf("hs%d" % k) for k in range(8)], S.buf("HbS")
        b_cst, b_cbufT, b_hupS, b_actS = S.buf("cst"), S.buf("cbufT"), S.buf("hupS"), S.buf("actS")
        b_cvS = [[S.buf("cvS%d_%d" % (i, g)) for g in range(2)] for i in range(3)]
        b_pss = [S.buf("pss%d" % i) for i in range(4)]
        b_psd = [S.buf("psd%d" % i) for i in range(2)]
        b_pst = [S.buf("pst%d" % i) for i in range(2)]
        arena_bufs.append(b_cst)
        N = NS
        def sample_phase_A():
            S.dma("sp", lambda e: e.dma_start(out=selt[:], in_=sel_d), writes=[b_sel])
            S.dma("sp", lambda e: e.dma_start(out=xs_sb[0:N, :], in_=xs_d), writes=[b_xs])
            S.dma("sp", lambda e: e.dma_start(out=spool_d[:, 0:14, :], in_=pst_d[:, 1:15, :]), out=True)
            S.dma("sp", lambda e: e.dma_start(out=sconv_d[:, 0, :], in_=cst_d[:, 1, :]), out=True)
            norm_to_hT([(xs_sb[0:N, :], b_xs, 0, N)], V_G1, dst=hTs, dbufs=b_hs)
            hcol = b_hs
            for m in range(8):
                slot = load_win(m)
                po, pbuf = next_bank()

                def proj(e, slot=slot, po=po):
                    last = None
                    for k in range(8):
                        last = e.matmul(po[:, 0:N], lhsT=w_in_ring[slot][:, k, :], rhs=hTs[:, k, :], start=(k == 0), stop=(k == 7))
                    return last
                S.c("pe", proj, reads=[b_win[slot]] + hcol, writes=[pbuf])
                if m < 4:
                    S.c("act", lambda e, m=m, po=po: e.activation(out=uT[:, m, 0:N], in_=po[:, 0:N], func=AF.Copy), reads=[pbuf], writes=[b_uT[m]])
                else:
                    S.c("act", lambda e, m=m, po=po: e.activation(out=v_sb[:, m - 4, 16:16 + N], in_=po[:, 0:N], func=AF.Copy), reads=[pbuf], writes=[b_v[m - 4]])
            if do_s5:
                h0r_tm = scr[0:N, :, :].rearrange("p s n -> p (s n)")
                h0i_tm = x_sb[0:N, 2:4, :].rearrange("p s n -> p (s n)")
                S.dma("sp", lambda e: e.dma_start(out=h0r_tm, in_=s5re_d), writes=b_scr)
                S.dma("sp", lambda e: e.dma_start(out=h0i_tm, in_=s5im_d), writes=[b_x[2], b_x[3]])

                def trh(e):
                    last = None
                    for ri, src in ((0, h0r_tm), (1, h0i_tm)):
                        for q in range(NQ):
                            qo = (q % 4) * 4 + q // 4
                            last = e.transpose(out=PB[:, ri * 256 + q * N:ri * 256 + (q + 1) * N], in_=src[:, qo * 128:(qo + 1) * 128], identity=ident[0:N, 0:N])
                    return last
                S.c("pe", trh, reads=b_scr + [b_x[2], b_x[3], b_ident], writes=[bank[1][0]])
                h0r, h0i, hnr, hni = tA[:, :, 0:N], tB[:, :, 0:N], tC[:, :, 0:N], tD[:, :, 0:N]
                S.c("act", lambda e: e.activation(out=h0r, in_=PB[:, 0:256].rearrange("p (q n) -> p q n", q=NQ), func=AF.Copy), reads=[bank[1][0]], writes=[b_tA])
                S.c("act", lambda e: e.activation(out=h0i, in_=PB[:, 256:512].rearrange("p (q n) -> p q n", q=NQ), func=AF.Copy), reads=[bank[1][0]], writes=[b_tB])

                def bu(e):
                    last = None
                    for ri in range(2):
                        for q in range(NQ):
                            q4, kc = q // 4, q % 4
                            c0_ = q4 * 512 + (kc * 2 + ri) * N
                            last = e.matmul(PCD[:, c0_:c0_ + N], lhsT=Bst[32 * q4:32 * q4 + 32, kc, T8 - 1, ri, :],
                                            rhs=uT[32 * q4:32 * q4 + 32, kc, 0:N], start=True, stop=True, tile_position=(32 * q4, 0))
                    return last
                S.c("pe", bu, reads=[b_Bst] + b_uT, writes=[bank[2][0], bank[2][1], bank[3][0], bank[3][1]])
                al = sp_small[:, PR0 + 1, :].unsqueeze(2).broadcast_to([128, NQ, N])
                be = sp_small[:, PI0 + 1, :].unsqueeze(2).broadcast_to([128, NQ, N])
                t1 = ytmp[:, 0, 0:256].rearrange("p (q n) -> p q n", q=NQ)
                t2 = ytmp[:, 1, 0:256].rearrange("p (q n) -> p q n", q=NQ)
                TT = lambda eng, o, a, b_, op, rd, wr: S.c(eng, lambda e: e.tensor_tensor(out=o, in0=a, in1=b_, op=op), reads=rd, writes=wr)
                PSv = PCD[:, :].rearrange("p (a x) -> p a x", a=4)[:, :, 0:8 * N].rearrange("p a (k r n) -> p a k r n", k=4, r=2)
                Sre, Sim = PSv[:, :, :, 0, :], PSv[:, :, :, 1, :]
                v4 = lambda a: a.rearrange("p (a k) c -> p a k c", a=4)
                TT("dve", t1, h0r, al, ALU.mult, [b_tA, b_sps], [b_ytmp[0]])
                TT("dve", t2, h0i, be, ALU.mult, [b_tB, b_sps], [b_ytmp[1]])
                TT("dve", hnr, t1, t2, ALU.subtract, b_ytmp, [b_tC])
                TT("dve", v4(hnr), v4(hnr), Sre, ALU.add, [b_tC, bank[2][0], bank[2][1], bank[3][0], bank[3][1]], [b_tC])
                TT("dve", t1, h0i, al, ALU.mult, [b_tB, b_sps], [b_ytmp[0]])
                TT("dve", t2, h0r, be, ALU.mult, [b_tA, b_sps], [b_ytmp[1]])
                TT("dve", hni, t1, t2, ALU.add, b_ytmp, [b_tD])
                TT("dve", v4(hni), v4(hni), Sim, ALU.add, [b_tD, bank[2][0], bank[2][1], bank[3][0], bank[3][1]], [b_tD])
                S.c("dve", lambda e: e.tensor_copy(out=HbS[:, :, 0, :], in_=hnr), reads=[b_tC], writes=[b_HbS])
                S.c("dve", lambda e: e.tensor_copy(out=HbS[:, :, 1, :], in_=hni), reads=[b_tD], writes=[b_HbS])
                for ri, (src, dst_tm, dbufs, dd) in enumerate(((hnr, h0r_tm, b_scr, sre_d), (hni, h0i_tm, [b_x[2], b_x[3]], sim_d))):
                    for half in range(2):
                        def trb(e, src=src, half=half):
                            last = None
                            for qq in range(8):
                                qo = half * 8 + qq
                                qp = (qo % 4) * 4 + qo // 4
                                last = e.transpose(out=PA[0:N, qq * 128:(qq + 1) * 128], in_=src[:, qp, :], identity=ident[:])
                            return last
                        S.c("pe", trb, reads=[b_tC, b_tD, b_ident], writes=[bank[0][0], bank[0][1]])
                        S.c("act", lambda e, dst_tm=dst_tm, half=half: e.activation(out=dst_tm[:, half * 1024:(half + 1) * 1024], in_=PA[0:N, :], func=AF.Copy),
                            reads=[bank[0][0], bank[0][1]], writes=dbufs)
                    S.dma("sp", lambda e, dd=dd, dst_tm=dst_tm: e.dma_start(out=dd, in_=dst_tm), reads=dbufs, out=True)
                for kc in range(4):
                    Y2 = pbank(3, 1)

                    def ys(e, kc=kc, Y2=Y2):
                        last = None
                        for q4 in range(4):
                            q = q4 * 4 + kc
                            o = Y2[32 * q4:32 * q4 + 32, 0:N]
                            e.matmul(o, lhsT=Cst[:, q, 0, 0, :], rhs=HbS[:, q, 0, :], start=True, stop=False, tile_position=(0, 32 * q4))
                            last = e.matmul(o, lhsT=Cst[:, q, 0, 1, :], rhs=HbS[:, q, 1, :], start=False, stop=True, tile_position=(0, 32 * q4))
                        return last
                    S.c("pe", ys, reads=[b_Cst, b_HbS], writes=[bank[3][1]])
                    S.c("dve", lambda e, kc=kc, Y2=Y2: e.scalar_tensor_tensor(out=ytmp[:, 0, 0:N], in0=uT[:, kc, 0:N], scalar=vecs[:, V_DS + kc:V_DS + kc + 1],
                                                                          in1=Y2[:, 0:N], op0=ALU.mult, op1=ALU.add),
                        reads=[bank[3][1], b_uT[kc], b_vecs], writes=[b_ytmp[0]])
                    S.c("act", lambda e, kc=kc: e.activation(out=yg[:, kc, 0:N], in_=ytmp[:, 0, 0:N], func=AF.Gelu), reads=[b_ytmp[0]], writes=[b_yg[kc]])
                for m in range(4):
                    po, pbuf = next_bank()

                    def glu(e, m=m, po=po):
                        last = None
                        for k in range(4):
                            last = e.matmul(po[:, 0:N], lhsT=w_glu_sb[:, k, m * 128:(m + 1) * 128], rhs=yg[:, k, 0:N], start=(k == 0), stop=(k == 3))
                        return last
                    S.c("pe", glu, reads=[b_wglu] + b_yg, writes=[pbuf])
                    S.c("act", lambda e, po=po: e.activation(out=ytmp[:, 1, 0:N], in_=po[:, 0:N], func=AF.Sigmoid), reads=[pbuf], writes=[b_ytmp[1]])
                    S.c("dve", lambda e, m=m: e.tensor_tensor(out=hTs[:, m, :], in0=yg[:, m, 0:N], in1=ytmp[:, 1, 0:N], op=ALU.mult),
                        reads=[b_ytmp[1], b_yg[m]], writes=[b_hs[m]])
            else:
                for m in range(4):
                    S.c("dve", lambda e, m=m: e.memset(hTs[:, m, :], 0.0), writes=[b_hs[m]])
            for g, w in enumerate((2, 4, 8, 16)):
                po, pbuf = next_bank()
                for half in range(2):
                    s = half
                    S.dma("sp", lambda e, g=g, half=half, s=s: e.dma_start(
                        out=xst[0:120, s, :], in_=pst_d[half * 8:(half + 1) * 8, :, g * 128:(g + 1) * 128].rearrange("t r c -> (t r) c")), writes=[b_xst[s]])
                    S.c("pe", lambda e, g=g, half=half, s=s, po=po: e.matmul(po[:, half * 8:(half + 1) * 8], lhsT=xst[0:120, s, :], rhs=selt[0:120, g * 8:(g + 1) * 8],
                                                                        start=True, stop=True), reads=[b_xst[s], b_sel], writes=[pbuf])
                S.c("dve", lambda e, g=g, w=w: e.tensor_scalar(out=ptmp[:, 0, 0:N], in0=v_sb[:, g, 16:16 + N], scalar1=1.0 / w - 1.0, scalar2=None, op0=ALU.mult),
                    reads=[b_v[g]], writes=[b_ptmp[0]])
                S.c("dve", lambda e, w=w, po=po: e.scalar_tensor_tensor(out=pooled[:, 0:N], in0=po[:, 0:N], scalar=1.0 / w, in1=ptmp[:, 0, 0:N], op0=ALU.mult, op1=ALU.add),
                    reads=[pbuf, b_ptmp[0]], writes=[b_pooled])
                po2, pbuf2 = next_bank()
                S.c("pe", lambda e, g=g, po2=po2: e.matmul(po2[:, 0:N], lhsT=pool_w_sb[:, g, :], rhs=pooled[:, 0:N], start=True, stop=True),
                    reads=[b_poolw, b_pooled], writes=[pbuf2])
                S.c("act", lambda e, g=g, po2=po2: e.activation(out=hTs[:, 4 + g, :], in_=po2[:, 0:N], func=AF.Copy, scale=vecs[:, V_PS + g:V_PS + g + 1]),
                    reads=[pbuf2, b_vecs], writes=[b_hs[4 + g]])

            def trv(e):
                last = None
                for g in range(4):
                    last = e.transpose(out=PA[0:N, g * 128:(g + 1) * 128], in_=v_sb[:, g, 16:16 + N], identity=ident[:])
                return last
            S.c("pe", trv, reads=b_v + [b_ident], writes=[bank[0][0]])
            S.c("act", lambda e: e.activation(out=x_sb[0:N, 1, 0:512], in_=PA[0:N, 0:512], func=AF.Copy), reads=[bank[0][0]], writes=[b_x[1]])
            S.dma("sp", lambda e: e.dma_start(out=spool_d[:, 14, :], in_=x_sb[0:N, 1, 0:512]), reads=[b_x[1]], out=True)

            load_wout()

            def oproj_s(e):
                last = None
                for h in range(2):
                    for k in range(8):
                        last = e.matmul(pbank(2, h)[0:N, :], lhsT=hTs[:, k, :], rhs=w_out_sb[:, k, h * 512:(h + 1) * 512], start=(k == 0), stop=(k == 7))
                return last
            S.c("pe", oproj_s, reads=[b_wout] + hcol, writes=[bank[2][0], bank[2][1]])
            S.c("dve", lambda e: e.tensor_tensor(out=xs_sb[0:N, :], in0=xs_sb[0:N, :], in1=PC[0:N, :], op=ALU.add),
                reads=[b_xs, bank[2][0], bank[2][1]], writes=[b_xs])
            norm_to_hT([(xs_sb[0:N, :], b_xs, 0, N)], V_G2, dst=hTs, dbufs=b_hs)
            fence()
            cst_tm = av(0, 5632)
            S.dma("sp", lambda e: e.dma_start(out=cst_tm[0:32, :], in_=cst_d.rearrange("t r f -> (t r) f")), reads=[b_fence], writes=[b_cst])
            for rnd in range(3):
                c0 = rnd * 16
                ncx = min(16, 2 * NF - c0)

                def trc(e, c0=c0, ncx=ncx):
                    last = None
                    for ci in range(ncx):
                        last = e.transpose(out=PA[:, ci * 32:(ci + 1) * 32], in_=cst_tm[0:32, (c0 + ci) * 128:(c0 + ci + 1) * 128], identity=ident[0:32, 0:32])
                    return last
                S.c("pe", trc, reads=[b_cst, b_ident], writes=[bank[0][0]])
                S.c("act", lambda e, c0=c0, ncx=ncx: e.activation(out=cbufT[:, c0:c0 + ncx, :], in_=PA[:, 0:ncx * 32].rearrange("p (c n) -> p c n", c=ncx), func=AF.Copy),
                    reads=[bank[0][0]], writes=[b_cbufT])
            fence()

        def sample_up(f, slot):
            pr = f % 2
            cs = f % 3
            psu = [PA[:, pr * 512 + gv * NS:pr * 512 + (gv + 1) * NS] for gv in range(2)]

            def up(e):
                last = None
                for gv in range(2):
                    for k in range(8):
                        last = e.matmul(psu[gv], lhsT=w_up_ring[slot][:, k, gv * 128:(gv + 1) * 128], rhs=hTs[:, k, :], start=(k == 0), stop=(k == 7))
                return last
            S.c("pe", up, reads=[b_wup[slot]] + b_hs, writes=[bank[0][pr]])
            for gv in range(2):
                ch = f + gv * NF
                cw = lambda k, ch=ch: vecs[:, V_CW + ch * 3 + k:V_CW + ch * 3 + k + 1]
                cb2 = cbufT[:, ch, :].rearrange("p (t r) -> p t r", r=2)
                cc = cvS[:, cs, gv, :]
                bc = b_cvS[cs][gv]
                S.c("act", lambda e, gv=gv, ch=ch: e.activation(out=hupS[:, ch, :], in_=psu[gv], func=AF.Copy), reads=[bank[0][pr]], writes=[b_hupS])
                S.c("act", lambda e, gv=gv, ch=ch, cw=cw, cc=cc: e.activation(out=cc, in_=psu[gv], func=AF.Identity, scale=cw(2),
                                                                       bias=vecs[:, V_CB + ch:V_CB + ch + 1]), reads=[bank[0][pr], b_vecs], writes=[bc])
                S.c("dve", lambda e, cw=cw, cb2=cb2, cc=cc: e.scalar_tensor_tensor(out=cc, in0=cb2[:, :, 1], scalar=cw(1), in1=cc, op0=ALU.mult, op1=ALU.add),
                    reads=[b_cbufT, b_vecs, bc], writes=[bc])
                S.c("dve", lambda e, cw=cw, cb2=cb2, cc=cc: e.scalar_tensor_tensor(out=cc, in0=cb2[:, :, 0], scalar=cw(0), in1=cc, op0=ALU.mult, op1=ALU.add),
                    reads=[b_cbufT, b_vecs, bc], writes=[bc])
            S.c("act", lambda e: e.activation(out=cvS[:, cs, 0, :], in_=cvS[:, cs, 0, :], func=AF.Gelu), reads=[b_cvS[cs][0]], writes=[b_cvS[cs][0]])
            S.c("dve", lambda e: e.tensor_tensor(out=actS[:, f, :], in0=cvS[:, cs, 0, :], in1=cvS[:, cs, 1, :], op=ALU.mult), reads=b_cvS[cs], writes=[b_actS])

        def sample_down(m, slot):
            s_ = m % 2
            psd = pbank(2 + s_, 0)[:, 0:NS]

            def down(e):
                last = None
                for k in range(NF):
                    last = e.matmul(psd, lhsT=w_dn_ring[slot][:, k, :], rhs=actS[:, k, :], start=(k == 0), stop=(k == NF - 1))
                return last
            S.c("pe", down, reads=[b_wdn[slot], b_actS], writes=[bank[2 + s_][0]])
            S.c("act", lambda e: e.activation(out=xst[:, s_, 0:N], in_=psd, func=AF.Copy), reads=[bank[2 + s_][0]], writes=[b_xst[s_]])

        def sample_btr(m):
            s_ = m % 2
            pst_ = pbank(2 + s_, 1)[0:N, 0:128]
            S.c("pe", lambda e: e.transpose(out=pst_, in_=xst[:, s_, 0:N], identity=ident[:]),
                reads=[b_xst[s_], b_ident], writes=[bank[2 + s_][1]])
            S.c("dve", lambda e: e.tensor_tensor(out=xs_sb[0:N, m * 128:(m + 1) * 128], in0=xs_sb[0:N, m * 128:(m + 1) * 128], in1=pst_, op=ALU.add),
                reads=[b_xs, bank[2 + s_][1]], writes=[b_xs])

        def sample_phase_C():
            for rnd in range(6):
                c0 = rnd * 8
                ncx = min(8, 2 * NF - c0)
                s = rnd % 2

                def trhup(e, c0=c0, ncx=ncx):
                    last = None
                    for ci in range(ncx):
                        last = e.transpose(out=PA[0:N, ci * 128:(ci + 1) * 128], in_=hupS[:, c0 + ci, :], identity=ident[:])
                    return last
                S.c("pe", trhup, reads=[b_hupS, b_ident], writes=[bank[0][0], bank[0][1]])
                S.c("act", lambda e, s=s, ncx=ncx: e.activation(out=scr[0:N, s, 0:ncx * 128], in_=PA[0:N, 0:ncx * 128], func=AF.Copy),
                    reads=[bank[0][0], bank[0][1]], writes=[b_scr[s]])
                S.dma("sp", lambda e, s=s, c0=c0, ncx=ncx: e.dma_start(out=sconv_d[:, 1, c0 * 128:(c0 + ncx) * 128], in_=scr[0:N, s, 0:ncx * 128]),
                      reads=[b_scr[s]], out=True)
            rms_stats(1, [xs_sb[0:N, :]], [b_xs])
            S.c("dve", lambda e: e.scalar_tensor_tensor(out=scr[0:N, 0, :], in0=xs_sb[0:N, :], scalar=stat[0:N, 8:9], in1=g3b[0:N, :],
                                                         op0=ALU.mult, op1=ALU.mult), reads=[b_xs, b_stat, b_g3b], writes=[b_scr[0]])
            S.dma("sp", lambda e: e.dma_start(out=ys_d, in_=scr[0:N, 0, :]), reads=[b_scr[0]], out=True)

        def block_early(blk):
            last_blk = (blk == NBLK - 1)
            if blk > 0:
                S.c("pool", lambda e: e.tensor_copy(out=v_sb[:, :, 1:16], in_=v_sb[:, :, BT + 1:BT + 16]), reads=b_v, writes=b_v)
            all_h = flat(b_h)
            for m in range(8):
                slot = load_win(m)
                po, pbuf = next_bank()

                def proj(e, slot=slot, po=po):
                    last = None
                    for k in range(8):
                        last = e.matmul(po, lhsT=w_in_ring[slot][:, k, :], rhs=hT[:, k, :], start=(k == 0), stop=(k == 7))
                    return last
                S.c("pe", proj, reads=[b_win[slot]] + all_h, writes=[pbuf])
                if m < 4:
                    S.c("act", lambda e, m=m, po=po: e.activation(out=uT[:, m, :], in_=po, func=AF.Copy), reads=[pbuf], writes=[b_uT[m]])
                else:
                    S.c("act", lambda e, m=m, po=po: e.activation(out=v_sb[:, m - 4, 16:16 + BT], in_=po, func=AF.Copy), reads=[pbuf], writes=[b_v[m - 4]])

            for g, w in enumerate((2, 4, 8, 16)):
                vg = v_sb[:, g, :]
                nsteps = g + 1
                cur = vg
                curb = [b_v[g]]
                sh = 1
                for si in range(nsteps):
                    dst = ptmp[:, si % 2, :]
                    lo = 2 * sh
                    S.c("dve", lambda e, dst=dst, cur=cur, sh=sh: e.tensor_tensor(out=dst[:, 2 * sh - 1:16 + BT], in0=cur[:, 2 * sh - 1:16 + BT],
                                                                          in1=cur[:, sh - 1:16 + BT - sh], op=ALU.add),
                        reads=curb, writes=[b_ptmp[si % 2]])
                    cur = dst
                    curb = [b_ptmp[si % 2]]
                    sh *= 2
                S.c("dve", lambda e, cur=cur, w=w, vg=vg: e.scalar_tensor_tensor(out=pooled, in0=cur[:, 16:16 + BT], scalar=1.0 / w, in1=vg[:, 16:16 + BT],
                                                                             op0=ALU.mult, op1=ALU.subtract),
                    reads=curb + [b_v[g]], writes=[b_pooled])
                if blk == 0:
                    o2 = ptmp[:, (nsteps) % 2, :]
                    S.c("dve", lambda e, cur=cur, w=w, o2=o2: e.tensor_tensor(out=o2[:, 0:w - 1], in0=cur[:, 16:16 + w - 1], in1=V(V_IC, w - 1), op=ALU.mult),
                        reads=curb + [b_vecs], writes=[b_ptmp[nsteps % 2]])
                    S.c("dve", lambda e, w=w, o2=o2, vg=vg: e.tensor_tensor(out=pooled[:, 0:w - 1], in0=o2[:, 0:w - 1], in1=vg[:, 16:16 + w - 1], op=ALU.subtract),
                        reads=[b_ptmp[nsteps % 2], b_v[g]], writes=[b_pooled])
                po, pbuf = next_bank()
                S.c("pe", lambda e, g=g, po=po: e.matmul(po, lhsT=pool_w_sb[:, g, :], rhs=pooled, start=True, stop=True),
                    reads=[b_poolw, b_pooled], writes=[pbuf])
                S.c("act", lambda e, g=g, po=po: e.activation(out=hT[:, 4 + g, :], in_=po, func=AF.Copy, scale=vecs[:, V_PS + g:V_PS + g + 1]),
                    reads=[pbuf, b_vecs], writes=b_h[4 + g])
            if last_blk:
                S.dma("sp", lambda e: [e.dma_start(out=ppool_d[:, g * 128:(g + 1) * 128].rearrange("t c -> c t"), in_=v_sb[:, g, BT + 1:BT + 16],
                                                    allow_slow_non_contiguous=True) for g in range(4)], reads=b_v, out=True, ndma=4)


        for blk in range(nblk):
            r0 = blk * BT
            last_blk = (blk == NBLK - 1)
            xb, bx = (x_sb, b_x) if blk % 2 == 0 else (x_sb2, b_x2)
            xn_, bxn = (x_sb2, b_x2) if blk % 2 == 0 else (x_sb, b_x)
            mrg = do_sample and blk == nblk - 1
            if mrg:
                sample_phase_A()
            if blk == 0:
                for t in range(4):
                    S.dma("sp", lambda e, t=t, r0=r0, xb=xb: e.dma_start(out=xb[:, t, :], in_=x_d[r0 + t * 128:r0 + (t + 1) * 128, :]), writes=[bx[t]])
            if blk == 0:
                norm_to_hT([(xb[:, t, :], bx[t], t * 128, 128) for t in range(4)], V_G1)
            if blk == 0 or mrg:
                block_early(blk)
            if do_s5 and stage >= 1:
                PCDv = PCD[:, :].rearrange("p (a k r c) -> p a k r c", a=4, k=4, r=2)
                PCv, PDv = PCDv[:, :, :, 0, :], PCDv[:, :, :, 1, :]
                v4 = lambda a: a.rearrange("p (a k) c -> p a k c", a=4)

                def lvl0(e):
                    last = None
                    for ri in range(2):
                        for kc in range(4):
                            for j in range(T8):
                                for q4 in range(4):
                                    c0_ = q4 * 512 + (kc * 2 + ri) * CH
                                    last = e.matmul(PCD[:, c0_:c0_ + CH], lhsT=Bst[32 * q4:32 * q4 + 32, kc, j, ri, :],
                                                    rhs=uT[32 * q4:32 * q4 + 32, kc, j:BT:T8], start=(j == 0), stop=(j == T8 - 1),
                                                    tile_position=(32 * q4, 0))
                    return last
                S.c("pe", lvl0, reads=[b_Bst] + b_uT, writes=[bank[2][0], bank[2][1], bank[3][0], bank[3][1]])
                if stage >= 2:
                    bC, bD = [bank[2][0], bank[2][1]], [bank[3][0], bank[3][1]]
                    A1, B1, C1, D1 = tA[:, :, 1:CH + 1], tB[:, :, 1:CH + 1], tC[:, :, 1:CH + 1], tD[:, :, 1:CH + 1]
                    TT = lambda eng, o, a, b_, op, rd, wr: S.c(eng, lambda e: e.tensor_tensor(out=o, in0=a, in1=b_, op=op), reads=rd, writes=wr)
                    TT("dve", v4(A1), PCv, v4(tabc[:]), ALU.mult, bC + bD + [b_tab], [b_tA])
                    TT("dve", v4(B1), PDv, v4(tabs[:]), ALU.mult, bC + bD + [b_tab], [b_tB])
                    TT("dve", A1, A1, B1, ALU.add, [b_tA, b_tB], [b_tA])
                    TT("dve", v4(C1), PDv, v4(tabc[:]), ALU.mult, bC + bD + [b_tab], [b_tC])
                    TT("dve", v4(D1), PCv, v4(tabs[:]), ALU.mult, bC + bD + [b_tab], [b_tD])
                    TT("dve", C1, C1, D1, ALU.subtract, [b_tC, b_tD], [b_tC])
                    S.c("pool", lambda e: e.tensor_copy(out=tA[:, :, 0], in_=sp_small[:, GCR, :]), reads=[b_gc], writes=[b_tA])
                    S.c("pool", lambda e: e.tensor_copy(out=tC[:, :, 0], in_=sp_small[:, GCI, :]), reads=[b_gc], writes=[b_tC])
                    fl = lambda a: a.rearrange("p q c -> p (q c)")
                    S.c("dve", lambda e: e.tensor_tensor_scan(out=fl(tB), data0=fl(rdec[:]), data1=fl(tA), initial=0.0, op0=ALU.mult, op1=ALU.add),
                        reads=[b_rdec, b_tA], writes=[b_tB])
                    S.c("dve", lambda e: e.tensor_tensor_scan(out=fl(tD), data0=fl(rdec[:]), data1=fl(tC), initial=0.0, op0=ALU.mult, op1=ALU.add),
                        reads=[b_rdec, b_tC], writes=[b_tD])
                    S.c("pool", lambda e: e.tensor_copy(out=sp_small[:, GCR, :], in_=tB[:, :, CH]), reads=[b_tB], writes=[b_gc])
                    S.c("pool", lambda e: e.tensor_copy(out=sp_small[:, GCI, :], in_=tD[:, :, CH]), reads=[b_tD], writes=[b_gc])
                    TT("dve", A1, B1, tabc[:], ALU.mult, [b_tB, b_tab], [b_tA])
                    TT("dve", C1, D1, tabs[:], ALU.mult, [b_tD, b_tab], [b_tC])
                    TT("dve", Hb[:, :, 0, 1:CH + 1], A1, C1, ALU.subtract, [b_tA, b_tC], [b_Hb])
                    if last_blk:
                        TT("pool", sp_small[:, HFR, :], tA[:, :, CH], tC[:, :, CH], ALU.subtract, [b_tA, b_tC], [b_hfin])
                    TT("dve", A1, B1, tabs[:], ALU.mult, [b_tB, b_tab], [b_tA])
                    TT("dve", C1, D1, tabc[:], ALU.mult, [b_tD, b_tab], [b_tC])
                    TT("dve", Hb[:, :, 1, 1:CH + 1], A1, C1, ALU.add, [b_tA, b_tC], [b_Hb])
                    if last_blk:
                        TT("pool", sp_small[:, HFI, :], tA[:, :, CH], tC[:, :, CH], ALU.add, [b_tA, b_tC], [b_hfin])
                        S.dma("sp", lambda e: e.dma_start(out=pre_d.rearrange("q p -> p q"), in_=sp_small[:, HFR, :], allow_slow_non_contiguous=True),
                              reads=[b_hfin], out=True)
                        S.dma("sp", lambda e: e.dma_start(out=pim_d.rearrange("q p -> p q"), in_=sp_small[:, HFI, :], allow_slow_non_contiguous=True),
                              reads=[b_hfin], out=True)
                if stage >= 4:
                    for kc in range(4):
                        pi = 2 + (kc % 2)
                        Y1, Y2 = pbank(pi, 0), pbank(pi, 1)

                        def ystage(e, kc=kc, Y1=Y1, Y2=Y2):
                            last = None
                            for jp in range(T8):
                                for j in range(jp + 1):
                                    last = e.matmul(Y1[:, jp:BT:T8], lhsT=Kblk[:, kc, jp - j, :], rhs=uT[:, kc, j:BT:T8], start=(j == 0), stop=(j == jp))
                            for jp in range(T8):
                                for q4 in range(4):
                                    q = q4 * 4 + kc
                                    o = Y2[32 * q4:32 * q4 + 32, jp:BT:T8]
                                    e.matmul(o, lhsT=Cst[:, q, jp + 1, 0, :], rhs=Hb[:, q, 0, 0:CH], start=True, stop=False, tile_position=(0, 32 * q4))
                                    last = e.matmul(o, lhsT=Cst[:, q, jp + 1, 1, :], rhs=Hb[:, q, 1, 0:CH], start=False, stop=True, tile_position=(0, 32 * q4))
                            return last
                        S.c("pe", ystage, reads=[b_Kblk, b_Cst, b_uT[kc], b_Hb, b_Hb0], writes=[bank[pi][0], bank[pi][1]])
                        s = kc % 2
                        S.c("act", lambda e, s=s, Y1=Y1: e.activation(out=ytmp[:, s, :], in_=Y1, func=AF.Copy), reads=[bank[pi][0]], writes=[b_ytmp[s]])
                        S.c("dve", lambda e, s=s, Y2=Y2: e.tensor_tensor(out=ytmp[:, s, :], in0=ytmp[:, s, :], in1=Y2, op=ALU.add),
                            reads=[b_ytmp[s], bank[pi][1]], writes=[b_ytmp[s]])
                        S.c("dve", lambda e, s=s, kc=kc: e.scalar_tensor_tensor(out=ytmp[:, s, :], in0=uT[:, kc, :], scalar=vecs[:, V_DS + kc:V_DS + kc + 1],
                                                                            in1=ytmp[:, s, :], op0=ALU.mult, op1=ALU.add),
                            reads=[b_ytmp[s], b_uT[kc], b_vecs], writes=[b_ytmp[s]])
                        S.c("act", lambda e, s=s, kc=kc: e.activation(out=yg[:, kc, :], in_=ytmp[:, s, :], func=AF.Gelu), reads=[b_ytmp[s]], writes=[b_yg[kc]])
                if stage >= 5:
                    S.c("pool", lambda e: e.tensor_copy(out=Hb[:, :, :, 0], in_=Hb[:, :, :, CH]), reads=[b_Hb], writes=[b_Hb0])
                    for m in range(4):
                        po, pbuf = next_bank()

                        def glu(e, m=m, po=po):
                            last = None
                            for k in range(4):
                                last = e.matmul(po, lhsT=w_glu_sb[:, k, m * 128:(m + 1) * 128], rhs=yg[:, k, :], start=(k == 0), stop=(k == 3))
                            return last
                        S.c("pe", glu, reads=[b_wglu] + b_yg, writes=[pbuf])
                        s = m % 2
                        S.c("act", lambda e, s=s, po=po: e.activation(out=ytmp[:, s, :], in_=po, func=AF.Sigmoid), reads=[pbuf], writes=[b_ytmp[s]])
                        S.c("dve", lambda e, s=s, m=m: e.tensor_tensor(out=hT[:, m, :], in0=yg[:, m, :], in1=ytmp[:, s, :], op=ALU.mult),
                            reads=[b_ytmp[s], b_yg[m]], writes=b_h[m])
                if stage < 5:
                    for m in range(4):
                        S.c("dve", lambda e, m=m: e.memset(hT[:, m, :], 0.0), writes=b_h[m])
            else:
                for m in range(4):
                    S.c("dve", lambda e, m=m: e.memset(hT[:, m, :], 0.0), writes=b_h[m])

            load_wout()
            for t in range(4):
                pi = 2 + (t % 2)

                def oproj(e, t=t, pi=pi):
                    last = None
                    for h in range(2):
                        for k in range(8):
                            last = e.matmul(pbank(pi, h), lhsT=hT[:, k, t * 128:(t + 1) * 128], rhs=w_out_sb[:, k, h * 512:(h + 1) * 512],
                                            start=(k == 0), stop=(k == 7))
                    return last
                S.c("pe", oproj, reads=[b_wout] + [b_h[k][t] for k in range(8)], writes=[bank[pi][0], bank[pi][1]])
                S.c("dve", lambda e, t=t, pi=pi, xb=xb: e.tensor_tensor(out=xb[:, t, :], in0=xb[:, t, :], in1=PSn[pi][:, :], op=ALU.add),
                    reads=[bx[t], bank[pi][0], bank[pi][1]], writes=[bx[t]])
            norm_to_hT([(xb[:, t, :], bx[t], t * 128, 128) for t in range(4)], V_G2)
            fence()
            if blk + 1 < nblk:
                for t in range(4):
                    S.dma("sp", lambda e, t=t, r1=r0 + BT, xn_=xn_: e.dma_start(out=xn_[:, t, :], in_=x_d[r1 + t * 128:r1 + (t + 1) * 128, :]), writes=[bxn[t]])
            all_h = flat(b_h)
            if blk > 0:
                cwv = vecs[:, V_CW:V_CW + 132].rearrange("p (c k) -> p c k", k=3)
                S.c("pool", lambda e: e.tensor_tensor(out=hfix[:], in0=hhalo[:], in1=cwv[:, :, 0:1].broadcast_to([128, 2 * NF, 2]), op=ALU.mult),
                    reads=[b_hhalo, b_vecs], writes=[b_hfix])
                S.c("pool", lambda e: e.tensor_tensor(out=hfix2[:], in0=hhalo[:, :, 1], in1=cwv[:, :, 1], op=ALU.mult),
                    reads=[b_hhalo, b_vecs], writes=[b_hfix2])
                S.c("pool", lambda e: e.tensor_tensor(out=hfix[:, :, 0], in0=hfix[:, :, 0], in1=hfix2[:], op=ALU.add),
                    reads=[b_hfix, b_hfix2], writes=[b_hfix])
            def ffn_tail(f):
                cs = f % 3
                S.c("act", lambda e, cs=cs: e.activation(out=cvt3[:, cs, 0, :], in_=cvt3[:, cs, 0, :], func=AF.Gelu), reads=[b_cvt3[cs][0]], writes=[b_cvt3[cs][0]])
                S.c("pool", lambda e, f=f, cs=cs: e.tensor_tensor(out=actT[:, f, :], in0=cvt3[:, cs, 0, :], in1=cvt3[:, cs, 1, :], op=ALU.mult),
                    reads=b_cvt3[cs], writes=[b_act[f]])

            for f in range(NF):
                slot = load_wup(f)
                pi = 1 + (f % 3)
                cs = f % 3

                def up(e, slot=slot, pi=pi):
                    last = None
                    for gv in range(2):
                        for k in range(8):
                            last = e.matmul(pbank(pi, gv), lhsT=w_up_ring[slot][:, k, gv * 128:(gv + 1) * 128], rhs=hT[:, k, :], start=(k == 0), stop=(k == 7))
                    return last
                S.c("pe", up, reads=[b_wup[slot]] + all_h, writes=[bank[pi][0], bank[pi][1]])
                if mrg:
                    sample_up(f, slot)
                for gv in range(2):
                    ch = f + gv * NF
                    cw2 = vecs[:, V_CW + ch * 3 + 2:V_CW + ch * 3 + 3]
                    S.c("act", lambda e, gv=gv, ch=ch, cw2=cw2, cs=cs, pi=pi: e.activation(out=cvt3[:, cs, gv, :], in_=pbank(pi, gv), func=AF.Identity, scale=cw2,
                                                                                     bias=vecs[:, V_CB + ch:V_CB + ch + 1]),
                        reads=[bank[pi][gv], b_vecs], writes=[b_cvt3[cs][gv]])
                if f > 0:
                    ffn_tail(f - 1)
                for gv in range(2):
                    ch = f + gv * NF
                    cw = lambda k, ch=ch: vecs[:, V_CW + ch * 3 + k:V_CW + ch * 3 + k + 1]
                    cc = cvt3[:, cs, gv, :]
                    pg = pbank(pi, gv)
                    bc = b_cvt3[cs][gv]
                    S.c("dve", lambda e, cc=cc, pg=pg, cw=cw: e.scalar_tensor_tensor(out=cc[:, 1:BT], in0=pg[:, 0:BT - 1], scalar=cw(1), in1=cc[:, 1:BT],
                                                                                 op0=ALU.mult, op1=ALU.add), reads=[bank[pi][gv], b_vecs, bc], writes=[bc])
                    S.c("dve", lambda e, cc=cc, pg=pg, cw=cw: e.scalar_tensor_tensor(out=cc[:, 2:BT], in0=pg[:, 0:BT - 2], scalar=cw(0), in1=cc[:, 2:BT],
                                                                                 op0=ALU.mult, op1=ALU.add), reads=[bank[pi][gv], b_vecs, bc], writes=[bc])
                    if blk > 0:
                        S.c("dve", lambda e, cc=cc, ch=ch: e.tensor_tensor(out=cc[:, 0:2], in0=cc[:, 0:2], in1=hfix[:, ch, :], op=ALU.add),
                            reads=[b_hfix, bc], writes=[bc])
                S.c("dve", lambda e, f=f, pi=pi: e.tensor_copy(out=hhalo[:, f:2 * NF:NF, :], in_=PSn[pi][:, :].rearrange("p (g n) -> p g n", g=2)[:, :, BT - 2:BT]),
                    reads=[bank[pi][0], bank[pi][1]], writes=[b_hhalo])
            ffn_tail(NF - 1)
            if last_blk:
                S.dma("sp", lambda e: [e.dma_start(out=pconv_d[r, :].rearrange("(c p) -> p c", p=128), in_=hhalo[:, :, r], allow_slow_non_contiguous=True)
                                        for r in range(2)], reads=[b_hhalo], out=True, ndma=2)
            def emit_down(m):
                slot = load_wdn(m)
                po, pbuf = next_bank()
                if mrg:
                    sample_down(m, slot)

                def down(e, slot=slot, po=po):
                    last = None
                    for k in range(NF):
                        last = e.matmul(po, lhsT=w_dn_ring[slot][:, k, :], rhs=actT[:, k, :], start=(k == 0), stop=(k == NF - 1))
                    return last
                S.c("pe", down, reads=[b_wdn[slot]] + b_act, writes=[pbuf])
                s = m % 2
                S.c("act", lambda e, s=s, po=po: e.activation(out=scr[:, s, 0:BT], in_=po, func=AF.Copy), reads=[pbuf], writes=[b_scr[s]])

            def emit_btr(m):
                s = m % 2
                hb = m % 2

                def btr(e, s=s, hb=hb):
                    last = None
                    for t in range(4):
                        last = e.transpose(out=PA[:, hb * 512 + t * 128:hb * 512 + (t + 1) * 128], in_=scr[:, s, t * 128:(t + 1) * 128], identity=ident[:])
                    return last
                S.c("pe", btr, reads=[b_scr[s], b_ident], writes=[bank[0][hb]])
                if mrg:
                    sample_btr(m)
                S.c("dve", lambda e, m=m, hb=hb, xb=xb: e.tensor_tensor(out=xb[:, :, m * 128:(m + 1) * 128], in0=xb[:, :, m * 128:(m + 1) * 128],
                                                             in1=PA[:, hb * 512:(hb + 1) * 512].rearrange("p (t n) -> p t n", t=4), op=ALU.add),
                    reads=bx + [bank[0][hb]], writes=bx)
            emit_down(0)
            if do_s5 and not last_blk:
                cWb = sp_small[:, CW, :].unsqueeze(2).broadcast_to([128, NQ, CH])
                sWb = sp_small[:, SW, :].unsqueeze(2).broadcast_to([128, NQ, CH])
                u1 = cvt3[:, 0, :, :].rearrange("p g n -> p (g n)").rearrange("p (q c) -> p q c", q=NQ)
                u2 = cvt3[:, 1, :, :].rearrange("p g n -> p (g n)").rearrange("p (q c) -> p q c", q=NQ)
                bu1, bu2 = b_cvt3[0], b_cvt3[1]
                TTp = lambda o, a_, b_, op, rd, wr: S.c("pool", lambda e: e.tensor_tensor(out=o, in0=a_, in1=b_, op=op), reads=rd, writes=wr)
                TTp(u1, tabc[:], sWb, ALU.mult, [b_tab, b_sps], bu1)
                TTp(u2, tabs[:], sWb, ALU.mult, [b_tab, b_sps], bu2)
                TTp(tabc[:], tabc[:], cWb, ALU.mult, [b_tab, b_sps], [b_tab])
                TTp(tabc[:], tabc[:], u2, ALU.subtract, [b_tab] + bu2, [b_tab])
                TTp(tabs[:], tabs[:], cWb, ALU.mult, [b_tab, b_sps], [b_tab])
                TTp(tabs[:], tabs[:], u1, ALU.add, [b_tab] + bu1, [b_tab])
            for m in range(8):
                if m + 1 < 8:
                    emit_down(m + 1)
                emit_btr(m)
            fence()
            if blk + 1 < nblk:
                norm_to_hT([(xn_[:, t, :], bxn[t], t * 128, 128) for t in range(4)], V_G1)
                if not (do_sample and blk + 1 == nblk - 1):
                    block_early(blk + 1)
            if mrg:
                sample_phase_C()
            rms_stats(4, [xb[:, t, :] for t in range(4)], bx)
            for t in range(4):
                s = t % 2
                S.c("dve", lambda e, t=t, s=s, xb=xb: e.scalar_tensor_tensor(out=scr[:, s, :], in0=xb[:, t, :], scalar=stat[:, 8 + t:9 + t], in1=g3b[:],
                                                                  op0=ALU.mult, op1=ALU.mult), reads=[bx[t], b_stat, b_g3b], writes=[b_scr[s]])
                S.dma("sp", lambda e, t=t, s=s, r0=r0: e.dma_start(out=y_d[r0 + t * 128:r0 + (t + 1) * 128, :], in_=scr[:, s, :]), reads=[b_scr[s]], out=True)

        S.emit(st)
        build_program.stats = S.stats
    return nc


_CACHE = {}


def _pack_inputs(inp):
    f32 = np.float32
    A = lambda a: np.ascontiguousarray(np.asarray(a, dtype=f32))
    vecs = np.zeros((128, NV), f32)
    vecs[:, V_G1:V_G1 + 8] = A(inp["norm_mix_g"])[0].reshape(8, 128).T
    vecs[:, V_G2:V_G2 + 8] = A(inp["norm_ffn_g"])[0].reshape(8, 128).T
    vecs[:, V_PS:V_PS + 4] = A(inp["pool_scale"])[0].reshape(4, 128).T
    vecs[:, V_DS:V_DS + 4] = A(inp["s5_d"])[0].reshape(4, 128).T
    vecs[:, V_CW:V_CW + 132] = A(inp["ffn_conv_w"])[0].reshape(3, 44, 128).transpose(2, 1, 0).reshape(128, 132)
    vecs[:, V_CB:V_CB + 44] = A(inp["ffn_conv_b"])[0].reshape(44, 128).T
    vecs[:, V_IC:V_IC + 16] = (1.0 / np.arange(1, 17, dtype=np.float64)).astype(f32)[None, :]
    vecs[:, V_IO:V_IO + 64] = np.arange(64, dtype=f32)[None, :]
    s5p = np.zeros((128, NSP), f32)
    perm = np.array([(qp % 4) * 4 + qp // 4 for qp in range(16)])
    lay = lambda a: A(a)[0].reshape(16, 2, 64)[perm].transpose(1, 2, 0).reshape(128, 16)
    s5p[:, SP_ARE:SP_ARE + 16] = lay(inp["s5_a_re"])
    s5p[:, SP_AIM:SP_AIM + 16] = lay(inp["s5_a_im"])
    s5p[:, SP_LDT:SP_LDT + 16] = np.broadcast_to(A(inp["s5_log_dt"])[0].reshape(16, 2, 1)[perm], (16, 2, 64)).transpose(1, 2, 0).reshape(128, 16)
    for off, key in ((SP_BRE, "s5_b_re"), (SP_BIM, "s5_b_im")):
        b = A(inp[key])[0].reshape(16, 2, 64, 16)[perm]
        o = np.zeros((2, 64, 16, 2, 16), f32)
        for g2 in range(2):
            o[g2, :, :, g2, :] = b[:, g2].transpose(1, 0, 2)
        s5p[:, off:off + 512] = o.reshape(128, 512)
    for off, key in ((SP_CRE, "s5_c_re"), (SP_CIM, "s5_c_im")):
        c = A(inp[key])[0].reshape(16, 2, 16, 64)[perm]
        o = np.zeros((2, 64, 16, 2, 16), f32)
        for g2 in range(2):
            o[g2, :, :, g2, :] = c[:, g2].transpose(2, 0, 1)
        s5p[:, off:off + 512] = o.reshape(128, 512)
    sel = np.zeros((128, 32), f32)
    for g, w in enumerate((2, 4, 8, 16)):
        for t in range(8):
            for r in range(16 - w, 15):
                sel[t * 15 + r, g * 8 + t] = 1.0
    shared = {
        "w_in": A(inp["w_in"])[0], "w_glu": A(inp["s5_w_glu"])[0], "pool_w": A(inp["pool_w"])[0].reshape(512, 128),
        "w_out": A(inp["w_out"])[0], "w_up": A(inp["ffn_w_up"])[0], "w_down": A(inp["ffn_w_down"])[0],
        "vecs": vecs, "g3b": np.ascontiguousarray(np.broadcast_to(A(inp["norm_final_g"])[None, :], (128, D))),
        "ident": np.eye(128, dtype=f32), "s5p": s5p, "sel": sel,
    }
    maps = []
    for i in range(NCORES):
        sl = slice(i * NS, (i + 1) * NS)
        m = dict(shared)
        m["x"] = A(inp["x_prompt"])[i]
        m["xs"] = A(inp["x_sample"])[sl, 0, :]
        m["s5re"] = A(inp["state_s5_re"])[0, sl].reshape(NS, 2048)
        m["s5im"] = A(inp["state_s5_im"])[0, sl].reshape(NS, 2048)
        m["pst"] = A(inp["state_pool"])[0, sl]
        m["cst"] = A(inp["state_ffn_conv"])[0, sl]
        maps.append(m)
    return maps


def kernel(**inputs):
    if "nc" not in _CACHE:
        _CACHE["nc"] = build_program()
    nc = _CACHE["nc"]
    maps = _pack_inputs(inputs)
    res = run_bass_kernel_spmd(nc, maps, core_ids=list(range(NCORES)))
    R = res.results
    f32 = np.float32
    y_prompt = np.stack([R[i]["y"] for i in range(NCORES)]).astype(f32)
    y_sample = np.concatenate([R[i]["ys"] for i in range(NCORES)])[:, None, :].astype(f32)
    inv = np.array([(q % 4) * 4 + q // 4 for q in range(16)])
    p_re = np.stack([R[i]["p_re"][inv].reshape(NG, NP_) for i in range(NCORES)])[None].astype(f32)
    p_im = np.stack([R[i]["p_im"][inv].reshape(NG, NP_) for i in range(NCORES)])[None].astype(f32)
    p_pool = np.stack([R[i]["p_pool"] for i in range(NCORES)])[None].astype(f32)
    p_conv = np.stack([R[i]["p_conv"] for i in range(NCORES)])[None].astype(f32)
    s_re = np.concatenate([R[i]["s_re"].reshape(NS, NG, NP_) for i in range(NCORES)])[None].astype(f32)
    s_im = np.concatenate([R[i]["s_im"].reshape(NS, NG, NP_) for i in range(NCORES)])[None].astype(f32)
    s_pool = np.concatenate([R[i]["s_pool"] for i in range(NCORES)])[None].astype(f32)
    s_conv = np.concatenate([R[i]["s_conv"] for i in range(NCORES)])[None].astype(f32)
    return (y_prompt, y_sample, p_re, p_im, p_pool, p_conv, s_re, s_im, s_pool, s_conv)
```

```python
import math
from contextlib import ExitStack

import numpy as np
import concourse.bass as bass
import concourse.mybir as mybir
from concourse.bass_utils import run_bass_kernel_spmd

F32 = mybir.dt.float32
BF16 = mybir.dt.bfloat16
AF = mybir.ActivationFunctionType
ALU = mybir.AluOpType
AX = mybir.AxisListType


class Buf:
    __slots__ = ("name", "writer", "readers")

    def __init__(self, name):
        self.name = name
        self.writer = None
        self.readers = []


class Op:
    __slots__ = ("eng", "fn", "kind", "deps", "signal", "tok", "waits", "clock", "ndma", "idx")


class Sched:
    ENGS = ("pe", "act", "dve", "pool", "sp")
    NRING = 40

    def __init__(self, nc):
        self.nc = nc
        self.ops = []
        self.ring_n = 0
        self.ring_cum = [0] * self.NRING
        self.ring_last = [None] * self.NRING
        self.out_dmas = []
        self.n_sw = 0

    def buf(self, name):
        return Buf(name)

    def _add(self, eng, fn, reads, writes, kind, ndma=0):
        op = Op()
        op.eng, op.fn, op.kind, op.ndma = eng, fn, kind, ndma
        op.signal = False
        op.idx = len(self.ops)
        deps = []
        for b in reads:
            if b.writer is not None:
                deps.append(b.writer)
        for b in writes:
            if b.writer is not None:
                deps.append(b.writer)
            deps.extend(b.readers)
        for b in reads:
            b.readers.append(op)
        for b in writes:
            b.writer = op
            b.readers = []
        if kind == "dma" and eng == "pool":
            op.tok = (("sw", self.n_sw), 16 * ndma)
            self.n_sw += 1
        elif kind == "dma":
            r = self.ring_n % self.NRING
            self.ring_n += 1
            if self.ring_last[r] is not None:
                deps.append(self.ring_last[r])
            self.ring_cum[r] += 16 * ndma
            op.tok = (("dma", r), self.ring_cum[r])
            self.ring_last[r] = op
        else:
            op.tok = None
        op.deps = [d for d in set(deps) if not (d.eng == "pe" and eng == "pe" and d.kind == "c" and kind == "c") and d is not op]
        self.ops.append(op)
        return op

    def c(self, eng, fn, reads=(), writes=()):
        return self._add(eng, fn, list(reads), list(writes), "c")

    def dma(self, eng, fn, reads=(), writes=(), ndma=1, out=False):
        op = self._add(eng, fn, list(reads), list(writes), "dma", ndma)
        if out:
            self.out_dmas.append(op)
        return op

    def emit(self, stack):
        nc = self.nc
        fin = self._add("sp", None, [], [], "c")
        fin.deps = list(self.out_dmas)
        for op in self.ops:
            for d in op.deps:
                d.signal = True
        cnt = {e: 0 for e in self.ENGS}
        for op in self.ops:
            if op.kind == "c" and op.signal:
                cnt[op.eng] += 1
                op.tok = (op.eng, cnt[op.eng])
        known = {e: {} for e in self.ENGS}
        for op in self.ops:
            kn = known[op.eng]
            waits = []
            for d in sorted(op.deps, key=lambda o: o.idx):
                k, v = d.tok
                if kn.get(k, 0) >= v:
                    continue
                waits.append((k, v))
                for kk, vv in d.clock.items():
                    if kn.get(kk, 0) < vv:
                        kn[kk] = vv
            best = {}
            for k, v in waits:
                best[k] = max(best.get(k, 0), v)
            op.waits = list(best.items())
            op.clock = dict(kn)
            if op.tok is not None:
                op.clock[op.tok[0]] = max(op.clock.get(op.tok[0], 0), op.tok[1])
        sems = {}
        for e in ("pe", "act", "dve", "pool", "sp"):
            sems[e] = stack.enter_context(nc.semaphore("s_" + e))
        for r in range(min(self.NRING, max(self.ring_n, 1))):
            sems[("dma", r)] = stack.enter_context(nc.semaphore("s_dma%d" % r))
        for r in range(self.n_sw):
            sems[("sw", r)] = stack.enter_context(nc.semaphore("s_sw%d" % r))
        block = stack.enter_context(nc.Block())
        per = {e: [o for o in self.ops if o.eng == e] for e in self.ENGS}
        self.stats = {e: (len(per[e]), sum(len(o.waits) for o in per[e])) for e in self.ENGS}

        def run(eng_handle, lst):
            for op in lst:
                for k, v in op.waits:
                    eng_handle.wait_ge(sems[k], v)
                if op.fn is None:
                    continue
                res = op.fn(eng_handle)
                if op.kind == "dma":
                    if not isinstance(res, (list, tuple)):
                        res = [res]
                    assert len(res) == op.ndma, (len(res), op.ndma)
                    for ins in res:
                        ins.then_inc(sems[op.tok[0]], 16)
                elif op.signal:
                    if isinstance(res, (list, tuple)):
                        res = res[-1]
                    res.then_inc(sems[op.tok[0]], 1)

        @block.tensor
        def _(e):
            run(e, per["pe"])

        @block.scalar
        def _(e):
            run(e, per["act"])

        @block.vector
        def _(e):
            run(e, per["dve"])

        @block.gpsimd
        def _(e):
            run(e, per["pool"])

        @block.sync
        def _(e):
            run(e, per["sp"])


NCORES = 8
D = 1024
L = 2048
BT = 512
NBLK = L // BT
NS = 16
NG, NP_, NH = 32, 64, 16
NQ = 16
T8 = 8
CH = BT // T8
DFF = 2816
NF = DFF // 128
EPS = 1e-6
MAGIC = 12582912.0
TWO_PI = 2.0 * math.pi
PI_LO = 3.1415925

V_G1, V_G2, V_PS, V_DS, V_CW, V_CB, V_IC, V_IO = 0, 8, 16, 20, 24, 156, 200, 216
NV = 280
SP_ARE, SP_AIM, SP_LDT, SP_BRE, SP_BIM, SP_CRE, SP_CIM = 0, 16, 32, 48, 560, 1072, 1584
NSP = 2096


def build_program(do_s5=True, do_sample=True, nblk=NBLK, stage=99):
    nc = bass.Bass("TRN2", target_bir_lowering=False)
    din = lambda name, shape: nc.dram_tensor(name, list(shape), F32, kind="ExternalInput").ap()
    dout = lambda name, shape: nc.dram_tensor(name, list(shape), F32, kind="ExternalOutput").ap()
    x_d = din("x", [L, D])
    xs_d = din("xs", [NS, D])
    w_in_d = din("w_in", [D, D])
    w_glu_d = din("w_glu", [512, 512])
    pool_w_d = din("pool_w", [512, 128])
    w_out_d = din("w_out", [D, D])
    w_up_d = din("w_up", [D, 2 * DFF])
    w_down_d = din("w_down", [DFF, D])
    vecs_d = din("vecs", [128, NV])
    g3b_d = din("g3b", [128, D])
    ident_d = din("ident", [128, 128])
    s5p_d = din("s5p", [128, NSP])
    s5re_d = din("s5re", [NS, 2048])
    s5im_d = din("s5im", [NS, 2048])
    pst_d = din("pst", [NS, 15, 512])
    cst_d = din("cst", [NS, 2, 2 * DFF])
    y_d = dout("y", [L, D])
    ys_d = dout("ys", [NS, D])
    pre_d = dout("p_re", [NQ, 128])
    pim_d = dout("p_im", [NQ, 128])
    ppool_d = dout("p_pool", [15, 512])
    pconv_d = dout("p_conv", [2, 2 * DFF])
    sre_d = dout("s_re", [NS, 2048])
    sim_d = dout("s_im", [NS, 2048])
    spool_d = dout("s_pool", [NS, 15, 512])
    sconv_d = dout("s_conv", [NS, 2, 2 * DFF])

    wupb = nc.dram_tensor("wupb", [NF, 128, 8, 256], BF16).ap()
    wdnb = nc.dram_tensor("wdnb", [8, 128, NF, 128], BF16).ap()
    winb = nc.dram_tensor("winb", [8, 128, 8, 128], BF16).ap()
    woutb = nc.dram_tensor("woutb", [128, 8, D], BF16).ap()
    st = ExitStack()
    with st:
        S = Sched(nc)
        sbt = lambda name, shape, dt: st.enter_context(nc.sbuf_tensor("sb_" + name, list(shape), dt))
        w_glu_sb = sbt("w_glu_sb", [128, 4, 512], BF16)
        pool_w_sb = sbt("pool_w_sb", [128, 4, 128], BF16)
        NWI, NWU, NWD = 2, 3, 2
        w_in_ring = [sbt("w_in_r%d" % i, [128, 8, 128], BF16) for i in range(NWI)]
        w_up_ring = [sbt("w_up_r%d" % i, [128, 8, 256], BF16) for i in range(NWU)]
        w_dn_ring = [sbt("w_dn_r%d" % i, [128, NF, 128], BF16) for i in range(NWD)]
        vecs = sbt("vecs", [128, NV], F32)
        g3b = sbt("g3b", [128, D], F32)
        ident = sbt("ident", [128, 128], F32)
        identb = sbt("identb", [128, 128], BF16)
        Bst = sbt("Bst", [128, 4, T8, 2, 128], BF16)
        Cst = sbt("Cst", [128, NQ, T8 + 1, 2, 32], BF16)
        Kblk = sbt("Kblk", [128, 4, T8, 128], BF16)
        tabc = sbt("tabc", [128, NQ, CH], F32)
        tabs = sbt("tabs", [128, NQ, CH], F32)
        rdec = sbt("rdec", [128, NQ, CH + 1], F32)
        sp_small = sbt("sp_small", [128, 40, NQ], F32)
        LR, LI, PR0, PI0, CW, SW, GCR, GCI, HFR, HFI, TMPA, TMPB, TMPC, TMPD, ANG0 = 0, 1, 2, 11, 20, 21, 22, 23, 24, 25, 26, 27, 28, 29, 30
        stat = sbt("stat", [128, 16], F32)
        x_sb = sbt("x_sb", [128, 4, D], F32)
        x_sb2 = sbt("x_sb2", [128, 4, D], F32)
        scr = sbt("scr", [128, 2, D], F32)
        hT = sbt("hT", [128, 8, BT], BF16)
        v_sb = sbt("v_sb", [128, 4, 16 + BT], F32)
        Hb = sbt("Hb", [128, NQ, 2, CH + 1], BF16)
        hhalo = sbt("hhalo", [128, 2 * NF, 2], F32)
        hfix = sbt("hfix", [128, 2 * NF, 2], F32)
        hfix2 = sbt("hfix2", [128, 2 * NF], F32)
        ARENA_W = 8712
        arena = sbt("arena", [128, ARENA_W], F32)
        fence_scr = sbt("fence_scr", [128, 2], F32)

        def av(off, nwords, dt=F32, **kw):
            a = arena[:, off:off + nwords]
            if dt != F32:
                a = a.bitcast(dt)
            return a

        actT = av(0, 5632, BF16).rearrange("p (k n) -> p k n", k=NF)
        hup = av(5632, 2056).rearrange("p (s g n) -> p s g n", s=2, g=2)
        cvt = av(7688, 1024).rearrange("p (g n) -> p g n", g=2)
        cvt3 = av(5632, 3072).rearrange("p (s g n) -> p s g n", s=3, g=2)
        w_out_sb = av(0, 4096, BF16).rearrange("p (k n) -> p k n", k=8)
        tA = av(0, 1040).rearrange("p (q c) -> p q c", q=NQ)
        tB = av(1040, 1040).rearrange("p (q c) -> p q c", q=NQ)
        tC = av(2080, 1040).rearrange("p (q c) -> p q c", q=NQ)
        tD = av(3120, 1040).rearrange("p (q c) -> p q c", q=NQ)
        uT = av(4160, 1024, BF16).rearrange("p (k n) -> p k n", k=4)
        yg = av(5184, 1024, BF16).rearrange("p (k n) -> p k n", k=4)
        ptmp = av(6208, 1056).rearrange("p (s n) -> p s n", s=2)
        pooled = av(7264, 256, BF16)
        ytmp = av(7520, 1024).rearrange("p (s n) -> p s n", s=2)
        s5p = av(0, NSP)
        G0r = av(2096, 512).rearrange("p (q h) -> p q h", q=NQ)
        G0i = av(2608, 512).rearrange("p (q h) -> p q h", q=NQ)
        Gnr = av(3120, 512).rearrange("p (q h) -> p q h", q=NQ)
        Gni = av(3632, 512).rearrange("p (q h) -> p q h", q=NQ)
        Gnb = av(4144, 512, BF16).rearrange("p (r q h) -> p r q h", r=2, q=NQ)
        Ccb = av(4656, 512, BF16).rearrange("p (r q h) -> p r q h", r=2, q=NQ)
        pt1 = av(5168, 512).rearrange("p (q h) -> p q h", q=NQ)
        pt2 = av(5680, 512).rearrange("p (q h) -> p q h", q=NQ)
        pang = av(6192, 1024).rearrange("p (q c) -> p q c", q=NQ)
        pang2 = av(7216, 1024).rearrange("p (q c) -> p q c", q=NQ)

        PA = st.enter_context(nc.psum_tensor("psA", [128, 1024], F32))
        PB = st.enter_context(nc.psum_tensor("psB", [128, 1024], F32))
        PCD = st.enter_context(nc.psum_tensor("psCD", [128, 2048], F32))
        PC, PD = PCD[:, 0:1024], PCD[:, 1024:2048]
        PSn = [PA, PB, PC, PD]
        bank = [[S.buf("ps%d_%d" % (i, h)) for h in range(2)] for i in range(4)]

        def pbank(i, h):
            return PSn[i][:, h * 512:(h + 1) * 512]

        b_wout, b_wglu, b_poolw = S.buf("wout"), S.buf("wglu"), S.buf("poolw")
        b_win = [S.buf("win%d" % i) for i in range(NWI)]
        b_wup = [S.buf("wup%d" % i) for i in range(NWU)]
        b_wdn = [S.buf("wdn%d" % i) for i in range(NWD)]
        b_vecs, b_g3b, b_ident, b_identb = S.buf("vecs"), S.buf("g3b"), S.buf("ident"), S.buf("identb")
        b_Bst, b_Cst, b_Kblk, b_tab, b_rdec, b_sps = S.buf("Bst"), S.buf("Cst"), S.buf("Kblk"), S.buf("tab"), S.buf("rdec"), S.buf("sps")
        b_gc, b_hfin = S.buf("gc"), S.buf("hfin")
        b_stat = S.buf("stat")
        b_x = [S.buf("x%d" % t) for t in range(4)]
        b_x2 = [S.buf("x2_%d" % t) for t in range(4)]
        b_scr = [S.buf("scr%d" % i) for i in range(2)]
        b_h = [[S.buf("h%d_%d" % (k, t)) for t in range(4)] for k in range(8)]
        b_v = [S.buf("v%d" % g) for g in range(4)]
        b_Hb, b_Hb0 = S.buf("Hb"), S.buf("Hb0")
        b_hhalo = S.buf("hhalo")
        b_hfix, b_hfix2 = S.buf("hfix"), S.buf("hfix2")
        b_act = [S.buf("act%d" % f) for f in range(NF)]
        b_hup = [S.buf("hup%d" % s) for s in range(2)]
        b_cvt = S.buf("cvt")
        b_cvt3 = [[S.buf("cvt3_%d_%d" % (i, g)) for g in range(2)] for i in range(3)]
        b_tA, b_tB, b_tC, b_tD = S.buf("tA"), S.buf("tB"), S.buf("tC"), S.buf("tD")
        b_uT = [S.buf("uT%d" % k) for k in range(4)]
        b_yg = [S.buf("yg%d" % k) for k in range(4)]
        b_ptmp = [S.buf("ptmp%d" % i) for i in range(2)]
        b_pooled = S.buf("pooled")
        b_ytmp = [S.buf("ytmp%d" % i) for i in range(2)]
        b_prep = S.buf("prep")
        flat_ = lambda L_: [b for l in L_ for b in l]
        arena_bufs = b_act + b_hup + flat_(b_cvt3) + [b_cvt, b_tA, b_tB, b_tC, b_tD] + b_uT + b_yg + b_ptmp + [b_pooled] + b_ytmp + [b_prep]
        b_fence = S.buf("fence")

        def fence():
            S.c("dve", lambda e: e.memset(fence_scr[:, 0:1], 0.0), writes=arena_bufs + [b_fence])

        flat = lambda L_: [b for l in L_ for b in l]
        V = lambda c0, n: vecs[:, c0:c0 + n]

        S.dma("sp", lambda e: e.dma_start(out=vecs[:], in_=vecs_d), writes=[b_vecs])
        S.dma("sp", lambda e: e.dma_start(out=ident[:], in_=ident_d), writes=[b_ident])
        S.dma("sp", lambda e: e.dma_start(out=g3b[:], in_=g3b_d), writes=[b_g3b])
        b_s5p = S.buf("s5p")
        S.dma("sp", lambda e: e.dma_start(out=s5p, in_=s5p_d), writes=[b_prep, b_s5p])
        S.c("dve", lambda e: e.tensor_copy(out=identb[:], in_=ident[:]), reads=[b_ident], writes=[b_identb])
        S.dma("pool", lambda e: e.dma_start(out=w_glu_sb[:], in_=w_glu_d.rearrange("(k p) n -> p k n", p=128)), writes=[b_wglu])
        S.dma("pool", lambda e: e.dma_start(out=pool_w_sb[:], in_=pool_w_d.rearrange("(k p) n -> p k n", p=128)), writes=[b_poolw])

        b_winb = [S.buf("winb%d" % m) for m in range(8)]
        b_wupb = [S.buf("wupb%d" % f) for f in range(NF)]
        b_wdnb = [S.buf("wdnb%d" % m) for m in range(8)]
        for m in range(8):
            S.dma("pool", lambda e, m=m: e.dma_start(out=winb[m], in_=w_in_d[:, m * 128:(m + 1) * 128].rearrange("(k p) n -> p k n", p=128)),
                  writes=[b_winb[m]])
        for f in range(NF):
            S.dma("pool", lambda e, f=f: [e.dma_start(out=wupb[f][:, :, 0:128], in_=w_up_d[:, f * 128:(f + 1) * 128].rearrange("(k p) n -> p k n", p=128)),
                                           e.dma_start(out=wupb[f][:, :, 128:256], in_=w_up_d[:, DFF + f * 128:DFF + (f + 1) * 128].rearrange("(k p) n -> p k n", p=128))],
                  writes=[b_wupb[f]], ndma=2)
        for m in range(8):
            S.dma("pool", lambda e, m=m: e.dma_start(out=wdnb[m], in_=w_down_d[:, m * 128:(m + 1) * 128].rearrange("(k p) n -> p k n", p=128)),
                  writes=[b_wdnb[m]])
        spv = lambda i, n=1: sp_small[:, i:i + n, :]

        def sincos(ang_ap, out_s, out_c, shape, tmp1, tmp2, reads, writes):
            S.c("dve", lambda e: e.tensor_scalar(out=tmp1, in0=ang_ap, scalar1=1.0 / TWO_PI, scalar2=MAGIC, op0=ALU.mult, op1=ALU.add),
                reads=reads, writes=[b_prep])
            S.c("dve", lambda e: e.tensor_scalar(out=tmp1, in0=tmp1, scalar1=MAGIC, scalar2=-TWO_PI, op0=ALU.subtract, op1=ALU.mult),
                reads=[b_prep], writes=[b_prep])
            S.c("dve", lambda e: e.tensor_tensor(out=tmp1, in0=tmp1, in1=ang_ap, op=ALU.add), reads=[b_prep] + reads, writes=[b_prep])
            S.c("dve", lambda e: e.tensor_scalar(out=tmp1, in0=tmp1, scalar1=PI_LO, scalar2=-PI_LO, op0=ALU.min, op1=ALU.max), reads=[b_prep], writes=[b_prep])
            S.c("act", lambda e: e.activation(out=out_s, in_=tmp1, func=AF.Sin), reads=[b_prep], writes=writes)
            S.c("dve", lambda e: e.tensor_scalar(out=tmp2, in0=ang_ap, scalar1=1.0 / TWO_PI, scalar2=0.25, op0=ALU.mult, op1=ALU.add),
                reads=reads, writes=[b_prep])
            S.c("dve", lambda e: e.tensor_scalar(out=tmp2, in0=tmp2, scalar1=MAGIC, scalar2=None, op0=ALU.add), reads=[b_prep], writes=[b_prep])
            S.c("dve", lambda e: e.tensor_scalar(out=tmp2, in0=tmp2, scalar1=MAGIC, scalar2=-TWO_PI, op0=ALU.subtract, op1=ALU.mult),
                reads=[b_prep], writes=[b_prep])
            S.c("dve", lambda e: e.scalar_tensor_tensor(out=tmp2, in0=ang_ap, scalar=0.5 * math.pi, in1=tmp2, op0=ALU.add, op1=ALU.add),
                reads=[b_prep] + reads, writes=[b_prep])
            S.c("dve", lambda e: e.tensor_scalar(out=tmp2, in0=tmp2, scalar1=PI_LO, scalar2=-PI_LO, op0=ALU.min, op1=ALU.max), reads=[b_prep], writes=[b_prep])
            S.c("act", lambda e: e.activation(out=out_c, in_=tmp2, func=AF.Sin), reads=[b_prep], writes=writes)

        if do_s5:
            are = s5p[:, SP_ARE:SP_ARE + 16]
            aim = s5p[:, SP_AIM:SP_AIM + 16]
            ldt = s5p[:, SP_LDT:SP_LDT + 16]
            Bre = s5p[:, SP_BRE:SP_BRE + 512].rearrange("p (q h) -> p q h", q=NQ)
            Bim = s5p[:, SP_BIM:SP_BIM + 512].rearrange("p (q h) -> p q h", q=NQ)
            Cre = s5p[:, SP_CRE:SP_CRE + 512].rearrange("p (q h) -> p q h", q=NQ)
            Cim = s5p[:, SP_CIM:SP_CIM + 512].rearrange("p (q h) -> p q h", q=NQ)
            sp2 = lambda i: sp_small[:, i, :]
            S.c("act", lambda e: e.activation(out=sp2(TMPA), in_=ldt, func=AF.Exp), reads=[b_prep], writes=[b_sps])
            S.c("dve", lambda e: e.tensor_tensor(out=sp2(LR), in0=sp2(TMPA), in1=are, op=ALU.mult), reads=[b_prep, b_sps], writes=[b_sps])
            S.c("dve", lambda e: e.tensor_tensor(out=sp2(LI), in0=sp2(TMPA), in1=aim, op=ALU.mult), reads=[b_prep, b_sps], writes=[b_sps])
            for n in range(9):
                S.c("act", lambda e, n=n: e.activation(out=sp2(PR0 + n), in_=sp2(LR), func=AF.Exp, scale=float(n)), reads=[b_sps], writes=[b_sps])
                S.c("dve", lambda e, n=n: e.tensor_scalar(out=sp2(ANG0 + n), in0=sp2(LI), scalar1=float(n), scalar2=None, op0=ALU.mult),
                    reads=[b_sps], writes=[b_sps])
            S.c("dve", lambda e: e.tensor_scalar(out=sp2(ANG0 + 9), in0=sp2(LI), scalar1=float(BT), scalar2=None, op0=ALU.mult),
                reads=[b_sps], writes=[b_sps])
            S.c("dve", lambda e: e.memset(rdec[:], 0.0), writes=[b_rdec])
            S.c("dve", lambda e: e.tensor_copy(out=rdec[:, :, 1:CH + 1], in_=sp_small[:, PR0 + 8, :].unsqueeze(2).broadcast_to([128, NQ, CH])),
                reads=[b_sps], writes=[b_rdec])
            angv = sp_small[:, ANG0:ANG0 + 10, :]
            sin10 = pang[:, 0:10, 0:16]
            cos10 = pang[:, 0:10, 16:32]
            sincos(angv, sin10, cos10, None, pang2[:, 0:10, 0:16], pang2[:, 0:10, 16:32], [b_sps], [b_prep])
            S.c("dve", lambda e: e.tensor_tensor(out=sp_small[:, PI0:PI0 + 9, :], in0=sp_small[:, PR0:PR0 + 9, :], in1=sin10[:, 0:9, :], op=ALU.mult),
                reads=[b_prep, b_sps], writes=[b_sps])
            S.c("dve", lambda e: e.tensor_tensor(out=sp_small[:, PR0:PR0 + 9, :], in0=sp_small[:, PR0:PR0 + 9, :], in1=cos10[:, 0:9, :], op=ALU.mult),
                reads=[b_prep, b_sps], writes=[b_sps])
            S.c("dve", lambda e: e.tensor_copy(out=sp2(SW), in_=sin10[:, 9, :]), reads=[b_prep], writes=[b_sps])
            S.c("dve", lambda e: e.tensor_copy(out=sp2(CW), in_=cos10[:, 9, :]), reads=[b_prep], writes=[b_sps])
            S.c("dve", lambda e: e.tensor_scalar(out=sp2(TMPB), in0=sp2(LI), scalar1=float(T8), scalar2=None, op0=ALU.mult), reads=[b_sps], writes=[b_sps])
            S.c("dve", lambda e: e.tensor_tensor(out=pang, in0=sp_small[:, TMPB, :].unsqueeze(2).broadcast_to([128, NQ, CH]),
                                                  in1=V(V_IO, 64).unsqueeze(1).broadcast_to([128, NQ, CH]), op=ALU.mult),
                reads=[b_sps, b_vecs, b_prep], writes=[b_prep])
            tmp_a = av(2096, 1024).rearrange("p (q c) -> p q c", q=NQ)
            tmp_b = av(3120, 1024).rearrange("p (q c) -> p q c", q=NQ)
            sincos(pang, tabs[:], tabc[:], None, tmp_a, tmp_b, [b_prep], [b_tab])
            S.c("dve", lambda e: e.tensor_scalar(out=sp2(TMPA), in0=sp2(PR0 + 1), scalar1=-1.0, scalar2=None, op0=ALU.add), reads=[b_sps], writes=[b_sps])
            S.c("dve", lambda e: e.tensor_tensor(out=sp2(TMPB), in0=are, in1=are, op=ALU.mult), reads=[b_prep], writes=[b_sps])
            S.c("dve", lambda e: e.tensor_tensor(out=sp2(TMPC), in0=aim, in1=aim, op=ALU.mult), reads=[b_prep], writes=[b_sps])
            S.c("dve", lambda e: e.tensor_tensor(out=sp2(TMPB), in0=sp2(TMPB), in1=sp2(TMPC), op=ALU.add), reads=[b_sps], writes=[b_sps])
            S.c("dve", lambda e: e.reciprocal(out=sp2(TMPB), in_=sp2(TMPB)), reads=[b_sps], writes=[b_sps])
            S.c("dve", lambda e: e.tensor_tensor(out=sp2(TMPC), in0=sp2(TMPA), in1=are, op=ALU.mult), reads=[b_sps, b_prep], writes=[b_sps])
            S.c("dve", lambda e: e.tensor_tensor(out=sp2(TMPD), in0=sp2(PI0 + 1), in1=aim, op=ALU.mult), reads=[b_sps, b_prep], writes=[b_sps])
            S.c("dve", lambda e: e.tensor_tensor(out=sp2(TMPC), in0=sp2(TMPC), in1=sp2(TMPD), op=ALU.add), reads=[b_sps], writes=[b_sps])
            S.c("dve", lambda e: e.tensor_tensor(out=sp2(TMPC), in0=sp2(TMPC), in1=sp2(TMPB), op=ALU.mult), reads=[b_sps], writes=[b_sps])
            S.c("dve", lambda e: e.tensor_tensor(out=sp2(TMPD), in0=sp2(PI0 + 1), in1=are, op=ALU.mult), reads=[b_sps, b_prep], writes=[b_sps])
            S.c("dve", lambda e: e.tensor_tensor(out=sp2(TMPA), in0=sp2(TMPA), in1=aim, op=ALU.mult), reads=[b_sps, b_prep], writes=[b_sps])
            S.c("dve", lambda e: e.tensor_tensor(out=sp2(TMPD), in0=sp2(TMPD), in1=sp2(TMPA), op=ALU.subtract), reads=[b_sps], writes=[b_sps])
            S.c("dve", lambda e: e.tensor_tensor(out=sp2(TMPD), in0=sp2(TMPD), in1=sp2(TMPB), op=ALU.mult), reads=[b_sps], writes=[b_sps])
            bq = lambda i: sp_small[:, i, :].unsqueeze(2).broadcast_to([128, NQ, 32])

            pt3 = av(6192, 512).rearrange("p (q h) -> p q h", q=NQ)
            pt4 = av(6704, 512).rearrange("p (q h) -> p q h", q=NQ)
            Gnb2 = av(7216, 512, BF16).rearrange("p (r q h) -> p r q h", r=2, q=NQ)
            pts = [pt1, pt2, pt3, pt4]
            b_pt = [S.buf("pt%d" % i) for i in range(4)]
            b_G0, b_Ccb, b_Gn2 = S.buf("G0"), S.buf("Ccb"), [S.buf("Gnb0"), S.buf("Gnb1")]
            arena_bufs.extend(b_pt + [b_G0, b_Ccb] + b_Gn2)
            S.c("dve", lambda e: e.memset(fence_scr[:, 1:2], 0.0), reads=[b_sps], writes=[b_prep] + b_pt + [b_G0, b_Ccb] + b_Gn2)

            def cmul(out_r, out_i, ar, ai, br_, bi_, in_bufs, out_bufs, neg_i=False):
                t1, t2, t3, t4 = pts
                S.c("dve", lambda e: e.tensor_tensor(out=t1, in0=ar, in1=br_, op=ALU.mult), reads=in_bufs, writes=[b_pt[0]])
                S.c("dve", lambda e: e.tensor_tensor(out=t2, in0=ai, in1=bi_, op=ALU.mult), reads=in_bufs, writes=[b_pt[1]])
                S.c("dve", lambda e: e.tensor_tensor(out=t3, in0=ar, in1=bi_, op=ALU.mult), reads=in_bufs, writes=[b_pt[2]])
                S.c("dve", lambda e: e.tensor_tensor(out=t4, in0=ai, in1=br_, op=ALU.mult), reads=in_bufs, writes=[b_pt[3]])
                S.c("dve", lambda e: e.tensor_tensor(out=out_r, in0=t1, in1=t2, op=ALU.subtract), reads=[b_pt[0], b_pt[1]], writes=out_bufs)
                if neg_i:
                    S.c("dve", lambda e: e.scalar_tensor_tensor(out=out_i, in0=t3, scalar=-1.0, in1=t4, op0=ALU.mult, op1=ALU.subtract),
                        reads=[b_pt[2], b_pt[3]], writes=out_bufs)
                else:
                    S.c("dve", lambda e: e.tensor_tensor(out=out_i, in0=t3, in1=t4, op=ALU.add), reads=[b_pt[2], b_pt[3]], writes=out_bufs)

            cmul(G0r, G0i, bq(TMPC), bq(TMPD), Bre, Bim, [b_sps, b_s5p], [b_G0])
            S.c("dve", lambda e: e.tensor_copy(out=Ccb[:, 0], in_=Cre), reads=[b_s5p], writes=[b_Ccb])
            S.c("dve", lambda e: e.tensor_scalar(out=Ccb[:, 1], in0=Cim, scalar1=-1.0, scalar2=None, op0=ALU.mult), reads=[b_s5p], writes=[b_Ccb])
            S.c("dve", lambda e: e.memset(Kblk[:], 0.0), writes=[b_Kblk])
            for n in range(T8):
                gb, bg = (Gnb, b_Gn2[0]) if n % 2 == 0 else (Gnb2, b_Gn2[1])
                cmul(gb[:, 0], gb[:, 1], bq(PR0 + n), bq(PI0 + n), G0r, G0i, [b_sps, b_G0], [bg])
                def tr_fn(e, gb=gb):
                    last = None
                    for ri in range(2):
                        for q in range(NQ):
                            q4, kc = q // 4, q % 4
                            col = (kc * 2 + ri) * 128
                            last = e.matmul(PSn[0][32 * q4:32 * q4 + 32, col:col + 128], lhsT=gb[:, ri, q, :], rhs=identb[:],
                                            start=True, stop=True, tile_position=(0, 32 * q4))
                    return last
                S.c("pe", tr_fn, reads=[bg, b_identb], writes=[bank[0][0], bank[0][1]])
                S.c("act", lambda e, n=n: e.activation(out=Bst[:, :, T8 - 1 - n, :, :],
                                                        in_=PSn[0][:, :].rearrange("p (k r m) -> p k r m", k=4, r=2), func=AF.Copy),
                    reads=[bank[0][0], bank[0][1]], writes=[b_Bst])
                def kb_fn(e, gb=gb):
                    last = None
                    for q in range(NQ):
                        q4, kc = q // 4, q % 4
                        o = PSn[1][32 * q4:32 * q4 + 32, kc * 128 + 32 * q4:kc * 128 + 32 * q4 + 32]
                        e.matmul(o, lhsT=gb[:, 0, q, :], rhs=Ccb[:, 0, q, :], start=True, stop=False, tile_position=(0, 32 * q4))
                        last = e.matmul(o, lhsT=gb[:, 1, q, :], rhs=Ccb[:, 1, q, :], start=False, stop=True, tile_position=(0, 32 * q4))
                    return last
                S.c("pe", kb_fn, reads=[bg, b_Ccb], writes=[bank[1][0]])
                for q4 in range(4):
                    S.c("act", lambda e, n=n, q4=q4: e.activation(
                        out=Kblk[32 * q4:32 * q4 + 32, :, n, 32 * q4:32 * q4 + 32],
                        in_=PSn[1][32 * q4:32 * q4 + 32, 0:512].rearrange("p (k m) -> p k m", k=4)[:, :, 32 * q4:32 * q4 + 32], func=AF.Copy),
                        reads=[bank[1][0]], writes=[b_Kblk])
            for n in range(T8 + 1):
                cmul(Cst[:, :, n, 0, :], Cst[:, :, n, 1, :], Cre, Cim, bq(PR0 + n), bq(PI0 + n), [b_sps, b_s5p], [b_Cst], neg_i=True)
            S.c("dve", lambda e: e.memset(sp_small[:, GCR:GCR + 2, :], 0.0), writes=[b_gc])
            S.c("dve", lambda e: e.memset(Hb[:], 0.0), writes=[b_Hb, b_Hb0])
        S.c("dve", lambda e: e.memset(hhalo[:], 0.0), writes=[b_hhalo])
        S.c("dve", lambda e: e.memset(v_sb[:], 0.0), writes=b_v)
        fence()
        b_woutb = S.buf("woutb")
        S.dma("pool", lambda e: e.dma_start(out=woutb, in_=w_out_d.rearrange("(k p) n -> p k n", p=128)), writes=[b_woutb])
        arena_bufs.append(b_wout)

        def load_wout():
            S.dma("sp", lambda e: e.dma_start(out=w_out_sb, in_=woutb), reads=[b_woutb], writes=[b_tA, b_tB, b_tC, b_tD, b_wout])
        cnt = {"win": 0, "wup": 0, "wdn": 0, "bk": 0}

        def rms_stats(nt, xt_aps, xbufs):
            S.c("dve", lambda e: e.memset(stat[:, 0:8], 0.0), writes=[b_stat])
            for i, xa in enumerate(xt_aps):
                np_ = xa.shape[0]
                S.c("act", lambda e, i=i, xa=xa, np_=np_: e.activation(out=scr[0:np_, 1, :], in_=xa, func=AF.Square, accum_out=stat[0:np_, i:i + 1]),
                    reads=[xbufs[i], b_stat], writes=[b_scr[1], b_stat])
            S.c("dve", lambda e: e.tensor_scalar(out=stat[:, 0:nt], in0=stat[:, 0:nt], scalar1=1.0 / D, scalar2=EPS, op0=ALU.mult, op1=ALU.add),
                reads=[b_stat], writes=[b_stat])
            S.c("act", lambda e: e.activation(out=stat[:, 0:nt], in_=stat[:, 0:nt], func=AF.Sqrt), reads=[b_stat], writes=[b_stat])
            S.c("dve", lambda e: e.reciprocal(out=stat[:, 8:8 + nt], in_=stat[:, 0:nt]), reads=[b_stat], writes=[b_stat])

        def norm_to_hT(tiles, gcol, dst=None, dbufs=None):
            rms_stats(len(tiles), [t_[0] for t_ in tiles], [t_[1] for t_ in tiles])
            for i, (xa, xb, c0, np_) in enumerate(tiles):
                s = i % 2
                S.c("dve", lambda e, xa=xa, i=i, np_=np_, s=s: e.tensor_scalar(out=scr[0:np_, s, :], in0=xa, scalar1=stat[0:np_, 8 + i:9 + i],
                                                                         scalar2=None, op0=ALU.mult),
                    reads=[xb, b_stat], writes=[b_scr[s]])

                def tr(e, np_=np_, s=s):
                    last = None
                    for k in range(8):
                        last = e.transpose(out=PA[:, k * 128:k * 128 + np_], in_=scr[0:np_, s, k * 128:(k + 1) * 128], identity=ident[0:np_, 0:np_])
                    return last
                S.c("pe", tr, reads=[b_scr[s], b_ident], writes=[bank[0][0], bank[0][1]])
                t_idx = c0 // 128
                dst_ = hT if dst is None else dst
                S.c("dve", lambda e, np_=np_, c0=c0, dst_=dst_: e.tensor_tensor(out=dst_[:, :, c0:c0 + np_],
                                                                 in0=PA[:, :].rearrange("p (k n) -> p k n", k=8)[:, :, 0:np_],
                                                                 in1=V(gcol, 8).unsqueeze(2).broadcast_to([128, 8, np_]), op=ALU.mult),
                    reads=[bank[0][0], bank[0][1], b_vecs], writes=([b_h[k][t_idx] for k in range(8)] if dbufs is None else dbufs))

        def load_win(m):
            slot = cnt["win"] % NWI
            cnt["win"] += 1
            S.dma("sp", lambda e: e.dma_start(out=w_in_ring[slot][:], in_=winb[m]), reads=[b_winb[m]], writes=[b_win[slot]])
            return slot

        def load_wup(f):
            slot = cnt["wup"] % NWU
            cnt["wup"] += 1
            S.dma("sp", lambda e: e.dma_start(out=w_up_ring[slot][:], in_=wupb[f]), reads=[b_wupb[f]], writes=[b_wup[slot]])
            return slot

        def load_wdn(m):
            slot = cnt["wdn"] % NWD
            cnt["wdn"] += 1
            S.dma("sp", lambda e: e.dma_start(out=w_dn_ring[slot][:], in_=wdnb[m]), reads=[b_wdnb[m]], writes=[b_wdn[slot]])
            return slot

        def next_bank():
            i = cnt["bk"] % 2
            cnt["bk"] += 1
            return pbank(1, i), bank[1][i]

        sel_d = din("sel", [128, 32])
        selt = sbt("selt", [128, 32], F32)
        xst = sbt("xst", [128, 2, 128], F32)
        xs_sb = sbt("xs_sb", [128, D], F32)
        hTs = sbt("hTs", [128, 8, NS], BF16)
        HbS = sbt("HbS", [128, NQ, 2, NS], BF16)
        cbufT = sbt("cbufT", [128, 2 * NF, 2 * NS], F32)
        hupS = sbt("hupS", [128, 2 * NF, NS], F32)
        actS = sbt("actS", [128, NF, NS], BF16)
        cvS = sbt("cvS", [128, 3, 2, NS], F32)
        b_sel, b_xst = S.buf("sel"), [S.buf("xst0"), S.buf("xst1")]
        b_xs, b_hs, b_HbS = S.buf("xs"), [S.buf("hs%d" % k) for k in range(8)], S.buf("HbS")
        b_cst, b_cbufT, b_hupS, b_actS = S.buf("cst"), S.buf("cbufT"), S.buf("hupS"), S.buf("actS")
        b_cvS = [[S.buf("cvS%d_%d" % (i, g)) for g in range(2)] for i in range(3)]
        b_pss = [S.buf("pss%d" % i) for i in range(4)]
        b_psd = [S.buf("psd%d" % i) for i in range(2)]
        b_pst = [S.buf("pst%d" % i) for i in range(2)]
        arena_bufs.append(b_cst)
        N = NS
        def sample_phase_A():
            S.dma("sp", lambda e: e.dma_start(out=selt[:], in_=sel_d), writes=[b_sel])
            S.dma("sp", lambda e: e.dma_start(out=xs_sb[0:N, :], in_=xs_d), writes=[b_xs])
            S.dma("sp", lambda e: e.dma_start(out=spool_d[:, 0:14, :], in_=pst_d[:, 1:15, :]), out=True)
            S.dma("sp", lambda e: e.dma_start(out=sconv_d[:, 0, :], in_=cst_d[:, 1, :]), out=True)
            norm_to_hT([(xs_sb[0:N, :], b_xs, 0, N)], V_G1, dst=hTs, dbufs=b_hs)
            hcol = b_hs
            for m in range(8):
                slot = load_win(m)
                po, pbuf = next_bank()

                def proj(e, slot=slot, po=po):
                    last = None
                    for k in range(8):
                        last = e.matmul(po[:, 0:N], lhsT=w_in_ring[slot][:, k, :], rhs=hTs[:, k, :], start=(k == 0), stop=(k == 7))
                    return last
                S.c("pe", proj, reads=[b_win[slot]] + hcol, writes=[pbuf])
                if m < 4:
                    S.c("act", lambda e, m=m, po=po: e.activation(out=uT[:, m, 0:N], in_=po[:, 0:N], func=AF.Copy), reads=[pbuf], writes=[b_uT[m]])
                else:
                    S.c("act", lambda e, m=m, po=po: e.activation(out=v_sb[:, m - 4, 16:16 + N], in_=po[:, 0:N], func=AF.Copy), reads=[pbuf], writes=[b_v[m - 4]])
            if do_s5:
                h0r_tm = scr[0:N, :, :].rearrange("p s n -> p (s n)")
                h0i_tm = x_sb[0:N, 2:4, :].rearrange("p s n -> p (s n)")
                S.dma("sp", lambda e: e.dma_start(out=h0r_tm, in_=s5re_d), writes=b_scr)
                S.dma("sp", lambda e: e.dma_start(out=h0i_tm, in_=s5im_d), writes=[b_x[2], b_x[3]])

                def trh(e):
                    last = None
                    for ri, src in ((0, h0r_tm), (1, h0i_tm)):
                        for q in range(NQ):
                            qo = (q % 4) * 4 + q // 4
                            last = e.transpose(out=PB[:, ri * 256 + q * N:ri * 256 + (q + 1) * N], in_=src[:, qo * 128:(qo + 1) * 128], identity=ident[0:N, 0:N])
                    return last
                S.c("pe", trh, reads=b_scr + [b_x[2], b_x[3], b_ident], writes=[bank[1][0]])
                h0r, h0i, hnr, hni = tA[:, :, 0:N], tB[:, :, 0:N], tC[:, :, 0:N], tD[:, :, 0:N]
                S.c("act", lambda e: e.activation(out=h0r, in_=PB[:, 0:256].rearrange("p (q n) -> p q n", q=NQ), func=AF.Copy), reads=[bank[1][0]], writes=[b_tA])
                S.c("act", lambda e: e.activation(out=h0i, in_=PB[:, 256:512].rearrange("p (q n) -> p q n", q=NQ), func=AF.Copy), reads=[bank[1][0]], writes=[b_tB])

                def bu(e):
                    last = None
                    for ri in range(2):
                        for q in range(NQ):
                            q4, kc = q // 4, q % 4
                            c0_ = q4 * 512 + (kc * 2 + ri) * N
                            last = e.matmul(PCD[:, c0_:c0_ + N], lhsT=Bst[32 * q4:32 * q4 + 32, kc, T8 - 1, ri, :],
                                            rhs=uT[32 * q4:32 * q4 + 32, kc, 0:N], start=True, stop=True, tile_position=(32 * q4, 0))
                    return last
                S.c("pe", bu, reads=[b_Bst] + b_uT, writes=[bank[2][0], bank[2][1], bank[3][0], bank[3][1]])
                al = sp_small[:, PR0 + 1, :].unsqueeze(2).broadcast_to([128, NQ, N])
                be = sp_small[:, PI0 + 1, :].unsqueeze(2).broadcast_to([128, NQ, N])
                t1 = ytmp[:, 0, 0:256].rearrange("p (q n) -> p q n", q=NQ)
                t2 = ytmp[:, 1, 0:256].rearrange("p (q n) -> p q n", q=NQ)
                TT = lambda eng, o, a, b_, op, rd, wr: S.c(eng, lambda e: e.tensor_tensor(out=o, in0=a, in1=b_, op=op), reads=rd, writes=wr)
                PSv = PCD[:, :].rearrange("p (a x) -> p a x", a=4)[:, :, 0:8 * N].rearrange("p a (k r n) -> p a k r n", k=4, r=2)
                Sre, Sim = PSv[:, :, :, 0, :], PSv[:, :, :, 1, :]
                v4 = lambda a: a.rearrange("p (a k) c -> p a k c", a=4)
                TT("dve", t1, h0r, al, ALU.mult, [b_tA, b_sps], [b_ytmp[0]])
                TT("dve", t2, h0i, be, ALU.mult, [b_tB, b_sps], [b_ytmp[1]])
                TT("dve", hnr, t1, t2, ALU.subtract, b_ytmp, [b_tC])
                TT("dve", v4(hnr), v4(hnr), Sre, ALU.add, [b_tC, bank[2][0], bank[2][1], bank[3][0], bank[3][1]], [b_tC])
                TT("dve", t1, h0i, al, ALU.mult, [b_tB, b_sps], [b_ytmp[0]])
                TT("dve", t2, h0r, be, ALU.mult, [b_tA, b_sps], [b_ytmp[1]])
                TT("dve", hni, t1, t2, ALU.add, b_ytmp, [b_tD])
                TT("dve", v4(hni), v4(hni), Sim, ALU.add, [b_tD, bank[2][0], bank[2][1], bank[3][0], bank[3][1]], [b_tD])
                S.c("dve", lambda e: e.tensor_copy(out=HbS[:, :, 0, :], in_=hnr), reads=[b_tC], writes=[b_HbS])
                S.c("dve", lambda e: e.tensor_copy(out=HbS[:, :, 1, :], in_=hni), reads=[b_tD], writes=[b_HbS])
                for ri, (src, dst_tm, dbufs, dd) in enumerate(((hnr, h0r_tm, b_scr, sre_d), (hni, h0i_tm, [b_x[2], b_x[3]], sim_d))):
                    for half in range(2):
                        def trb(e, src=src, half=half):
                            last = None
                            for qq in range(8):
                                qo = half * 8 + qq
                                qp = (qo % 4) * 4 + qo // 4
                                last = e.transpose(out=PA[0:N, qq * 128:(qq + 1) * 128], in_=src[:, qp, :], identity=ident[:])
                            return last
                        S.c("pe", trb, reads=[b_tC, b_tD, b_ident], writes=[bank[0][0], bank[0][1]])
                        S.c("act", lambda e, dst_tm=dst_tm, half=half: e.activation(out=dst_tm[:, half * 1024:(half + 1) * 1024], in_=PA[0:N, :], func=AF.Copy),
                            reads=[bank[0][0], bank[0][1]], writes=dbufs)
                    S.dma("sp", lambda e, dd=dd, dst_tm=dst_tm: e.dma_start(out=dd, in_=dst_tm), reads=dbufs, out=True)
                for kc in range(4):
                    Y2 = pbank(3, 1)

                    def ys(e, kc=kc, Y2=Y2):
                        last = None
                        for q4 in range(4):
                            q = q4 * 4 + kc
                            o = Y2[32 * q4:32 * q4 + 32, 0:N]
                            e.matmul(o, lhsT=Cst[:, q, 0, 0, :], rhs=HbS[:, q, 0, :], start=True, stop=False, tile_position=(0, 32 * q4))
                            last = e.matmul(o, lhsT=Cst[:, q, 0, 1, :], rhs=HbS[:, q, 1, :], start=False, stop=True, tile_position=(0, 32 * q4))
                        return last
                    S.c("pe", ys, reads=[b_Cst, b_HbS], writes=[bank[3][1]])
                    S.c("dve", lambda e, kc=kc, Y2=Y2: e.scalar_tensor_tensor(out=ytmp[:, 0, 0:N], in0=uT[:, kc, 0:N], scalar=vecs[:, V_DS + kc:V_DS + kc + 1],
                                                                          in1=Y2[:, 0:N], op0=ALU.mult, op1=ALU.add),
                        reads=[bank[3][1], b_uT[kc], b_vecs], writes=[b_ytmp[0]])
                    S.c("act", lambda e, kc=kc: e.activation(out=yg[:, kc, 0:N], in_=ytmp[:, 0, 0:N], func=AF.Gelu), reads=[b_ytmp[0]], writes=[b_yg[kc]])
                for m in range(4):
                    po, pbuf = next_bank()

                    def glu(e, m=m, po=po):
                        last = None
                        for k in range(4):
                            last = e.matmul(po[:, 0:N], lhsT=w_glu_sb[:, k, m * 128:(m + 1) * 128], rhs=yg[:, k, 0:N], start=(k == 0), stop=(k == 3))
                        return last
                    S.c("pe", glu, reads=[b_wglu] + b_yg, writes=[pbuf])
                    S.c("act", lambda e, po=po: e.activation(out=ytmp[:, 1, 0:N], in_=po[:, 0:N], func=AF.Sigmoid), reads=[pbuf], writes=[b_ytmp[1]])
                    S.c("dve", lambda e, m=m: e.tensor_tensor(out=hTs[:, m, :], in0=yg[:, m, 0:N], in1=ytmp[:, 1, 0:N], op=ALU.mult),
                        reads=[b_ytmp[1], b_yg[m]], writes=[b_hs[m]])
            else:
                for m in range(4):
                    S.c("dve", lambda e, m=m: e.memset(hTs[:, m, :], 0.0), writes=[b_hs[m]])
            for g, w in enumerate((2, 4, 8, 16)):
                po, pbuf = next_bank()
                for half in range(2):
                    s = half
                    S.dma("sp", lambda e, g=g, half=half, s=s: e.dma_start(
                        out=xst[0:120, s, :], in_=pst_d[half * 8:(half + 1) * 8, :, g * 128:(g + 1) * 128].rearrange("t r c -> (t r) c")), writes=[b_xst[s]])
                    S.c("pe", lambda e, g=g, half=half, s=s, po=po: e.matmul(po[:, half * 8:(half + 1) * 8], lhsT=xst[0:120, s, :], rhs=selt[0:120, g * 8:(g + 1) * 8],
                                                                        start=True, stop=True), reads=[b_xst[s], b_sel], writes=[pbuf])
                S.c("dve", lambda e, g=g, w=w: e.tensor_scalar(out=ptmp[:, 0, 0:N], in0=v_sb[:, g, 16:16 + N], scalar1=1.0 / w - 1.0, scalar2=None, op0=ALU.mult),
                    reads=[b_v[g]], writes=[b_ptmp[0]])
                S.c("dve", lambda e, w=w, po=po: e.scalar_tensor_tensor(out=pooled[:, 0:N], in0=po[:, 0:N], scalar=1.0 / w, in1=ptmp[:, 0, 0:N], op0=ALU.mult, op1=ALU.add),
                    reads=[pbuf, b_ptmp[0]], writes=[b_pooled])
                po2, pbuf2 = next_bank()
                S.c("pe", lambda e, g=g, po2=po2: e.matmul(po2[:, 0:N], lhsT=pool_w_sb[:, g, :], rhs=pooled[:, 0:N], start=True, stop=True),
                    reads=[b_poolw, b_pooled], writes=[pbuf2])
                S.c("act", lambda e, g=g, po2=po2: e.activation(out=hTs[:, 4 + g, :], in_=po2[:, 0:N], func=AF.Copy, scale=vecs[:, V_PS + g:V_PS + g + 1]),
                    reads=[pbuf2, b_vecs], writes=[b_hs[4 + g]])

            def trv(e):
                last = None
                for g in range(4):
                    last = e.transpose(out=PA[0:N, g * 128:(g + 1) * 128], in_=v_sb[:, g, 16:16 + N], identity=ident[:])
                return last
            S.c("pe", trv, reads=b_v + [b_ident], writes=[bank[0][0]])
            S.c("act", lambda e: e.activation(out=x_sb[0:N, 1, 0:512], in_=PA[0:N, 0:512], func=AF.Copy), reads=[bank[0][0]], writes=[b_x[1]])
            S.dma("sp", lambda e: e.dma_start(out=spool_d[:, 14, :], in_=x_sb[0:N, 1, 0:512]), reads=[b_x[1]], out=True)

            load_wout()

            def oproj_s(e):
                last = None
                for h in range(2):
                    for k in range(8):
                        last = e.matmul(pbank(2, h)[0:N, :], lhsT=hTs[:, k, :], rhs=w_out_sb[:, k, h * 512:(h + 1) * 512], start=(k == 0), stop=(k == 7))
                return last
            S.c("pe", oproj_s, reads=[b_wout] + hcol, writes=[bank[2][0], bank[2][1]])
            S.c("dve", lambda e: e.tensor_tensor(out=xs_sb[0:N, :], in0=xs_sb[0:N, :], in1=PC[0:N, :], op=ALU.add),
                reads=[b_xs, bank[2][0], bank[2][1]], writes=[b_xs])
            norm_to_hT([(xs_sb[0:N, :], b_xs, 0, N)], V_G2, dst=hTs, dbufs=b_hs)
            fence()
            cst_tm = av(0, 5632)
            S.dma("sp", lambda e: e.dma_start(out=cst_tm[0:32, :], in_=cst_d.rearrange("t r f -> (t r) f")), reads=[b_fence], writes=[b_cst])
            for rnd in range(3):
                c0 = rnd * 16
                ncx = min(16, 2 * NF - c0)

                def trc(e, c0=c0, ncx=ncx):
                    last = None
                    for ci in range(ncx):
                        last = e.transpose(out=PA[:, ci * 32:(ci + 1) * 32], in_=cst_tm[0:32, (c0 + ci) * 128:(c0 + ci + 1) * 128], identity=ident[0:32, 0:32])
                    return last
                S.c("pe", trc, reads=[b_cst, b_ident], writes=[bank[0][0]])
                S.c("act", lambda e, c0=c0, ncx=ncx: e.activation(out=cbufT[:, c0:c0 + ncx, :], in_=PA[:, 0:ncx * 32].rearrange("p (c n) -> p c n", c=ncx), func=AF.Copy),
                    reads=[bank[0][0]], writes=[b_cbufT])
            fence()

        def sample_up(f, slot):
            pr = f % 2
            cs = f % 3
            psu = [PA[:, pr * 512 + gv * NS:pr * 512 + (gv + 1) * NS] for gv in range(2)]

            def up(e):
                last = None
                for gv in range(2):
                    for k in range(8):
                        last = e.matmul(psu[gv], lhsT=w_up_ring[slot][:, k, gv * 128:(gv + 1) * 128], rhs=hTs[:, k, :], start=(k == 0), stop=(k == 7))
                return last
            S.c("pe", up, reads=[b_wup[slot]] + b_hs, writes=[bank[0][pr]])
            for gv in range(2):
                ch = f + gv * NF
                cw = lambda k, ch=ch: vecs[:, V_CW + ch * 3 + k:V_CW + ch * 3 + k + 1]
                cb2 = cbufT[:, ch, :].rearrange("p (t r) -> p t r", r=2)
                cc = cvS[:, cs, gv, :]
                bc = b_cvS[cs][gv]
                S.c("act", lambda e, gv=gv, ch=ch: e.activation(out=hupS[:, ch, :], in_=psu[gv], func=AF.Copy), reads=[bank[0][pr]], writes=[b_hupS])
                S.c("act", lambda e, gv=gv, ch=ch, cw=cw, cc=cc: e.activation(out=cc, in_=psu[gv], func=AF.Identity, scale=cw(2),
                                                                       bias=vecs[:, V_CB + ch:V_CB + ch + 1]), reads=[bank[0][pr], b_vecs], writes=[bc])
                S.c("dve", lambda e, cw=cw, cb2=cb2, cc=cc: e.scalar_tensor_tensor(out=cc, in0=cb2[:, :, 1], scalar=cw(1), in1=cc, op0=ALU.mult, op1=ALU.add),
                    reads=[b_cbufT, b_vecs, bc], writes=[bc])
                S.c("dve", lambda e, cw=cw, cb2=cb2, cc=cc: e.scalar_tensor_tensor(out=cc, in0=cb2[:, :, 0], scalar=cw(0), in1=cc, op0=ALU.mult, op1=ALU.add),
                    reads=[b_cbufT, b_vecs, bc], writes=[bc])
            S.c("act", lambda e: e.activation(out=cvS[:, cs, 0, :], in_=cvS[:, cs, 0, :], func=AF.Gelu), reads=[b_cvS[cs][0]], writes=[b_cvS[cs][0]])
            S.c("dve", lambda e: e.tensor_tensor(out=actS[:, f, :], in0=cvS[:, cs, 0, :], in1=cvS[:, cs, 1, :], op=ALU.mult), reads=b_cvS[cs], writes=[b_actS])

        def sample_down(m, slot):
            s_ = m % 2
            psd = pbank(2 + s_, 0)[:, 0:NS]

            def down(e):
                last = None
                for k in range(NF):
                    last = e.matmul(psd, lhsT=w_dn_ring[slot][:, k, :], rhs=actS[:, k, :], start=(k == 0), stop=(k == NF - 1))
                return last
            S.c("pe", down, reads=[b_wdn[slot], b_actS], writes=[bank[2 + s_][0]])
            S.c("act", lambda e: e.activation(out=xst[:, s_, 0:N], in_=psd, func=AF.Copy), reads=[bank[2 + s_][0]], writes=[b_xst[s_]])

        def sample_btr(m):
            s_ = m % 2
            pst_ = pbank(2 + s_, 1)[0:N, 0:128]
            S.c("pe", lambda e: e.transpose(out=pst_, in_=xst[:, s_, 0:N], identity=ident[:]),
                reads=[b_xst[s_], b_ident], writes=[bank[2 + s_][1]])
            S.c("dve", lambda e: e.tensor_tensor(out=xs_sb[0:N, m * 128:(m + 1) * 128], in0=xs_sb[0:N, m * 128:(m + 1) * 128], in1=pst_, op=ALU.add),
                reads=[b_xs, bank[2 + s_][1]], writes=[b_xs])

        def sample_phase_C():
            for rnd in range(6):
                c0 = rnd * 8
                ncx = min(8, 2 * NF - c0)
                s = rnd % 2

                def trhup(e, c0=c0, ncx=ncx):
                    last = None
                    for ci in range(ncx):
                        last = e.transpose(out=PA[0:N, ci * 128:(ci + 1) * 128], in_=hupS[:, c0 + ci, :], identity=ident[:])
                    return last
                S.c("pe", trhup, reads=[b_hupS, b_ident], writes=[bank[0][0], bank[0][1]])
                S.c("act", lambda e, s=s, ncx=ncx: e.activation(out=scr[0:N, s, 0:ncx * 128], in_=PA[0:N, 0:ncx * 128], func=AF.Copy),
                    reads=[bank[0][0], bank[0][1]], writes=[b_scr[s]])
                S.dma("sp", lambda e, s=s, c0=c0, ncx=ncx: e.dma_start(out=sconv_d[:, 1, c0 * 128:(c0 + ncx) * 128], in_=scr[0:N, s, 0:ncx * 128]),
                      reads=[b_scr[s]], out=True)
            rms_stats(1, [xs_sb[0:N, :]], [b_xs])
            S.c("dve", lambda e: e.scalar_tensor_tensor(out=scr[0:N, 0, :], in0=xs_sb[0:N, :], scalar=stat[0:N, 8:9], in1=g3b[0:N, :],
                                                         op0=ALU.mult, op1=ALU.mult), reads=[b_xs, b_stat, b_g3b], writes=[b_scr[0]])
            S.dma("sp", lambda e: e.dma_start(out=ys_d, in_=scr[0:N, 0, :]), reads=[b_scr[0]], out=True)

        def block_early(blk):
            last_blk = (blk == NBLK - 1)
            if blk > 0:
                S.c("pool", lambda e: e.tensor_copy(out=v_sb[:, :, 1:16], in_=v_sb[:, :, BT + 1:BT + 16]), reads=b_v, writes=b_v)
            all_h = flat(b_h)
            for m in range(8):
                slot = load_win(m)
                po, pbuf = next_bank()

                def proj(e, slot=slot, po=po):
                    last = None
                    for k in range(8):
                        last = e.matmul(po, lhsT=w_in_ring[slot][:, k, :], rhs=hT[:, k, :], start=(k == 0), stop=(k == 7))
                    return last
                S.c("pe", proj, reads=[b_win[slot]] + all_h, writes=[pbuf])
                if m < 4:
                    S.c("act", lambda e, m=m, po=po: e.activation(out=uT[:, m, :], in_=po, func=AF.Copy), reads=[pbuf], writes=[b_uT[m]])
                else:
                    S.c("act", lambda e, m=m, po=po: e.activation(out=v_sb[:, m - 4, 16:16 + BT], in_=po, func=AF.Copy), reads=[pbuf], writes=[b_v[m - 4]])

            for g, w in enumerate((2, 4, 8, 16)):
                vg = v_sb[:, g, :]
                nsteps = g + 1
                cur = vg
                curb = [b_v[g]]
                sh = 1
                for si in range(nsteps):
                    dst = ptmp[:, si % 2, :]
                    lo = 2 * sh
                    S.c("dve", lambda e, dst=dst, cur=cur, sh=sh: e.tensor_tensor(out=dst[:, 2 * sh - 1:16 + BT], in0=cur[:, 2 * sh - 1:16 + BT],
                                                                          in1=cur[:, sh - 1:16 + BT - sh], op=ALU.add),
                        reads=curb, writes=[b_ptmp[si % 2]])
                    cur = dst
                    curb = [b_ptmp[si % 2]]
                    sh *= 2
                S.c("dve", lambda e, cur=cur, w=w, vg=vg: e.scalar_tensor_tensor(out=pooled, in0=cur[:, 16:16 + BT], scalar=1.0 / w, in1=vg[:, 16:16 + BT],
                                                                             op0=ALU.mult, op1=ALU.subtract),
                    reads=curb + [b_v[g]], writes=[b_pooled])
                if blk == 0:
                    o2 = ptmp[:, (nsteps) % 2, :]
                    S.c("dve", lambda e, cur=cur, w=w, o2=o2: e.tensor_tensor(out=o2[:, 0:w - 1], in0=cur[:, 16:16 + w - 1], in1=V(V_IC, w - 1), op=ALU.mult),
                        reads=curb + [b_vecs], writes=[b_ptmp[nsteps % 2]])
                    S.c("dve", lambda e, w=w, o2=o2, vg=vg: e.tensor_tensor(out=pooled[:, 0:w - 1], in0=o2[:, 0:w - 1], in1=vg[:, 16:16 + w - 1], op=ALU.subtract),
                        reads=[b_ptmp[nsteps % 2], b_v[g]], writes=[b_pooled])
                po, pbuf = next_bank()
                S.c("pe", lambda e, g=g, po=po: e.matmul(po, lhsT=pool_w_sb[:, g, :], rhs=pooled, start=True, stop=True),
                    reads=[b_poolw, b_pooled], writes=[pbuf])
                S.c("act", lambda e, g=g, po=po: e.activation(out=hT[:, 4 + g, :], in_=po, func=AF.Copy, scale=vecs[:, V_PS + g:V_PS + g + 1]),
                    reads=[pbuf, b_vecs], writes=b_h[4 + g])
            if last_blk:
                S.dma("sp", lambda e: [e.dma_start(out=ppool_d[:, g * 128:(g + 1) * 128].rearrange("t c -> c t"), in_=v_sb[:, g, BT + 1:BT + 16],
                                                    allow_slow_non_contiguous=True) for g in range(4)], reads=b_v, out=True, ndma=4)


        for blk in range(nblk):
            r0 = blk * BT
            last_blk = (blk == NBLK - 1)
            xb, bx = (x_sb, b_x) if blk % 2 == 0 else (x_sb2, b_x2)
            xn_, bxn = (x_sb2, b_x2) if blk % 2 == 0 else (x_sb, b_x)
            mrg = do_sample and blk == nblk - 1
            if mrg:
                sample_phase_A()
            if blk == 0:
                for t in range(4):
                    S.dma("sp", lambda e, t=t, r0=r0, xb=xb: e.dma_start(out=xb[:, t, :], in_=x_d[r0 + t * 128:r0 + (t + 1) * 128, :]), writes=[bx[t]])
            if blk == 0:
                norm_to_hT([(xb[:, t, :], bx[t], t * 128, 128) for t in range(4)], V_G1)
            if blk == 0 or mrg:
                block_early(blk)
            if do_s5 and stage >= 1:
                PCDv = PCD[:, :].rearrange("p (a k r c) -> p a k r c", a=4, k=4, r=2)
                PCv, PDv = PCDv[:, :, :, 0, :], PCDv[:, :, :, 1, :]
                v4 = lambda a: a.rearrange("p (a k) c -> p a k c", a=4)

                def lvl0(e):
                    last = None
                    for ri in range(2):
                        for kc in range(4):
                            for j in range(T8):
                                for q4 in range(4):
                                    c0_ = q4 * 512 + (kc * 2 + ri) * CH
                                    last = e.matmul(PCD[:, c0_:c0_ + CH], lhsT=Bst[32 * q4:32 * q4 + 32, kc, j, ri, :],
                                                    rhs=uT[32 * q4:32 * q4 + 32, kc, j:BT:T8], start=(j == 0), stop=(j == T8 - 1),
                                                    tile_position=(32 * q4, 0))
                    return last
                S.c("pe", lvl0, reads=[b_Bst] + b_uT, writes=[bank[2][0], bank[2][1], bank[3][0], bank[3][1]])
                if stage >= 2:
                    bC, bD = [bank[2][0], bank[2][1]], [bank[3][0], bank[3][1]]
                    A1, B1, C1, D1 = tA[:, :, 1:CH + 1], tB[:, :, 1:CH + 1], tC[:, :, 1:CH + 1], tD[:, :, 1:CH + 1]
                    TT = lambda eng, o, a, b_, op, rd, wr: S.c(eng, lambda e: e.tensor_tensor(out=o, in0=a, in1=b_, op=op), reads=rd, writes=wr)
                    TT("dve", v4(A1), PCv, v4(tabc[:]), ALU.mult, bC + bD + [b_tab], [b_tA])
                    TT("dve", v4(B1), PDv, v4(tabs[:]), ALU.mult, bC + bD + [b_tab], [b_tB])
                    TT("dve", A1, A1, B1, ALU.add, [b_tA, b_tB], [b_tA])
                    TT("dve", v4(C1), PDv, v4(tabc[:]), ALU.mult, bC + bD + [b_tab], [b_tC])
                    TT("dve", v4(D1), PCv, v4(tabs[:]), ALU.mult, bC + bD + [b_tab], [b_tD])
                    TT("dve", C1, C1, D1, ALU.subtract, [b_tC, b_tD], [b_tC])
                    S.c("pool", lambda e: e.tensor_copy(out=tA[:, :, 0], in_=sp_small[:, GCR, :]), reads=[b_gc], writes=[b_tA])
                    S.c("pool", lambda e: e.tensor_copy(out=tC[:, :, 0], in_=sp_small[:, GCI, :]), reads=[b_gc], writes=[b_tC])
                    fl = lambda a: a.rearrange("p q c -> p (q c)")
                    S.c("dve", lambda e: e.tensor_tensor_scan(out=fl(tB), data0=fl(rdec[:]), data1=fl(tA), initial=0.0, op0=ALU.mult, op1=ALU.add),
                        reads=[b_rdec, b_tA], writes=[b_tB])
                    S.c("dve", lambda e: e.tensor_tensor_scan(out=fl(tD), data0=fl(rdec[:]), data1=fl(tC), initial=0.0, op0=ALU.mult, op1=ALU.add),
                        reads=[b_rdec, b_tC], writes=[b_tD])
                    S.c("pool", lambda e: e.tensor_copy(out=sp_small[:, GCR, :], in_=tB[:, :, CH]), reads=[b_tB], writes=[b_gc])
                    S.c("pool", lambda e: e.tensor_copy(out=sp_small[:, GCI, :], in_=tD[:, :, CH]), reads=[b_tD], writes=[b_gc])
                    TT("dve", A1, B1, tabc[:], ALU.mult, [b_tB, b_tab], [b_tA])
                    TT("dve", C1, D1, tabs[:], ALU.mult, [b_tD, b_tab], [b_tC])
                    TT("dve", Hb[:, :, 0, 1:CH + 1], A1, C1, ALU.subtract, [b_tA, b_tC], [b_Hb])
                    if last_blk:
                        TT("pool", sp_small[:, HFR, :], tA[:, :, CH], tC[:, :, CH], ALU.subtract, [b_tA, b_tC], [b_hfin])
                    TT("dve", A1, B1, tabs[:], ALU.mult, [b_tB, b_tab], [b_tA])
                    TT("dve", C1, D1, tabc[:], ALU.mult, [b_tD, b_tab], [b_tC])
                    TT("dve", Hb[:, :, 1, 1:CH + 1], A1, C1, ALU.add, [b_tA, b_tC], [b_Hb])
                    if last_blk:
                        TT("pool", sp_small[:, HFI, :], tA[:, :, CH], tC[:, :, CH], ALU.add, [b_tA, b_tC], [b_hfin])
                        S.dma("sp", lambda e: e.dma_start(out=pre_d.rearrange("q p -> p q"), in_=sp_small[:, HFR, :], allow_slow_non_contiguous=True),
                              reads=[b_hfin], out=True)
                        S.dma("sp", lambda e: e.dma_start(out=pim_d.rearrange("q p -> p q"), in_=sp_small[:, HFI, :], allow_slow_non_contiguous=True),
                              reads=[b_hfin], out=True)
                if stage >= 4:
                    for kc in range(4):
                        pi = 2 + (kc % 2)
                        Y1, Y2 = pbank(pi, 0), pbank(pi, 1)

                        def ystage(e, kc=kc, Y1=Y1, Y2=Y2):
                            last = None
                            for jp in range(T8):
                                for j in range(jp + 1):
                                    last = e.matmul(Y1[:, jp:BT:T8], lhsT=Kblk[:, kc, jp - j, :], rhs=uT[:, kc, j:BT:T8], start=(j == 0), stop=(j == jp))
                            for jp in range(T8):
                                for q4 in range(4):
                                    q = q4 * 4 + kc
                                    o = Y2[32 * q4:32 * q4 + 32, jp:BT:T8]
                                    e.matmul(o, lhsT=Cst[:, q, jp + 1, 0, :], rhs=Hb[:, q, 0, 0:CH], start=True, stop=False, tile_position=(0, 32 * q4))
                                    last = e.matmul(o, lhsT=Cst[:, q, jp + 1, 1, :], rhs=Hb[:, q, 1, 0:CH], start=False, stop=True, tile_position=(0, 32 * q4))
                            return last
                        S.c("pe", ystage, reads=[b_Kblk, b_Cst, b_uT[kc], b_Hb, b_Hb0], writes=[bank[pi][0], bank[pi][1]])
                        s = kc % 2
                        S.c("act", lambda e, s=s, Y1=Y1: e.activation(out=ytmp[:, s, :], in_=Y1, func=AF.Copy), reads=[bank[pi][0]], writes=[b_ytmp[s]])
                        S.c("dve", lambda e, s=s, Y2=Y2: e.tensor_tensor(out=ytmp[:, s, :], in0=ytmp[:, s, :], in1=Y2, op=ALU.add),
                            reads=[b_ytmp[s], bank[pi][1]], writes=[b_ytmp[s]])
                        S.c("dve", lambda e, s=s, kc=kc: e.scalar_tensor_tensor(out=ytmp[:, s, :], in0=uT[:, kc, :], scalar=vecs[:, V_DS + kc:V_DS + kc + 1],
                                                                            in1=ytmp[:, s, :], op0=ALU.mult, op1=ALU.add),
                            reads=[b_ytmp[s], b_uT[kc], b_vecs], writes=[b_ytmp[s]])
                        S.c("act", lambda e, s=s, kc=kc: e.activation(out=yg[:, kc, :], in_=ytmp[:, s, :], func=AF.Gelu), reads=[b_ytmp[s]], writes=[b_yg[kc]])
                if stage >= 5:
                    S.c("pool", lambda e: e.tensor_copy(out=Hb[:, :, :, 0], in_=Hb[:, :, :, CH]), reads=[b_Hb], writes=[b_Hb0])
                    for m in range(4):
                        po, pbuf = next_bank()

                        def glu(e, m=m, po=po):
                            last = None
                            for k in range(4):
                                last = e.matmul(po, lhsT=w_glu_sb[:, k, m * 128:(m + 1) * 128], rhs=yg[:, k, :], start=(k == 0), stop=(k == 3))
                            return last
                        S.c("pe", glu, reads=[b_wglu] + b_yg, writes=[pbuf])
                        s = m % 2
                        S.c("act", lambda e, s=s, po=po: e.activation(out=ytmp[:, s, :], in_=po, func=AF.Sigmoid), reads=[pbuf], writes=[b_ytmp[s]])
                        S.c("dve", lambda e, s=s, m=m: e.tensor_tensor(out=hT[:, m, :], in0=yg[:, m, :], in1=ytmp[:, s, :], op=ALU.mult),
                            reads=[b_ytmp[s], b_yg[m]], writes=b_h[m])
                if stage < 5:
                    for m in range(4):
                        S.c("dve", lambda e, m=m: e.memset(hT[:, m, :], 0.0), writes=b_h[m])
            else:
                for m in range(4):
                    S.c("dve", lambda e, m=m: e.memset(hT[:, m, :], 0.0), writes=b_h[m])

            load_wout()
            for t in range(4):
                pi = 2 + (t % 2)

                def oproj(e, t=t, pi=pi):
                    last = None
                    for h in range(2):
                        for k in range(8):
                            last = e.matmul(pbank(pi, h), lhsT=hT[:, k, t * 128:(t + 1) * 128], rhs=w_out_sb[:, k, h * 512:(h + 1) * 512],
                                            start=(k == 0), stop=(k == 7))
                    return last
                S.c("pe", oproj, reads=[b_wout] + [b_h[k][t] for k in range(8)], writes=[bank[pi][0], bank[pi][1]])
                S.c("dve", lambda e, t=t, pi=pi, xb=xb: e.tensor_tensor(out=xb[:, t, :], in0=xb[:, t, :], in1=PSn[pi][:, :], op=ALU.add),
                    reads=[bx[t], bank[pi][0], bank[pi][1]], writes=[bx[t]])
            norm_to_hT([(xb[:, t, :], bx[t], t * 128, 128) for t in range(4)], V_G2)
            fence()
            if blk + 1 < nblk:
                for t in range(4):
                    S.dma("sp", lambda e, t=t, r1=r0 + BT, xn_=xn_: e.dma_start(out=xn_[:, t, :], in_=x_d[r1 + t * 128:r1 + (t + 1) * 128, :]), writes=[bxn[t]])
            all_h = flat(b_h)
            if blk > 0:
                cwv = vecs[:, V_CW:V_CW + 132].rearrange("p (c k) -> p c k", k=3)
                S.c("pool", lambda e: e.tensor_tensor(out=hfix[:], in0=hhalo[:], in1=cwv[:, :, 0:1].broadcast_to([128, 2 * NF, 2]), op=ALU.mult),
                    reads=[b_hhalo, b_vecs], writes=[b_hfix])
                S.c("pool", lambda e: e.tensor_tensor(out=hfix2[:], in0=hhalo[:, :, 1], in1=cwv[:, :, 1], op=ALU.mult),
                    reads=[b_hhalo, b_vecs], writes=[b_hfix2])
                S.c("pool", lambda e: e.tensor_tensor(out=hfix[:, :, 0], in0=hfix[:, :, 0], in1=hfix2[:], op=ALU.add),
                    reads=[b_hfix, b_hfix2], writes=[b_hfix])
            def ffn_tail(f):
                cs = f % 3
                S.c("act", lambda e, cs=cs: e.activation(out=cvt3[:, cs, 0, :], in_=cvt3[:, cs, 0, :], func=AF.Gelu), reads=[b_cvt3[cs][0]], writes=[b_cvt3[cs][0]])
                S.c("pool", lambda e, f=f, cs=cs: e.tensor_tensor(out=actT[:, f, :], in0=cvt3[:, cs, 0, :], in1=cvt3[:, cs, 1, :], op=ALU.mult),
                    reads=b_cvt3[cs], writes=[b_act[f]])

            for f in range(NF):
                slot = load_wup(f)
                pi = 1 + (f % 3)
                cs = f % 3

                def up(e, slot=slot, pi=pi):
                    last = None
                    for gv in range(2):
                        for k in range(8):
                            last = e.matmul(pbank(pi, gv), lhsT=w_up_ring[slot][:, k, gv * 128:(gv + 1) * 128], rhs=hT[:, k, :], start=(k == 0), stop=(k == 7))
                    return last
                S.c("pe", up, reads=[b_wup[slot]] + all_h, writes=[bank[pi][0], bank[pi][1]])
                if mrg:
                    sample_up(f, slot)
                for gv in range(2):
                    ch = f + gv * NF
                    cw2 = vecs[:, V_CW + ch * 3 + 2:V_CW + ch * 3 + 3]
                    S.c("act", lambda e, gv=gv, ch=ch, cw2=cw2, cs=cs, pi=pi: e.activation(out=cvt3[:, cs, gv, :], in_=pbank(pi, gv), func=AF.Identity, scale=cw2,
                                                                                     bias=vecs[:, V_CB + ch:V_CB + ch + 1]),
                        reads=[bank[pi][gv], b_vecs], writes=[b_cvt3[cs][gv]])
                if f > 0:
                    ffn_tail(f - 1)
                for gv in range(2):
                    ch = f + gv * NF
                    cw = lambda k, ch=ch: vecs[:, V_CW + ch * 3 + k:V_CW + ch * 3 + k + 1]
                    cc = cvt3[:, cs, gv, :]
                    pg = pbank(pi, gv)
                    bc = b_cvt3[cs][gv]
                    S.c("dve", lambda e, cc=cc, pg=pg, cw=cw: e.scalar_tensor_tensor(out=cc[:, 1:BT], in0=pg[:, 0:BT - 1], scalar=cw(1), in1=cc[:, 1:BT],
                                                                                 op0=ALU.mult, op1=ALU.add), reads=[bank[pi][gv], b_vecs, bc], writes=[bc])
                    S.c("dve", lambda e, cc=cc, pg=pg, cw=cw: e.scalar_tensor_tensor(out=cc[:, 2:BT], in0=pg[:, 0:BT - 2], scalar=cw(0), in1=cc[:, 2:BT],
                                                                                 op0=ALU.mult, op1=ALU.add), reads=[bank[pi][gv], b_vecs, bc], writes=[bc])
                    if blk > 0:
                        S.c("pool", lambda e, cc=cc, ch=ch: e.tensor_tensor(out=cc[:, 0:2], in0=cc[:, 0:2], in1=hfix[:, ch, :], op=ALU.add),
                            reads=[b_hfix, bc], writes=[bc])
                S.c("dve", lambda e, f=f, pi=pi: e.tensor_copy(out=hhalo[:, f:2 * NF:NF, :], in_=PSn[pi][:, :].rearrange("p (g n) -> p g n", g=2)[:, :, BT - 2:BT]),
                    reads=[bank[pi][0], bank[pi][1]], writes=[b_hhalo])
            ffn_tail(NF - 1)
            if last_blk:
                S.dma("sp", lambda e: [e.dma_start(out=pconv_d[r, :].rearrange("(c p) -> p c", p=128), in_=hhalo[:, :, r], allow_slow_non_contiguous=True)
                                        for r in range(2)], reads=[b_hhalo], out=True, ndma=2)
            def emit_down(m):
                slot = load_wdn(m)
                po, pbuf = next_bank()
                if mrg:
                    sample_down(m, slot)

                def down(e, slot=slot, po=po):
                    last = None
                    for k in range(NF):
                        last = e.matmul(po, lhsT=w_dn_ring[slot][:, k, :], rhs=actT[:, k, :], start=(k == 0), stop=(k == NF - 1))
                    return last
                S.c("pe", down, reads=[b_wdn[slot]] + b_act, writes=[pbuf])
                s = m % 2
                S.c("act", lambda e, s=s, po=po: e.activation(out=scr[:, s, 0:BT], in_=po, func=AF.Copy), reads=[pbuf], writes=[b_scr[s]])

            def emit_btr(m):
                s = m % 2
                hb = m % 2

                def btr(e, s=s, hb=hb):
                    last = None
                    for t in range(4):
                        last = e.transpose(out=PA[:, hb * 512 + t * 128:hb * 512 + (t + 1) * 128], in_=scr[:, s, t * 128:(t + 1) * 128], identity=ident[:])
                    return last
                S.c("pe", btr, reads=[b_scr[s], b_ident], writes=[bank[0][hb]])
                if mrg:
                    sample_btr(m)
                S.c("dve", lambda e, m=m, hb=hb, xb=xb: e.tensor_tensor(out=xb[:, :, m * 128:(m + 1) * 128], in0=xb[:, :, m * 128:(m + 1) * 128],
                                                             in1=PA[:, hb * 512:(hb + 1) * 512].rearrange("p (t n) -> p t n", t=4), op=ALU.add),
                    reads=bx + [bank[0][hb]], writes=bx)
            emit_down(0)
            if do_s5 and not last_blk:
                cWb = sp_small[:, CW, :].unsqueeze(2).broadcast_to([128, NQ, CH])
                sWb = sp_small[:, SW, :].unsqueeze(2).broadcast_to([128, NQ, CH])
                u1 = cvt3[:, 0, :, :].rearrange("p g n -> p (g n)").rearrange("p (q c) -> p q c", q=NQ)
                u2 = cvt3[:, 1, :, :].rearrange("p g n -> p (g n)").rearrange("p (q c) -> p q c", q=NQ)
                bu1, bu2 = b_cvt3[0], b_cvt3[1]
                TTp = lambda o, a_, b_, op, rd, wr: S.c("pool", lambda e: e.tensor_tensor(out=o, in0=a_, in1=b_, op=op), reads=rd, writes=wr)
                TTp(u1, tabc[:], sWb, ALU.mult, [b_tab, b_sps], bu1)
                TTp(u2, tabs[:], sWb, ALU.mult, [b_tab, b_sps], bu2)
                TTp(tabc[:], tabc[:], cWb, ALU.mult, [b_tab, b_sps], [b_tab])
                TTp(tabc[:], tabc[:], u2, ALU.subtract, [b_tab] + bu2, [b_tab])
                TTp(tabs[:], tabs[:], cWb, ALU.mult, [b_tab, b_sps], [b_tab])
                TTp(tabs[:], tabs[:], u1, ALU.add, [b_tab] + bu1, [b_tab])
            for m in range(8):
                if m + 1 < 8:
                    emit_down(m + 1)
                emit_btr(m)
            fence()
            if blk + 1 < nblk:
                norm_to_hT([(xn_[:, t, :], bxn[t], t * 128, 128) for t in range(4)], V_G1)
                if not (do_sample and blk + 1 == nblk - 1):
                    block_early(blk + 1)
            if mrg:
                sample_phase_C()
            rms_stats(4, [xb[:, t, :] for t in range(4)], bx)
            for t in range(4):
                s = t % 2
                S.c("dve", lambda e, t=t, s=s, xb=xb: e.scalar_tensor_tensor(out=scr[:, s, :], in0=xb[:, t, :], scalar=stat[:, 8 + t:9 + t], in1=g3b[:],
                                                                  op0=ALU.mult, op1=ALU.mult), reads=[bx[t], b_stat, b_g3b], writes=[b_scr[s]])
                S.dma("sp", lambda e, t=t, s=s, r0=r0: e.dma_start(out=y_d[r0 + t * 128:r0 + (t + 1) * 128, :], in_=scr[:, s, :]), reads=[b_scr[s]], out=True)

        S.emit(st)
        build_program.stats = S.stats
    return nc


_CACHE = {}


def _pack_inputs(inp):
    f32 = np.float32
    A = lambda a: np.ascontiguousarray(np.asarray(a, dtype=f32))
    vecs = np.zeros((128, NV), f32)
    vecs[:, V_G1:V_G1 + 8] = A(inp["norm_mix_g"])[0].reshape(8, 128).T
    vecs[:, V_G2:V_G2 + 8] = A(inp["norm_ffn_g"])[0].reshape(8, 128).T
    vecs[:, V_PS:V_PS + 4] = A(inp["pool_scale"])[0].reshape(4, 128).T
    vecs[:, V_DS:V_DS + 4] = A(inp["s5_d"])[0].reshape(4, 128).T
    vecs[:, V_CW:V_CW + 132] = A(inp["ffn_conv_w"])[0].reshape(3, 44, 128).transpose(2, 1, 0).reshape(128, 132)
    vecs[:, V_CB:V_CB + 44] = A(inp["ffn_conv_b"])[0].reshape(44, 128).T
    vecs[:, V_IC:V_IC + 16] = (1.0 / np.arange(1, 17, dtype=np.float64)).astype(f32)[None, :]
    vecs[:, V_IO:V_IO + 64] = np.arange(64, dtype=f32)[None, :]
    s5p = np.zeros((128, NSP), f32)
    perm = np.array([(qp % 4) * 4 + qp // 4 for qp in range(16)])
    lay = lambda a: A(a)[0].reshape(16, 2, 64)[perm].transpose(1, 2, 0).reshape(128, 16)
    s5p[:, SP_ARE:SP_ARE + 16] = lay(inp["s5_a_re"])
    s5p[:, SP_AIM:SP_AIM + 16] = lay(inp["s5_a_im"])
    s5p[:, SP_LDT:SP_LDT + 16] = np.broadcast_to(A(inp["s5_log_dt"])[0].reshape(16, 2, 1)[perm], (16, 2, 64)).transpose(1, 2, 0).reshape(128, 16)
    for off, key in ((SP_BRE, "s5_b_re"), (SP_BIM, "s5_b_im")):
        b = A(inp[key])[0].reshape(16, 2, 64, 16)[perm]
        o = np.zeros((2, 64, 16, 2, 16), f32)
        for g2 in range(2):
            o[g2, :, :, g2, :] = b[:, g2].transpose(1, 0, 2)
        s5p[:, off:off + 512] = o.reshape(128, 512)
    for off, key in ((SP_CRE, "s5_c_re"), (SP_CIM, "s5_c_im")):
        c = A(inp[key])[0].reshape(16, 2, 16, 64)[perm]
        o = np.zeros((2, 64, 16, 2, 16), f32)
        for g2 in range(2):
            o[g2, :, :, g2, :] = c[:, g2].transpose(2, 0, 1)
        s5p[:, off:off + 512] = o.reshape(128, 512)
    sel = np.zeros((128, 32), f32)
    for g, w in enumerate((2, 4, 8, 16)):
        for t in range(8):
            for r in range(16 - w, 15):
                sel[t * 15 + r, g * 8 + t] = 1.0
    shared = {
        "w_in": A(inp["w_in"])[0], "w_glu": A(inp["s5_w_glu"])[0], "pool_w": A(inp["pool_w"])[0].reshape(512, 128),
        "w_out": A(inp["w_out"])[0], "w_up": A(inp["ffn_w_up"])[0], "w_down": A(inp["ffn_w_down"])[0],
        "vecs": vecs, "g3b": np.ascontiguousarray(np.broadcast_to(A(inp["norm_final_g"])[None, :], (128, D))),
        "ident": np.eye(128, dtype=f32), "s5p": s5p, "sel": sel,
    }
    maps = []
    for i in range(NCORES):
        sl = slice(i * NS, (i + 1) * NS)
        m = dict(shared)
        m["x"] = A(inp["x_prompt"])[i]
        m["xs"] = A(inp["x_sample"])[sl, 0, :]
        m["s5re"] = A(inp["state_s5_re"])[0, sl].reshape(NS, 2048)
        m["s5im"] = A(inp["state_s5_im"])[0, sl].reshape(NS, 2048)
        m["pst"] = A(inp["state_pool"])[0, sl]
        m["cst"] = A(inp["state_ffn_conv"])[0, sl]
        maps.append(m)
    return maps


def kernel(**inputs):
    if "nc" not in _CACHE:
        _CACHE["nc"] = build_program()
    nc = _CACHE["nc"]
    maps = _pack_inputs(inputs)
    res = run_bass_kernel_spmd(nc, maps, core_ids=list(range(NCORES)))
    R = res.results
    f32 = np.float32
    y_prompt = np.stack([R[i]["y"] for i in range(NCORES)]).astype(f32)
    y_sample = np.concatenate([R[i]["ys"] for i in range(NCORES)])[:, None, :].astype(f32)
    inv = np.array([(q % 4) * 4 + q // 4 for q in range(16)])
    p_re = np.stack([R[i]["p_re"][inv].reshape(NG, NP_) for i in range(NCORES)])[None].astype(f32)
    p_im = np.stack([R[i]["p_im"][inv].reshape(NG, NP_) for i in range(NCORES)])[None].astype(f32)
    p_pool = np.stack([R[i]["p_pool"] for i in range(NCORES)])[None].astype(f32)
    p_conv = np.stack([R[i]["p_conv"] for i in range(NCORES)])[None].astype(f32)
    s_re = np.concatenate([R[i]["s_re"].reshape(NS, NG, NP_) for i in range(NCORES)])[None].astype(f32)
    s_im = np.concatenate([R[i]["s_im"].reshape(NS, NG, NP_) for i in range(NCORES)])[None].astype(f32)
    s_pool = np.concatenate([R[i]["s_pool"] for i in range(NCORES)])[None].astype(f32)
    s_conv = np.concatenate([R[i]["s_conv"] for i in range(NCORES)])[None].astype(f32)
    return (y_prompt, y_sample, p_re, p_im, p_pool, p_conv, s_re, s_im, s_pool, s_conv)
```

```python
import math
from contextlib import ExitStack

import numpy as np
import concourse.bass as bass
import concourse.mybir as mybir
from concourse.bass_utils import run_bass_kernel_spmd

F32 = mybir.dt.float32
BF16 = mybir.dt.bfloat16
AF = mybir.ActivationFunctionType
ALU = mybir.AluOpType
AX = mybir.AxisListType


class Buf:
    __slots__ = ("name", "writer", "readers")

    def __init__(self, name):
        self.name = name
        self.writer = None
        self.readers = []


class Op:
    __slots__ = ("eng", "fn", "kind", "deps", "signal", "tok", "waits", "clock", "ndma", "idx")


class Sched:
    ENGS = ("pe", "act", "dve", "pool", "sp")
    NRING = 40

    def __init__(self, nc):
        self.nc = nc
        self.ops = []
        self.ring_n = 0
        self.ring_cum = [0] * self.NRING
        self.ring_last = [None] * self.NRING
        self.out_dmas = []
        self.n_sw = 0

    def buf(self, name):
        return Buf(name)

    def _add(self, eng, fn, reads, writes, kind, ndma=0):
        op = Op()
        op.eng, op.fn, op.kind, op.ndma = eng, fn, kind, ndma
        op.signal = False
        op.idx = len(self.ops)
        deps = []
        for b in reads:
            if b.writer is not None:
                deps.append(b.writer)
        for b in writes:
            if b.writer is not None:
                deps.append(b.writer)
            deps.extend(b.readers)
        for b in reads:
            b.readers.append(op)
        for b in writes:
            b.writer = op
            b.readers = []
        if kind == "dma" and eng == "pool":
            op.tok = (("sw", self.n_sw), 16 * ndma)
            self.n_sw += 1
        elif kind == "dma":
            r = self.ring_n % self.NRING
            self.ring_n += 1
            if self.ring_last[r] is not None:
                deps.append(self.ring_last[r])
            self.ring_cum[r] += 16 * ndma
            op.tok = (("dma", r), self.ring_cum[r])
            self.ring_last[r] = op
        else:
            op.tok = None
        op.deps = [d for d in set(deps) if not (d.eng == "pe" and eng == "pe" and d.kind == "c" and kind == "c") and d is not op]
        self.ops.append(op)
        return op

    def c(self, eng, fn, reads=(), writes=()):
        return self._add(eng, fn, list(reads), list(writes), "c")

    def dma(self, eng, fn, reads=(), writes=(), ndma=1, out=False):
        op = self._add(eng, fn, list(reads), list(writes), "dma", ndma)
        if out:
            self.out_dmas.append(op)
        return op

    def emit(self, stack):
        nc = self.nc
        fin = self._add("sp", None, [], [], "c")
        fin.deps = list(self.out_dmas)
        for op in self.ops:
            for d in op.deps:
                d.signal = True
        cnt = {e: 0 for e in self.ENGS}
        for op in self.ops:
            if op.kind == "c" and op.signal:
                cnt[op.eng] += 1
                op.tok = (op.eng, cnt[op.eng])
        known = {e: {} for e in self.ENGS}
        for op in self.ops:
            kn = known[op.eng]
            waits = []
            for d in sorted(op.deps, key=lambda o: o.idx):
                k, v = d.tok
                if kn.get(k, 0) >= v:
                    continue
                waits.append((k, v))
                for kk, vv in d.clock.items():
                    if kn.get(kk, 0) < vv:
                        kn[kk] = vv
            best = {}
            for k, v in waits:
                best[k] = max(best.get(k, 0), v)
            op.waits = list(best.items())
            op.clock = dict(kn)
            if op.tok is not None:
                op.clock[op.tok[0]] = max(op.clock.get(op.tok[0], 0), op.tok[1])
        sems = {}
        for e in ("pe", "act", "dve", "pool", "sp"):
            sems[e] = stack.enter_context(nc.semaphore("s_" + e))
        for r in range(min(self.NRING, max(self.ring_n, 1))):
            sems[("dma", r)] = stack.enter_context(nc.semaphore("s_dma%d" % r))
        for r in range(self.n_sw):
            sems[("sw", r)] = stack.enter_context(nc.semaphore("s_sw%d" % r))
        block = stack.enter_context(nc.Block())
        per = {e: [o for o in self.ops if o.eng == e] for e in self.ENGS}
        self.stats = {e: (len(per[e]), sum(len(o.waits) for o in per[e])) for e in self.ENGS}

        def run(eng_handle, lst):
            for op in lst:
                for k, v in op.waits:
                    eng_handle.wait_ge(sems[k], v)
                if op.fn is None:
                    continue
                res = op.fn(eng_handle)
                if op.kind == "dma":
                    if not isinstance(res, (list, tuple)):
                        res = [res]
                    assert len(res) == op.ndma, (len(res), op.ndma)
                    for ins in res:
                        ins.then_inc(sems[op.tok[0]], 16)
                elif op.signal:
                    if isinstance(res, (list, tuple)):
                        res = res[-1]
                    res.then_inc(sems[op.tok[0]], 1)

        @block.tensor
        def _(e):
            run(e, per["pe"])

        @block.scalar
        def _(e):
            run(e, per["act"])

        @block.vector
        def _(e):
            run(e, per["dve"])

        @block.gpsimd
        def _(e):
            run(e, per["pool"])

        @block.sync
        def _(e):
            run(e, per["sp"])


NCORES = 8
D = 1024
L = 2048
BT = 512
NBLK = L // BT
NS = 16
NG, NP_, NH = 32, 64, 16
NQ = 16
T8 = 8
CH = BT // T8
DFF = 2816
NF = DFF // 128
EPS = 1e-6
MAGIC = 12582912.0
TWO_PI = 2.0 * math.pi
PI_LO = 3.1415925

V_G1, V_G2, V_PS, V_DS, V_CW, V_CB, V_IC, V_IO = 0, 8, 16, 20, 24, 156, 200, 216
NV = 280
SP_ARE, SP_AIM, SP_LDT, SP_BRE, SP_BIM, SP_CRE, SP_CIM = 0, 16, 32, 48, 560, 1072, 1584
NSP = 2096


def build_program(do_s5=True, do_sample=True, nblk=NBLK, stage=99):
    nc = bass.Bass("TRN2", target_bir_lowering=False)
    din = lambda name, shape: nc.dram_tensor(name, list(shape), F32, kind="ExternalInput").ap()
    dout = lambda name, shape: nc.dram_tensor(name, list(shape), F32, kind="ExternalOutput").ap()
    x_d = din("x", [L, D])
    xs_d = din("xs", [NS, D])
    w_in_d = din("w_in", [D, D])
    w_glu_d = din("w_glu", [512, 512])
    pool_w_d = din("pool_w", [512, 128])
    w_out_d = din("w_out", [D, D])
    w_up_d = din("w_up", [D, 2 * DFF])
    w_down_d = din("w_down", [DFF, D])
    vecs_d = din("vecs", [128, NV])
    g3b_d = din("g3b", [128, D])
    ident_d = din("ident", [128, 128])
    s5p_d = din("s5p", [128, NSP])
    s5re_d = din("s5re", [NS, 2048])
    s5im_d = din("s5im", [NS, 2048])
    pst_d = din("pst", [NS, 15, 512])
    cst_d = din("cst", [NS, 2, 2 * DFF])
    y_d = dout("y", [L, D])
    ys_d = dout("ys", [NS, D])
    pre_d = dout("p_re", [NQ, 128])
    pim_d = dout("p_im", [NQ, 128])
    ppool_d = dout("p_pool", [15, 512])
    pconv_d = dout("p_conv", [2, 2 * DFF])
    sre_d = dout("s_re", [NS, 2048])
    sim_d = dout("s_im", [NS, 2048])
    spool_d = dout("s_pool", [NS, 15, 512])
    sconv_d = dout("s_conv", [NS, 2, 2 * DFF])

    wupb = nc.dram_tensor("wupb", [NF, 128, 8, 256], BF16).ap()
    wdnb = nc.dram_tensor("wdnb", [8, 128, NF, 128], BF16).ap()
    winb = nc.dram_tensor("winb", [8, 128, 8, 128], BF16).ap()
    woutb = nc.dram_tensor("woutb", [128, 8, D], BF16).ap()
    st = ExitStack()
    with st:
        S = Sched(nc)
        sbt = lambda name, shape, dt: st.enter_context(nc.sbuf_tensor("sb_" + name, list(shape), dt))
        w_glu_sb = sbt("w_glu_sb", [128, 4, 512], BF16)
        pool_w_sb = sbt("pool_w_sb", [128, 4, 128], BF16)
        NWI, NWU, NWD = 2, 3, 2
        w_in_ring = [sbt("w_in_r%d" % i, [128, 8, 128], BF16) for i in range(NWI)]
        w_up_ring = [sbt("w_up_r%d" % i, [128, 8, 256], BF16) for i in range(NWU)]
        w_dn_ring = [sbt("w_dn_r%d" % i, [128, NF, 128], BF16) for i in range(NWD)]
        vecs = sbt("vecs", [128, NV], F32)
        g3b = sbt("g3b", [128, D], F32)
        ident = sbt("ident", [128, 128], F32)
        identb = sbt("identb", [128, 128], BF16)
        Bst = sbt("Bst", [128, 4, T8, 2, 128], BF16)
        Cst = sbt("Cst", [128, NQ, T8 + 1, 2, 32], BF16)
        Kblk = sbt("Kblk", [128, 4, T8, 128], BF16)
        tabc = sbt("tabc", [128, NQ, CH], F32)
        tabs = sbt("tabs", [128, NQ, CH], F32)
        rdec = sbt("rdec", [128, NQ, CH + 1], F32)
        sp_small = sbt("sp_small", [128, 40, NQ], F32)
        LR, LI, PR0, PI0, CW, SW, GCR, GCI, HFR, HFI, TMPA, TMPB, TMPC, TMPD, ANG0 = 0, 1, 2, 11, 20, 21, 22, 23, 24, 25, 26, 27, 28, 29, 30
        stat = sbt("stat", [128, 16], F32)
        x_sb = sbt("x_sb", [128, 4, D], F32)
        x_sb2 = sbt("x_sb2", [128, 4, D], F32)
        scr = sbt("scr", [128, 2, D], F32)
        hT = sbt("hT", [128, 8, BT], BF16)
        v_sb = sbt("v_sb", [128, 4, 16 + BT], F32)
        Hb = sbt("Hb", [128, NQ, 2, CH + 1], BF16)
        hhalo = sbt("hhalo", [128, 2 * NF, 2], F32)
        hfix = sbt("hfix", [128, 2 * NF, 2], F32)
        hfix2 = sbt("hfix2", [128, 2 * NF], F32)
        ARENA_W = 8712
        arena = sbt("arena", [128, ARENA_W], F32)
        fence_scr = sbt("fence_scr", [128, 2], F32)

        def av(off, nwords, dt=F32, **kw):
            a = arena[:, off:off + nwords]
            if dt != F32:
                a = a.bitcast(dt)
            return a

        actT = av(0, 5632, BF16).rearrange("p (k n) -> p k n", k=NF)
        hup = av(5632, 2056).rearrange("p (s g n) -> p s g n", s=2, g=2)
        cvt = av(7688, 1024).rearrange("p (g n) -> p g n", g=2)
        cvt3 = av(5632, 3072).rearrange("p (s g n) -> p s g n", s=3, g=2)
        w_out_sb = av(0, 4096, BF16).rearrange("p (k n) -> p k n", k=8)
        tA = av(0, 1040).rearrange("p (q c) -> p q c", q=NQ)
        tB = av(1040, 1040).rearrange("p (q c) -> p q c", q=NQ)
        tC = av(2080, 1040).rearrange("p (q c) -> p q c", q=NQ)
        tD = av(3120, 1040).rearrange("p (q c) -> p q c", q=NQ)
        uT = av(4160, 1024, BF16).rearrange("p (k n) -> p k n", k=4)
        yg = av(5184, 1024, BF16).rearrange("p (k n) -> p k n", k=4)
        ptmp = av(6208, 1056).rearrange("p (s n) -> p s n", s=2)
        pooled = av(7264, 256, BF16)
        ytmp = av(7520, 1024).rearrange("p (s n) -> p s n", s=2)
        s5p = av(0, NSP)
        G0r = av(2096, 512).rearrange("p (q h) -> p q h", q=NQ)
        G0i = av(2608, 512).rearrange("p (q h) -> p q h", q=NQ)
        Gnr = av(3120, 512).rearrange("p (q h) -> p q h", q=NQ)
        Gni = av(3632, 512).rearrange("p (q h) -> p q h", q=NQ)
        Gnb = av(4144, 512, BF16).rearrange("p (r q h) -> p r q h", r=2, q=NQ)
        Ccb = av(4656, 512, BF16).rearrange("p (r q h) -> p r q h", r=2, q=NQ)
        pt1 = av(5168, 512).rearrange("p (q h) -> p q h", q=NQ)
        pt2 = av(5680, 512).rearrange("p (q h) -> p q h", q=NQ)
        pang = av(6192, 1024).rearrange("p (q c) -> p q c", q=NQ)
        pang2 = av(7216, 1024).rearrange("p (q c) -> p q c", q=NQ)

        PA = st.enter_context(nc.psum_tensor("psA", [128, 1024], F32))
        PB = st.enter_context(nc.psum_tensor("psB", [128, 1024], F32))
        PCD = st.enter_context(nc.psum_tensor("psCD", [128, 2048], F32))
        PC, PD = PCD[:, 0:1024], PCD[:, 1024:2048]
        PSn = [PA, PB, PC, PD]
        bank = [[S.buf("ps%d_%d" % (i, h)) for h in range(2)] for i in range(4)]

        def pbank(i, h):
            return PSn[i][:, h * 512:(h + 1) * 512]

        b_wout, b_wglu, b_poolw = S.buf("wout"), S.buf("wglu"), S.buf("poolw")
        b_win = [S.buf("win%d" % i) for i in range(NWI)]
        b_wup = [S.buf("wup%d" % i) for i in range(NWU)]
        b_wdn = [S.buf("wdn%d" % i) for i in range(NWD)]
        b_vecs, b_g3b, b_ident, b_identb = S.buf("vecs"), S.buf("g3b"), S.buf("ident"), S.buf("identb")
        b_Bst, b_Cst, b_Kblk, b_tab, b_rdec, b_sps = S.buf("Bst"), S.buf("Cst"), S.buf("Kblk"), S.buf("tab"), S.buf("rdec"), S.buf("sps")
        b_gc, b_hfin = S.buf("gc"), S.buf("hfin")
        b_stat = S.buf("stat")
        b_x = [S.buf("x%d" % t) for t in range(4)]
        b_x2 = [S.buf("x2_%d" % t) for t in range(4)]
        b_scr = [S.buf("scr%d" % i) for i in range(2)]
        b_h = [[S.buf("h%d_%d" % (k, t)) for t in range(4)] for k in range(8)]
        b_v = [S.buf("v%d" % g) for g in range(4)]
        b_Hb, b_Hb0 = S.buf("Hb"), S.buf("Hb0")
        b_hhalo = S.buf("hhalo")
        b_hfix, b_hfix2 = S.buf("hfix"), S.buf("hfix2")
        b_act = [S.buf("act%d" % f) for f in range(NF)]
        b_hup = [S.buf("hup%d" % s) for s in range(2)]
        b_cvt = S.buf("cvt")
        b_cvt3 = [[S.buf("cvt3_%d_%d" % (i, g)) for g in range(2)] for i in range(3)]
        b_tA, b_tB, b_tC, b_tD = S.buf("tA"), S.buf("tB"), S.buf("tC"), S.buf("tD")
        b_uT = [S.buf("uT%d" % k) for k in range(4)]
        b_yg = [S.buf("yg%d" % k) for k in range(4)]
        b_ptmp = [S.buf("ptmp%d" % i) for i in range(2)]
        b_pooled = S.buf("pooled")
        b_ytmp = [S.buf("ytmp%d" % i) for i in range(2)]
        b_prep = S.buf("prep")
        flat_ = lambda L_: [b for l in L_ for b in l]
        arena_bufs = b_act + b_hup + flat_(b_cvt3) + [b_cvt, b_tA, b_tB, b_tC, b_tD] + b_uT + b_yg + b_ptmp + [b_pooled] + b_ytmp + [b_prep]
        b_fence = S.buf("fence")

        def fence():
            S.c("dve", lambda e: e.memset(fence_scr[:, 0:1], 0.0), writes=arena_bufs + [b_fence])

        flat = lambda L_: [b for l in L_ for b in l]
        V = lambda c0, n: vecs[:, c0:c0 + n]

        S.dma("sp", lambda e: e.dma_start(out=vecs[:], in_=vecs_d), writes=[b_vecs])
        S.dma("sp", lambda e: e.dma_start(out=ident[:], in_=ident_d), writes=[b_ident])
        S.dma("sp", lambda e: e.dma_start(out=g3b[:], in_=g3b_d), writes=[b_g3b])
        b_s5p = S.buf("s5p")
        S.dma("sp", lambda e: e.dma_start(out=s5p, in_=s5p_d), writes=[b_prep, b_s5p])
        S.c("dve", lambda e: e.tensor_copy(out=identb[:], in_=ident[:]), reads=[b_ident], writes=[b_identb])
        S.dma("pool", lambda e: e.dma_start(out=w_glu_sb[:], in_=w_glu_d.rearrange("(k p) n -> p k n", p=128)), writes=[b_wglu])
        S.dma("pool", lambda e: e.dma_start(out=pool_w_sb[:], in_=pool_w_d.rearrange("(k p) n -> p k n", p=128)), writes=[b_poolw])

        b_winb = [S.buf("winb%d" % m) for m in range(8)]
        b_wupb = [S.buf("wupb%d" % f) for f in range(NF)]
        b_wdnb = [S.buf("wdnb%d" % m) for m in range(8)]
        for m in range(8):
            S.dma("pool", lambda e, m=m: e.dma_start(out=winb[m], in_=w_in_d[:, m * 128:(m + 1) * 128].rearrange("(k p) n -> p k n", p=128)),
                  writes=[b_winb[m]])
        for f in range(NF):
            S.dma("pool", lambda e, f=f: [e.dma_start(out=wupb[f][:, :, 0:128], in_=w_up_d[:, f * 128:(f + 1) * 128].rearrange("(k p) n -> p k n", p=128)),
                                           e.dma_start(out=wupb[f][:, :, 128:256], in_=w_up_d[:, DFF + f * 128:DFF + (f + 1) * 128].rearrange("(k p) n -> p k n", p=128))],
                  writes=[b_wupb[f]], ndma=2)
        for m in range(8):
            S.dma("pool", lambda e, m=m: e.dma_start(out=wdnb[m], in_=w_down_d[:, m * 128:(m + 1) * 128].rearrange("(k p) n -> p k n", p=128)),
                  writes=[b_wdnb[m]])
        spv = lambda i, n=1: sp_small[:, i:i + n, :]

        def sincos(ang_ap, out_s, out_c, shape, tmp1, tmp2, reads, writes):
            S.c("dve", lambda e: e.tensor_scalar(out=tmp1, in0=ang_ap, scalar1=1.0 / TWO_PI, scalar2=MAGIC, op0=ALU.mult, op1=ALU.add),
                reads=reads, writes=[b_prep])
            S.c("dve", lambda e: e.tensor_scalar(out=tmp1, in0=tmp1, scalar1=MAGIC, scalar2=-TWO_PI, op0=ALU.subtract, op1=ALU.mult),
                reads=[b_prep], writes=[b_prep])
            S.c("dve", lambda e: e.tensor_tensor(out=tmp1, in0=tmp1, in1=ang_ap, op=ALU.add), reads=[b_prep] + reads, writes=[b_prep])
            S.c("dve", lambda e: e.tensor_scalar(out=tmp1, in0=tmp1, scalar1=PI_LO, scalar2=-PI_LO, op0=ALU.min, op1=ALU.max), reads=[b_prep], writes=[b_prep])
            S.c("act", lambda e: e.activation(out=out_s, in_=tmp1, func=AF.Sin), reads=[b_prep], writes=writes)
            S.c("dve", lambda e: e.tensor_scalar(out=tmp2, in0=ang_ap, scalar1=1.0 / TWO_PI, scalar2=0.25, op0=ALU.mult, op1=ALU.add),
                reads=reads, writes=[b_prep])
            S.c("dve", lambda e: e.tensor_scalar(out=tmp2, in0=tmp2, scalar1=MAGIC, scalar2=None, op0=ALU.add), reads=[b_prep], writes=[b_prep])
            S.c("dve", lambda e: e.tensor_scalar(out=tmp2, in0=tmp2, scalar1=MAGIC, scalar2=-TWO_PI, op0=ALU.subtract, op1=ALU.mult),
                reads=[b_prep], writes=[b_prep])
            S.c("dve", lambda e: e.scalar_tensor_tensor(out=tmp2, in0=ang_ap, scalar=0.5 * math.pi, in1=tmp2, op0=ALU.add, op1=ALU.add),
                reads=[b_prep] + reads, writes=[b_prep])
            S.c("dve", lambda e: e.tensor_scalar(out=tmp2, in0=tmp2, scalar1=PI_LO, scalar2=-PI_LO, op0=ALU.min, op1=ALU.max), reads=[b_prep], writes=[b_prep])
            S.c("act", lambda e: e.activation(out=out_c, in_=tmp2, func=AF.Sin), reads=[b_prep], writes=writes)

        if do_s5:
            are = s5p[:, SP_ARE:SP_ARE + 16]
            aim = s5p[:, SP_AIM:SP_AIM + 16]
            ldt = s5p[:, SP_LDT:SP_LDT + 16]
            Bre = s5p[:, SP_BRE:SP_BRE + 512].rearrange("p (q h) -> p q h", q=NQ)
            Bim = s5p[:, SP_BIM:SP_BIM + 512].rearrange("p (q h) -> p q h", q=NQ)
            Cre = s5p[:, SP_CRE:SP_CRE + 512].rearrange("p (q h) -> p q h", q=NQ)
            Cim = s5p[:, SP_CIM:SP_CIM + 512].rearrange("p (q h) -> p q h", q=NQ)
            sp2 = lambda i: sp_small[:, i, :]
            S.c("act", lambda e: e.activation(out=sp2(TMPA), in_=ldt, func=AF.Exp), reads=[b_prep], writes=[b_sps])
            S.c("dve", lambda e: e.tensor_tensor(out=sp2(LR), in0=sp2(TMPA), in1=are, op=ALU.mult), reads=[b_prep, b_sps], writes=[b_sps])
            S.c("dve", lambda e: e.tensor_tensor(out=sp2(LI), in0=sp2(TMPA), in1=aim, op=ALU.mult), reads=[b_prep, b_sps], writes=[b_sps])
            for n in range(9):
                S.c("act", lambda e, n=n: e.activation(out=sp2(PR0 + n), in_=sp2(LR), func=AF.Exp, scale=float(n)), reads=[b_sps], writes=[b_sps])
                S.c("dve", lambda e, n=n: e.tensor_scalar(out=sp2(ANG0 + n), in0=sp2(LI), scalar1=float(n), scalar2=None, op0=ALU.mult),
                    reads=[b_sps], writes=[b_sps])
            S.c("dve", lambda e: e.tensor_scalar(out=sp2(ANG0 + 9), in0=sp2(LI), scalar1=float(BT), scalar2=None, op0=ALU.mult),
                reads=[b_sps], writes=[b_sps])
            S.c("dve", lambda e: e.memset(rdec[:], 0.0), writes=[b_rdec])
            S.c("dve", lambda e: e.tensor_copy(out=rdec[:, :, 1:CH + 1], in_=sp_small[:, PR0 + 8, :].unsqueeze(2).broadcast_to([128, NQ, CH])),
                reads=[b_sps], writes=[b_rdec])
            angv = sp_small[:, ANG0:ANG0 + 10, :]
            sin10 = pang[:, 0:10, 0:16]
            cos10 = pang[:, 0:10, 16:32]
            sincos(angv, sin10, cos10, None, pang2[:, 0:10, 0:16], pang2[:, 0:10, 16:32], [b_sps], [b_prep])
            S.c("dve", lambda e: e.tensor_tensor(out=sp_small[:, PI0:PI0 + 9, :], in0=sp_small[:, PR0:PR0 + 9, :], in1=sin10[:, 0:9, :], op=ALU.mult),
                reads=[b_prep, b_sps], writes=[b_sps])
            S.c("dve", lambda e: e.tensor_tensor(out=sp_small[:, PR0:PR0 + 9, :], in0=sp_small[:, PR0:PR0 + 9, :], in1=cos10[:, 0:9, :], op=ALU.mult),
                reads=[b_prep, b_sps], writes=[b_sps])
            S.c("dve", lambda e: e.tensor_copy(out=sp2(SW), in_=sin10[:, 9, :]), reads=[b_prep], writes=[b_sps])
            S.c("dve", lambda e: e.tensor_copy(out=sp2(CW), in_=cos10[:, 9, :]), reads=[b_prep], writes=[b_sps])
            S.c("dve", lambda e: e.tensor_scalar(out=sp2(TMPB), in0=sp2(LI), scalar1=float(T8), scalar2=None, op0=ALU.mult), reads=[b_sps], writes=[b_sps])
            S.c("dve", lambda e: e.tensor_tensor(out=pang, in0=sp_small[:, TMPB, :].unsqueeze(2).broadcast_to([128, NQ, CH]),
                                                  in1=V(V_IO, 64).unsqueeze(1).broadcast_to([128, NQ, CH]), op=ALU.mult),
                reads=[b_sps, b_vecs, b_prep], writes=[b_prep])
            tmp_a = av(2096, 1024).rearrange("p (q c) -> p q c", q=NQ)
            tmp_b = av(3120, 1024).rearrange("p (q c) -> p q c", q=NQ)
            sincos(pang, tabs[:], tabc[:], None, tmp_a, tmp_b, [b_prep], [b_tab])
            S.c("dve", lambda e: e.tensor_scalar(out=sp2(TMPA), in0=sp2(PR0 + 1), scalar1=-1.0, scalar2=None, op0=ALU.add), reads=[b_sps], writes=[b_sps])
            S.c("dve", lambda e: e.tensor_tensor(out=sp2(TMPB), in0=are, in1=are, op=ALU.mult), reads=[b_prep], writes=[b_sps])
            S.c("dve", lambda e: e.tensor_tensor(out=sp2(TMPC), in0=aim, in1=aim, op=ALU.mult), reads=[b_prep], writes=[b_sps])
            S.c("dve", lambda e: e.tensor_tensor(out=sp2(TMPB), in0=sp2(TMPB), in1=sp2(TMPC), op=ALU.add), reads=[b_sps], writes=[b_sps])
            S.c("dve", lambda e: e.reciprocal(out=sp2(TMPB), in_=sp2(TMPB)), reads=[b_sps], writes=[b_sps])
            S.c("dve", lambda e: e.tensor_tensor(out=sp2(TMPC), in0=sp2(TMPA), in1=are, op=ALU.mult), reads=[b_sps, b_prep], writes=[b_sps])
            S.c("dve", lambda e: e.tensor_tensor(out=sp2(TMPD), in0=sp2(PI0 + 1), in1=aim, op=ALU.mult), reads=[b_sps, b_prep], writes=[b_sps])
            S.c("dve", lambda e: e.tensor_tensor(out=sp2(TMPC), in0=sp2(TMPC), in1=sp2(TMPD), op=ALU.add), reads=[b_sps], writes=[b_sps])
            S.c("dve", lambda e: e.tensor_tensor(out=sp2(TMPC), in0=sp2(TMPC), in1=sp2(TMPB), op=ALU.mult), reads=[b_sps], writes=[b_sps])
            S.c("dve", lambda e: e.tensor_tensor(out=sp2(TMPD), in0=sp2(PI0 + 1), in1=are, op=ALU.mult), reads=[b_sps, b_prep], writes=[b_sps])
            S.c("dve", lambda e: e.tensor_tensor(out=sp2(TMPA), in0=sp2(TMPA), in1=aim, op=ALU.mult), reads=[b_sps, b_prep], writes=[b_sps])
            S.c("dve", lambda e: e.tensor_tensor(out=sp2(TMPD), in0=sp2(TMPD), in1=sp2(TMPA), op=ALU.subtract), reads=[b_sps], writes=[b_sps])
            S.c("dve", lambda e: e.tensor_tensor(out=sp2(TMPD), in0=sp2(TMPD), in1=sp2(TMPB), op=ALU.mult), reads=[b_sps], writes=[b_sps])
            bq = lambda i: sp_small[:, i, :].unsqueeze(2).broadcast_to([128, NQ, 32])

            pt3 = av(6192, 512).rearrange("p (q h) -> p q h", q=NQ)
            pt4 = av(6704, 512).rearrange("p (q h) -> p q h", q=NQ)
            Gnb2 = av(7216, 512, BF16).rearrange("p (r q h) -> p r q h", r=2, q=NQ)
            pts = [pt1, pt2, pt3, pt4]
            b_pt = [S.buf("pt%d" % i) for i in range(4)]
            b_G0, b_Ccb, b_Gn2 = S.buf("G0"), S.buf("Ccb"), [S.buf("Gnb0"), S.buf("Gnb1")]
            arena_bufs.extend(b_pt + [b_G0, b_Ccb] + b_Gn2)
            S.c("dve", lambda e: e.memset(fence_scr[:, 1:2], 0.0), reads=[b_sps], writes=[b_prep] + b_pt + [b_G0, b_Ccb] + b_Gn2)

            def cmul(out_r, out_i, ar, ai, br_, bi_, in_bufs, out_bufs, neg_i=False):
                t1, t2, t3, t4 = pts
                S.c("dve", lambda e: e.tensor_tensor(out=t1, in0=ar, in1=br_, op=ALU.mult), reads=in_bufs, writes=[b_pt[0]])
                S.c("dve", lambda e: e.tensor_tensor(out=t2, in0=ai, in1=bi_, op=ALU.mult), reads=in_bufs, writes=[b_pt[1]])
                S.c("dve", lambda e: e.tensor_tensor(out=t3, in0=ar, in1=bi_, op=ALU.mult), reads=in_bufs, writes=[b_pt[2]])
                S.c("dve", lambda e: e.tensor_tensor(out=t4, in0=ai, in1=br_, op=ALU.mult), reads=in_bufs, writes=[b_pt[3]])
                S.c("dve", lambda e: e.tensor_tensor(out=out_r, in0=t1, in1=t2, op=ALU.subtract), reads=[b_pt[0], b_pt[1]], writes=out_bufs)
                if neg_i:
                    S.c("dve", lambda e: e.scalar_tensor_tensor(out=out_i, in0=t3, scalar=-1.0, in1=t4, op0=ALU.mult, op1=ALU.subtract),
                        reads=[b_pt[2], b_pt[3]], writes=out_bufs)
                else:
                    S.c("dve", lambda e: e.tensor_tensor(out=out_i, in0=t3, in1=t4, op=ALU.add), reads=[b_pt[2], b_pt[3]], writes=out_bufs)

            cmul(G0r, G0i, bq(TMPC), bq(TMPD), Bre, Bim, [b_sps, b_s5p], [b_G0])
            S.c("dve", lambda e: e.tensor_copy(out=Ccb[:, 0], in_=Cre), reads=[b_s5p], writes=[b_Ccb])
            S.c("dve", lambda e: e.tensor_scalar(out=Ccb[:, 1], in0=Cim, scalar1=-1.0, scalar2=None, op0=ALU.mult), reads=[b_s5p], writes=[b_Ccb])
            S.c("dve", lambda e: e.memset(Kblk[:], 0.0), writes=[b_Kblk])
            for n in range(T8):
                gb, bg = (Gnb, b_Gn2[0]) if n % 2 == 0 else (Gnb2, b_Gn2[1])
                cmul(gb[:, 0], gb[:, 1], bq(PR0 + n), bq(PI0 + n), G0r, G0i, [b_sps, b_G0], [bg])
                def tr_fn(e, gb=gb):
                    last = None
                    for ri in range(2):
                        for q in range(NQ):
                            q4, kc = q // 4, q % 4
                            col = (kc * 2 + ri) * 128
                            last = e.matmul(PSn[0][32 * q4:32 * q4 + 32, col:col + 128], lhsT=gb[:, ri, q, :], rhs=identb[:],
                                            start=True, stop=True, tile_position=(0, 32 * q4))
                    return last
                S.c("pe", tr_fn, reads=[bg, b_identb], writes=[bank[0][0], bank[0][1]])
                S.c("act", lambda e, n=n: e.activation(out=Bst[:, :, T8 - 1 - n, :, :],
                                                        in_=PSn[0][:, :].rearrange("p (k r m) -> p k r m", k=4, r=2), func=AF.Copy),
                    reads=[bank[0][0], bank[0][1]], writes=[b_Bst])
                def kb_fn(e, gb=gb):
                    last = None
                    for q in range(NQ):
                        q4, kc = q // 4, q % 4
                        o = PSn[1][32 * q4:32 * q4 + 32, kc * 128 + 32 * q4:kc * 128 + 32 * q4 + 32]
                        e.matmul(o, lhsT=gb[:, 0, q, :], rhs=Ccb[:, 0, q, :], start=True, stop=False, tile_position=(0, 32 * q4))
                        last = e.matmul(o, lhsT=gb[:, 1, q, :], rhs=Ccb[:, 1, q, :], start=False, stop=True, tile_position=(0, 32 * q4))
                    return last
                S.c("pe", kb_fn, reads=[bg, b_Ccb], writes=[bank[1][0]])
                for q4 in range(4):
                    S.c("act", lambda e, n=n, q4=q4: e.activation(
                        out=Kblk[32 * q4:32 * q4 + 32, :, n, 32 * q4:32 * q4 + 32],
                        in_=PSn[1][32 * q4:32 * q4 + 32, 0:512].rearrange("p (k m) -> p k m", k=4)[:, :, 32 * q4:32 * q4 + 32], func=AF.Copy),
                        reads=[bank[1][0]], writes=[b_Kblk])
            for n in range(T8 + 1):
                cmul(Cst[:, :, n, 0, :], Cst[:, :, n, 1, :], Cre, Cim, bq(PR0 + n), bq(PI0 + n), [b_sps, b_s5p], [b_Cst], neg_i=True)
            S.c("dve", lambda e: e.memset(sp_small[:, GCR:GCR + 2, :], 0.0), writes=[b_gc])
            S.c("dve", lambda e: e.memset(Hb[:], 0.0), writes=[b_Hb, b_Hb0])
        S.c("dve", lambda e: e.memset(hhalo[:], 0.0), writes=[b_hhalo])
        S.c("dve", lambda e: e.memset(v_sb[:], 0.0), writes=b_v)
        fence()
        b_woutb = S.buf("woutb")
        S.dma("pool", lambda e: e.dma_start(out=woutb, in_=w_out_d.rearrange("(k p) n -> p k n", p=128)), writes=[b_woutb])
        arena_bufs.append(b_wout)

        def load_wout():
            S.dma("sp", lambda e: e.dma_start(out=w_out_sb, in_=woutb), reads=[b_woutb], writes=[b_tA, b_tB, b_tC, b_tD, b_wout])
        cnt = {"win": 0, "wup": 0, "wdn": 0, "bk": 0}

        def rms_stats(nt, xt_aps, xbufs, premem=False):
            if not premem:
                S.c("dve", lambda e: e.memset(stat[:, 0:8], 0.0), writes=[b_stat])
            for i, xa in enumerate(xt_aps):
                np_ = xa.shape[0]
                S.c("act", lambda e, i=i, xa=xa, np_=np_: e.activation(out=scr[0:np_, 1, :], in_=xa, func=AF.Square, accum_out=stat[0:np_, i:i + 1]),
                    reads=[xbufs[i], b_stat], writes=[b_scr[1], b_stat])
            S.c("dve", lambda e: e.tensor_scalar(out=stat[:, 0:nt], in0=stat[:, 0:nt], scalar1=1.0 / D, scalar2=EPS, op0=ALU.mult, op1=ALU.add),
                reads=[b_stat], writes=[b_stat])
            S.c("act", lambda e: e.activation(out=stat[:, 0:nt], in_=stat[:, 0:nt], func=AF.Sqrt), reads=[b_stat], writes=[b_stat])
            S.c("dve", lambda e: e.reciprocal(out=stat[:, 8:8 + nt], in_=stat[:, 0:nt]), reads=[b_stat], writes=[b_stat])

        def norm_to_hT(tiles, gcol, dst=None, dbufs=None, premem=False):
            rms_stats(len(tiles), [t_[0] for t_ in tiles], [t_[1] for t_ in tiles], premem=premem)
            for i, (xa, xb, c0, np_) in enumerate(tiles):
                s = i % 2
                S.c("dve", lambda e, xa=xa, i=i, np_=np_, s=s: e.tensor_scalar(out=scr[0:np_, s, :], in0=xa, scalar1=stat[0:np_, 8 + i:9 + i],
                                                                         scalar2=None, op0=ALU.mult),
                    reads=[xb, b_stat], writes=[b_scr[s]])

                def tr(e, np_=np_, s=s):
                    last = None
                    for k in range(8):
                        last = e.transpose(out=PA[:, k * 128:k * 128 + np_], in_=scr[0:np_, s, k * 128:(k + 1) * 128], identity=ident[0:np_, 0:np_])
                    return last
                S.c("pe", tr, reads=[b_scr[s], b_ident], writes=[bank[0][0], bank[0][1]])
                t_idx = c0 // 128
                dst_ = hT if dst is None else dst
                S.c("dve", lambda e, np_=np_, c0=c0, dst_=dst_: e.tensor_tensor(out=dst_[:, :, c0:c0 + np_],
                                                                 in0=PA[:, :].rearrange("p (k n) -> p k n", k=8)[:, :, 0:np_],
                                                                 in1=V(gcol, 8).unsqueeze(2).broadcast_to([128, 8, np_]), op=ALU.mult),
                    reads=[bank[0][0], bank[0][1], b_vecs], writes=([b_h[k][t_idx] for k in range(8)] if dbufs is None else dbufs))

        def load_win(m):
            slot = cnt["win"] % NWI
            cnt["win"] += 1
            S.dma("sp", lambda e: e.dma_start(out=w_in_ring[slot][:], in_=winb[m]), reads=[b_winb[m]], writes=[b_win[slot]])
            return slot

        def load_wup(f):
            slot = cnt["wup"] % NWU
            cnt["wup"] += 1
            S.dma("sp", lambda e: e.dma_start(out=w_up_ring[slot][:], in_=wupb[f]), reads=[b_wupb[f]], writes=[b_wup[slot]])
            return slot

        def load_wdn(m):
            slot = cnt["wdn"] % NWD
            cnt["wdn"] += 1
            S.dma("sp", lambda e: e.dma_start(out=w_dn_ring[slot][:], in_=wdnb[m]), reads=[b_wdnb[m]], writes=[b_wdn[slot]])
            return slot

        def next_bank():
            i = cnt["bk"] % 2
            cnt["bk"] += 1
            return pbank(1, i), bank[1][i]

        sel_d = din("sel", [128, 32])
        selt = sbt("selt", [128, 32], F32)
        xst = sbt("xst", [128, 2, 128], F32)
        xs_sb = sbt("xs_sb", [128, D], F32)
        hTs = sbt("hTs", [128, 8, NS], BF16)
        HbS = sbt("HbS", [128, NQ, 2, NS], BF16)
        cbufT = sbt("cbufT", [128, 2 * NF, 2 * NS], F32)
        hupS = sbt("hupS", [128, 2 * NF, NS], F32)
        actS = sbt("actS", [128, NF, NS], BF16)
        cvS = sbt("cvS", [128, 3, 2, NS], F32)
        b_sel, b_xst = S.buf("sel"), [S.buf("xst0"), S.buf("xst1")]
        b_xs, b_hs, b_HbS = S.buf("xs"), [S.buf("hs%d" % k) for k in range(8)], S.buf("HbS")
        b_cst, b_cbufT, b_hupS, b_actS = S.buf("cst"), S.buf("cbufT"), S.buf("hupS"), S.buf("actS")
        b_cvS = [[S.buf("cvS%d_%d" % (i, g)) for g in range(2)] for i in range(3)]
        b_pss = [S.buf("pss%d" % i) for i in range(4)]
        b_psd = [S.buf("psd%d" % i) for i in range(2)]
        b_pst = [S.buf("pst%d" % i) for i in range(2)]
        arena_bufs.append(b_cst)
        N = NS
        def sample_phase_A():
            S.dma("sp", lambda e: e.dma_start(out=selt[:], in_=sel_d), writes=[b_sel])
            S.dma("sp", lambda e: e.dma_start(out=xs_sb[0:N, :], in_=xs_d), writes=[b_xs])
            S.dma("sp", lambda e: e.dma_start(out=spool_d[:, 0:14, :], in_=pst_d[:, 1:15, :]), out=True)
            S.dma("sp", lambda e: e.dma_start(out=sconv_d[:, 0, :], in_=cst_d[:, 1, :]), out=True)
            norm_to_hT([(xs_sb[0:N, :], b_xs, 0, N)], V_G1, dst=hTs, dbufs=b_hs)
            hcol = b_hs
            for m in range(8):
                slot = load_win(m)
                po, pbuf = next_bank()

                def proj(e, slot=slot, po=po):
                    last = None
                    for k in range(8):
                        last = e.matmul(po[:, 0:N], lhsT=w_in_ring[slot][:, k, :], rhs=hTs[:, k, :], start=(k == 0), stop=(k == 7))
                    return last
                S.c("pe", proj, reads=[b_win[slot]] + hcol, writes=[pbuf])
                if m < 4:
                    S.c("act", lambda e, m=m, po=po: e.activation(out=uT[:, m, 0:N], in_=po[:, 0:N], func=AF.Copy), reads=[pbuf], writes=[b_uT[m]])
                else:
                    S.c("act", lambda e, m=m, po=po: e.activation(out=v_sb[:, m - 4, 16:16 + N], in_=po[:, 0:N], func=AF.Copy), reads=[pbuf], writes=[b_v[m - 4]])
            if do_s5:
                h0r_tm = scr[0:N, :, :].rearrange("p s n -> p (s n)")
                h0i_tm = x_sb[0:N, 2:4, :].rearrange("p s n -> p (s n)")
                S.dma("sp", lambda e: e.dma_start(out=h0r_tm, in_=s5re_d), writes=b_scr)
                S.dma("sp", lambda e: e.dma_start(out=h0i_tm, in_=s5im_d), writes=[b_x[2], b_x[3]])

                def trh(e):
                    last = None
                    for ri, src in ((0, h0r_tm), (1, h0i_tm)):
                        for q in range(NQ):
                            qo = (q % 4) * 4 + q // 4
                            last = e.transpose(out=PB[:, ri * 256 + q * N:ri * 256 + (q + 1) * N], in_=src[:, qo * 128:(qo + 1) * 128], identity=ident[0:N, 0:N])
                    return last
                S.c("pe", trh, reads=b_scr + [b_x[2], b_x[3], b_ident], writes=[bank[1][0]])
                h0r, h0i, hnr, hni = tA[:, :, 0:N], tB[:, :, 0:N], tC[:, :, 0:N], tD[:, :, 0:N]
                S.c("act", lambda e: e.activation(out=h0r, in_=PB[:, 0:256].rearrange("p (q n) -> p q n", q=NQ), func=AF.Copy), reads=[bank[1][0]], writes=[b_tA])
                S.c("act", lambda e: e.activation(out=h0i, in_=PB[:, 256:512].rearrange("p (q n) -> p q n", q=NQ), func=AF.Copy), reads=[bank[1][0]], writes=[b_tB])

                def bu(e):
                    last = None
                    for ri in range(2):
                        for q in range(NQ):
                            q4, kc = q // 4, q % 4
                            c0_ = q4 * 512 + (kc * 2 + ri) * N
                            last = e.matmul(PCD[:, c0_:c0_ + N], lhsT=Bst[32 * q4:32 * q4 + 32, kc, T8 - 1, ri, :],
                                            rhs=uT[32 * q4:32 * q4 + 32, kc, 0:N], start=True, stop=True, tile_position=(32 * q4, 0))
                    return last
                S.c("pe", bu, reads=[b_Bst] + b_uT, writes=[bank[2][0], bank[2][1], bank[3][0], bank[3][1]])
                al = sp_small[:, PR0 + 1, :].unsqueeze(2).broadcast_to([128, NQ, N])
                be = sp_small[:, PI0 + 1, :].unsqueeze(2).broadcast_to([128, NQ, N])
                t1 = ytmp[:, 0, 0:256].rearrange("p (q n) -> p q n", q=NQ)
                t2 = ytmp[:, 1, 0:256].rearrange("p (q n) -> p q n", q=NQ)
                TT = lambda eng, o, a, b_, op, rd, wr: S.c(eng, lambda e: e.tensor_tensor(out=o, in0=a, in1=b_, op=op), reads=rd, writes=wr)
                PSv = PCD[:, :].rearrange("p (a x) -> p a x", a=4)[:, :, 0:8 * N].rearrange("p a (k r n) -> p a k r n", k=4, r=2)
                Sre, Sim = PSv[:, :, :, 0, :], PSv[:, :, :, 1, :]
                v4 = lambda a: a.rearrange("p (a k) c -> p a k c", a=4)
                TT("dve", t1, h0r, al, ALU.mult, [b_tA, b_sps], [b_ytmp[0]])
                TT("dve", t2, h0i, be, ALU.mult, [b_tB, b_sps], [b_ytmp[1]])
                TT("dve", hnr, t1, t2, ALU.subtract, b_ytmp, [b_tC])
                TT("dve", v4(hnr), v4(hnr), Sre, ALU.add, [b_tC, bank[2][0], bank[2][1], bank[3][0], bank[3][1]], [b_tC])
                TT("dve", t1, h0i, al, ALU.mult, [b_tB, b_sps], [b_ytmp[0]])
                TT("dve", t2, h0r, be, ALU.mult, [b_tA, b_sps], [b_ytmp[1]])
                TT("dve", hni, t1, t2, ALU.add, b_ytmp, [b_tD])
                TT("dve", v4(hni), v4(hni), Sim, ALU.add, [b_tD, bank[2][0], bank[2][1], bank[3][0], bank[3][1]], [b_tD])
                S.c("dve", lambda e: e.tensor_copy(out=HbS[:, :, 0, :], in_=hnr), reads=[b_tC], writes=[b_HbS])
                S.c("dve", lambda e: e.tensor_copy(out=HbS[:, :, 1, :], in_=hni), reads=[b_tD], writes=[b_HbS])
                for ri, (src, dst_tm, dbufs, dd) in enumerate(((hnr, h0r_tm, b_scr, sre_d), (hni, h0i_tm, [b_x[2], b_x[3]], sim_d))):
                    for half in range(2):
                        def trb(e, src=src, half=half):
                            last = None
                            for qq in range(8):
                                qo = half * 8 + qq
                                qp = (qo % 4) * 4 + qo // 4
                                last = e.transpose(out=PA[0:N, qq * 128:(qq + 1) * 128], in_=src[:, qp, :], identity=ident[:])
                            return last
                        S.c("pe", trb, reads=[b_tC, b_tD, b_ident], writes=[bank[0][0], bank[0][1]])
                        S.c("act", lambda e, dst_tm=dst_tm, half=half: e.activation(out=dst_tm[:, half * 1024:(half + 1) * 1024], in_=PA[0:N, :], func=AF.Copy),
                            reads=[bank[0][0], bank[0][1]], writes=dbufs)
                    S.dma("sp", lambda e, dd=dd, dst_tm=dst_tm: e.dma_start(out=dd, in_=dst_tm), reads=dbufs, out=True)
                for kc in range(4):
                    Y2 = pbank(3, 1)

                    def ys(e, kc=kc, Y2=Y2):
                        last = None
                        for q4 in range(4):
                            q = q4 * 4 + kc
                            o = Y2[32 * q4:32 * q4 + 32, 0:N]
                            e.matmul(o, lhsT=Cst[:, q, 0, 0, :], rhs=HbS[:, q, 0, :], start=True, stop=False, tile_position=(0, 32 * q4))
                            last = e.matmul(o, lhsT=Cst[:, q, 0, 1, :], rhs=HbS[:, q, 1, :], start=False, stop=True, tile_position=(0, 32 * q4))
                        return last
                    S.c("pe", ys, reads=[b_Cst, b_HbS], writes=[bank[3][1]])
                    S.c("dve", lambda e, kc=kc, Y2=Y2: e.scalar_tensor_tensor(out=ytmp[:, 0, 0:N], in0=uT[:, kc, 0:N], scalar=vecs[:, V_DS + kc:V_DS + kc + 1],
                                                                          in1=Y2[:, 0:N], op0=ALU.mult, op1=ALU.add),
                        reads=[bank[3][1], b_uT[kc], b_vecs], writes=[b_ytmp[0]])
                    S.c("act", lambda e, kc=kc: e.activation(out=yg[:, kc, 0:N], in_=ytmp[:, 0, 0:N], func=AF.Gelu), reads=[b_ytmp[0]], writes=[b_yg[kc]])
                for m in range(4):
                    po, pbuf = next_bank()

                    def glu(e, m=m, po=po):
                        last = None
                        for k in range(4):
                            last = e.matmul(po[:, 0:N], lhsT=w_glu_sb[:, k, m * 128:(m + 1) * 128], rhs=yg[:, k, 0:N], start=(k == 0), stop=(k == 3))
                        return last
                    S.c("pe", glu, reads=[b_wglu] + b_yg, writes=[pbuf])
                    S.c("act", lambda e, po=po: e.activation(out=ytmp[:, 1, 0:N], in_=po[:, 0:N], func=AF.Sigmoid), reads=[pbuf], writes=[b_ytmp[1]])
                    S.c("dve", lambda e, m=m: e.tensor_tensor(out=hTs[:, m, :], in0=yg[:, m, 0:N], in1=ytmp[:, 1, 0:N], op=ALU.mult),
                        reads=[b_ytmp[1], b_yg[m]], writes=[b_hs[m]])
            else:
                for m in range(4):
                    S.c("dve", lambda e, m=m: e.memset(hTs[:, m, :], 0.0), writes=[b_hs[m]])
            for g, w in enumerate((2, 4, 8, 16)):
                po, pbuf = next_bank()
                for half in range(2):
                    s = half
                    S.dma("sp", lambda e, g=g, half=half, s=s: e.dma_start(
                        out=xst[0:120, s, :], in_=pst_d[half * 8:(half + 1) * 8, :, g * 128:(g + 1) * 128].rearrange("t r c -> (t r) c")), writes=[b_xst[s]])
                    S.c("pe", lambda e, g=g, half=half, s=s, po=po: e.matmul(po[:, half * 8:(half + 1) * 8], lhsT=xst[0:120, s, :], rhs=selt[0:120, g * 8:(g + 1) * 8],
                                                                        start=True, stop=True), reads=[b_xst[s], b_sel], writes=[pbuf])
                S.c("dve", lambda e, g=g, w=w: e.tensor_scalar(out=ptmp[:, 0, 0:N], in0=v_sb[:, g, 16:16 + N], scalar1=1.0 / w - 1.0, scalar2=None, op0=ALU.mult),
                    reads=[b_v[g]], writes=[b_ptmp[0]])
                S.c("dve", lambda e, w=w, po=po: e.scalar_tensor_tensor(out=pooled[:, 0:N], in0=po[:, 0:N], scalar=1.0 / w, in1=ptmp[:, 0, 0:N], op0=ALU.mult, op1=ALU.add),
                    reads=[pbuf, b_ptmp[0]], writes=[b_pooled])
                po2, pbuf2 = next_bank()
                S.c("pe", lambda e, g=g, po2=po2: e.matmul(po2[:, 0:N], lhsT=pool_w_sb[:, g, :], rhs=pooled[:, 0:N], start=True, stop=True),
                    reads=[b_poolw, b_pooled], writes=[pbuf2])
                S.c("act", lambda e, g=g, po2=po2: e.activation(out=hTs[:, 4 + g, :], in_=po2[:, 0:N], func=AF.Copy, scale=vecs[:, V_PS + g:V_PS + g + 1]),
                    reads=[pbuf2, b_vecs], writes=[b_hs[4 + g]])

            def trv(e):
                last = None
                for g in range(4):
                    last = e.transpose(out=PA[0:N, g * 128:(g + 1) * 128], in_=v_sb[:, g, 16:16 + N], identity=ident[:])
                return last
            S.c("pe", trv, reads=b_v + [b_ident], writes=[bank[0][0]])
            S.c("act", lambda e: e.activation(out=x_sb[0:N, 1, 0:512], in_=PA[0:N, 0:512], func=AF.Copy), reads=[bank[0][0]], writes=[b_x[1]])
            S.dma("sp", lambda e: e.dma_start(out=spool_d[:, 14, :], in_=x_sb[0:N, 1, 0:512]), reads=[b_x[1]], out=True)

            load_wout()

            def oproj_s(e):
                last = None
                for h in range(2):
                    for k in range(8):
                        last = e.matmul(pbank(2, h)[0:N, :], lhsT=hTs[:, k, :], rhs=w_out_sb[:, k, h * 512:(h + 1) * 512], start=(k == 0), stop=(k == 7))
                return last
            S.c("pe", oproj_s, reads=[b_wout] + hcol, writes=[bank[2][0], bank[2][1]])
            S.c("dve", lambda e: e.tensor_tensor(out=xs_sb[0:N, :], in0=xs_sb[0:N, :], in1=PC[0:N, :], op=ALU.add),
                reads=[b_xs, bank[2][0], bank[2][1]], writes=[b_xs])
            norm_to_hT([(xs_sb[0:N, :], b_xs, 0, N)], V_G2, dst=hTs, dbufs=b_hs)
            fence()
            cst_tm = av(0, 5632)
            S.dma("sp", lambda e: e.dma_start(out=cst_tm[0:32, :], in_=cst_d.rearrange("t r f -> (t r) f")), reads=[b_fence], writes=[b_cst])
            for rnd in range(3):
                c0 = rnd * 16
                ncx = min(16, 2 * NF - c0)

                def trc(e, c0=c0, ncx=ncx):
                    last = None
                    for ci in range(ncx):
                        last = e.transpose(out=PA[:, ci * 32:(ci + 1) * 32], in_=cst_tm[0:32, (c0 + ci) * 128:(c0 + ci + 1) * 128], identity=ident[0:32, 0:32])
                    return last
                S.c("pe", trc, reads=[b_cst, b_ident], writes=[bank[0][0]])
                S.c("act", lambda e, c0=c0, ncx=ncx: e.activation(out=cbufT[:, c0:c0 + ncx, :], in_=PA[:, 0:ncx * 32].rearrange("p (c n) -> p c n", c=ncx), func=AF.Copy),
                    reads=[bank[0][0]], writes=[b_cbufT])
            fence()

        def sample_up(f, slot):
            pr = f % 2
            cs = f % 3
            psu = [PA[:, pr * 512 + gv * NS:pr * 512 + (gv + 1) * NS] for gv in range(2)]

            def up(e):
                last = None
                for gv in range(2):
                    for k in range(8):
                        last = e.matmul(psu[gv], lhsT=w_up_ring[slot][:, k, gv * 128:(gv + 1) * 128], rhs=hTs[:, k, :], start=(k == 0), stop=(k == 7))
                return last
            S.c("pe", up, reads=[b_wup[slot]] + b_hs, writes=[bank[0][pr]])
            for gv in range(2):
                ch = f + gv * NF
                cw = lambda k, ch=ch: vecs[:, V_CW + ch * 3 + k:V_CW + ch * 3 + k + 1]
                cb2 = cbufT[:, ch, :].rearrange("p (t r) -> p t r", r=2)
                cc = cvS[:, cs, gv, :]
                bc = b_cvS[cs][gv]
                S.c("act", lambda e, gv=gv, ch=ch: e.activation(out=hupS[:, ch, :], in_=psu[gv], func=AF.Copy), reads=[bank[0][pr]], writes=[b_hupS])
                S.c("act", lambda e, gv=gv, ch=ch, cw=cw, cc=cc: e.activation(out=cc, in_=psu[gv], func=AF.Identity, scale=cw(2),
                                                                       bias=vecs[:, V_CB + ch:V_CB + ch + 1]), reads=[bank[0][pr], b_vecs], writes=[bc])
                S.c("dve", lambda e, cw=cw, cb2=cb2, cc=cc: e.scalar_tensor_tensor(out=cc, in0=cb2[:, :, 1], scalar=cw(1), in1=cc, op0=ALU.mult, op1=ALU.add),
                    reads=[b_cbufT, b_vecs, bc], writes=[bc])
                S.c("dve", lambda e, cw=cw, cb2=cb2, cc=cc: e.scalar_tensor_tensor(out=cc, in0=cb2[:, :, 0], scalar=cw(0), in1=cc, op0=ALU.mult, op1=ALU.add),
                    reads=[b_cbufT, b_vecs, bc], writes=[bc])
            S.c("act", lambda e: e.activation(out=cvS[:, cs, 0, :], in_=cvS[:, cs, 0, :], func=AF.Gelu), reads=[b_cvS[cs][0]], writes=[b_cvS[cs][0]])
            S.c("dve", lambda e: e.tensor_tensor(out=actS[:, f, :], in0=cvS[:, cs, 0, :], in1=cvS[:, cs, 1, :], op=ALU.mult), reads=b_cvS[cs], writes=[b_actS])

        def sample_down(m, slot):
            s_ = m % 2
            psd = pbank(2 + s_, 0)[:, 0:NS]

            def down(e):
                last = None
                for k in range(NF):
                    last = e.matmul(psd, lhsT=w_dn_ring[slot][:, k, :], rhs=actS[:, k, :], start=(k == 0), stop=(k == NF - 1))
                return last
            S.c("pe", down, reads=[b_wdn[slot], b_actS], writes=[bank[2 + s_][0]])
            S.c("act", lambda e: e.activation(out=xst[:, s_, 0:N], in_=psd, func=AF.Copy), reads=[bank[2 + s_][0]], writes=[b_xst[s_]])

        def sample_btr(m):
            s_ = m % 2
            pst_ = pbank(2 + s_, 1)[0:N, 0:128]
            S.c("pe", lambda e: e.transpose(out=pst_, in_=xst[:, s_, 0:N], identity=ident[:]),
                reads=[b_xst[s_], b_ident], writes=[bank[2 + s_][1]])
            S.c("dve", lambda e: e.tensor_tensor(out=xs_sb[0:N, m * 128:(m + 1) * 128], in0=xs_sb[0:N, m * 128:(m + 1) * 128], in1=pst_, op=ALU.add),
                reads=[b_xs, bank[2 + s_][1]], writes=[b_xs])

        def sample_phase_C():
            for rnd in range(6):
                c0 = rnd * 8
                ncx = min(8, 2 * NF - c0)
                s = rnd % 2

                def trhup(e, c0=c0, ncx=ncx):
                    last = None
                    for ci in range(ncx):
                        last = e.transpose(out=PA[0:N, ci * 128:(ci + 1) * 128], in_=hupS[:, c0 + ci, :], identity=ident[:])
                    return last
                S.c("pe", trhup, reads=[b_hupS, b_ident], writes=[bank[0][0], bank[0][1]])
                S.c("act", lambda e, s=s, ncx=ncx: e.activation(out=scr[0:N, s, 0:ncx * 128], in_=PA[0:N, 0:ncx * 128], func=AF.Copy),
                    reads=[bank[0][0], bank[0][1]], writes=[b_scr[s]])
                S.dma("sp", lambda e, s=s, c0=c0, ncx=ncx: e.dma_start(out=sconv_d[:, 1, c0 * 128:(c0 + ncx) * 128], in_=scr[0:N, s, 0:ncx * 128]),
                      reads=[b_scr[s]], out=True)
            rms_stats(1, [xs_sb[0:N, :]], [b_xs])
            S.c("dve", lambda e: e.scalar_tensor_tensor(out=scr[0:N, 0, :], in0=xs_sb[0:N, :], scalar=stat[0:N, 8:9], in1=g3b[0:N, :],
                                                         op0=ALU.mult, op1=ALU.mult), reads=[b_xs, b_stat, b_g3b], writes=[b_scr[0]])
            S.dma("sp", lambda e: e.dma_start(out=ys_d, in_=scr[0:N, 0, :]), reads=[b_scr[0]], out=True)

        def block_early(blk):
            last_blk = (blk == NBLK - 1)
            if blk > 0:
                S.c("pool", lambda e: e.tensor_copy(out=v_sb[:, :, 1:16], in_=v_sb[:, :, BT + 1:BT + 16]), reads=b_v, writes=b_v)
            all_h = flat(b_h)
            for m in range(8):
                slot = load_win(m)
                po, pbuf = next_bank()

                def proj(e, slot=slot, po=po):
                    last = None
                    for k in range(8):
                        last = e.matmul(po, lhsT=w_in_ring[slot][:, k, :], rhs=hT[:, k, :], start=(k == 0), stop=(k == 7))
                    return last
                S.c("pe", proj, reads=[b_win[slot]] + all_h, writes=[pbuf])
                if m < 4:
                    S.c("act", lambda e, m=m, po=po: e.activation(out=uT[:, m, :], in_=po, func=AF.Copy), reads=[pbuf], writes=[b_uT[m]])
                else:
                    S.c("act", lambda e, m=m, po=po: e.activation(out=v_sb[:, m - 4, 16:16 + BT], in_=po, func=AF.Copy), reads=[pbuf], writes=[b_v[m - 4]])

            for g, w in enumerate((2, 4, 8, 16)):
                vg = v_sb[:, g, :]
                nsteps = g + 1
                cur = vg
                curb = [b_v[g]]
                sh = 1
                for si in range(nsteps):
                    dst = ptmp[:, si % 2, :]
                    lo = 2 * sh
                    S.c("dve", lambda e, dst=dst, cur=cur, sh=sh: e.tensor_tensor(out=dst[:, 2 * sh - 1:16 + BT], in0=cur[:, 2 * sh - 1:16 + BT],
                                                                          in1=cur[:, sh - 1:16 + BT - sh], op=ALU.add),
                        reads=curb, writes=[b_ptmp[si % 2]])
                    cur = dst
                    curb = [b_ptmp[si % 2]]
                    sh *= 2
                S.c("dve", lambda e, cur=cur, w=w, vg=vg: e.scalar_tensor_tensor(out=pooled, in0=cur[:, 16:16 + BT], scalar=1.0 / w, in1=vg[:, 16:16 + BT],
                                                                             op0=ALU.mult, op1=ALU.subtract),
                    reads=curb + [b_v[g]], writes=[b_pooled])
                if blk == 0:
                    o2 = ptmp[:, (nsteps) % 2, :]
                    S.c("dve", lambda e, cur=cur, w=w, o2=o2: e.tensor_tensor(out=o2[:, 0:w - 1], in0=cur[:, 16:16 + w - 1], in1=V(V_IC, w - 1), op=ALU.mult),
                        reads=curb + [b_vecs], writes=[b_ptmp[nsteps % 2]])
                    S.c("dve", lambda e, w=w, o2=o2, vg=vg: e.tensor_tensor(out=pooled[:, 0:w - 1], in0=o2[:, 0:w - 1], in1=vg[:, 16:16 + w - 1], op=ALU.subtract),
                        reads=[b_ptmp[nsteps % 2], b_v[g]], writes=[b_pooled])
                po, pbuf = next_bank()
                S.c("pe", lambda e, g=g, po=po: e.matmul(po, lhsT=pool_w_sb[:, g, :], rhs=pooled, start=True, stop=True),
                    reads=[b_poolw, b_pooled], writes=[pbuf])
                S.c("act", lambda e, g=g, po=po: e.activation(out=hT[:, 4 + g, :], in_=po, func=AF.Copy, scale=vecs[:, V_PS + g:V_PS + g + 1]),
                    reads=[pbuf, b_vecs], writes=b_h[4 + g])
            if last_blk:
                S.dma("sp", lambda e: [e.dma_start(out=ppool_d[:, g * 128:(g + 1) * 128].rearrange("t c -> c t"), in_=v_sb[:, g, BT + 1:BT + 16],
                                                    allow_slow_non_contiguous=True) for g in range(4)], reads=b_v, out=True, ndma=4)


        for blk in range(nblk):
            r0 = blk * BT
            last_blk = (blk == NBLK - 1)
            xb, bx = (x_sb, b_x) if blk % 2 == 0 else (x_sb2, b_x2)
            xn_, bxn = (x_sb2, b_x2) if blk % 2 == 0 else (x_sb, b_x)
            mrg = do_sample and blk == nblk - 1
            if mrg:
                sample_phase_A()
            if blk == 0:
                for t in range(4):
                    S.dma("sp", lambda e, t=t, r0=r0, xb=xb: e.dma_start(out=xb[:, t, :], in_=x_d[r0 + t * 128:r0 + (t + 1) * 128, :]), writes=[bx[t]])
            if blk == 0:
                norm_to_hT([(xb[:, t, :], bx[t], t * 128, 128) for t in range(4)], V_G1)
            if blk == 0 or mrg:
                block_early(blk)
            if do_s5 and stage >= 1:
                PCDv = PCD[:, :].rearrange("p (a k r c) -> p a k r c", a=4, k=4, r=2)
                PCv, PDv = PCDv[:, :, :, 0, :], PCDv[:, :, :, 1, :]
                v4 = lambda a: a.rearrange("p (a k) c -> p a k c", a=4)

                def lvl0(e):
                    last = None
                    for ri in range(2):
                        for kc in range(4):
                            for j in range(T8):
                                for q4 in range(4):
                                    c0_ = q4 * 512 + (kc * 2 + ri) * CH
                                    last = e.matmul(PCD[:, c0_:c0_ + CH], lhsT=Bst[32 * q4:32 * q4 + 32, kc, j, ri, :],
                                                    rhs=uT[32 * q4:32 * q4 + 32, kc, j:BT:T8], start=(j == 0), stop=(j == T8 - 1),
                                                    tile_position=(32 * q4, 0))
                    return last
                S.c("pe", lvl0, reads=[b_Bst] + b_uT, writes=[bank[2][0], bank[2][1], bank[3][0], bank[3][1]])
                if stage >= 2:
                    bC, bD = [bank[2][0], bank[2][1]], [bank[3][0], bank[3][1]]
                    A1, B1, C1, D1 = tA[:, :, 1:CH + 1], tB[:, :, 1:CH + 1], tC[:, :, 1:CH + 1], tD[:, :, 1:CH + 1]
                    TT = lambda eng, o, a, b_, op, rd, wr: S.c(eng, lambda e: e.tensor_tensor(out=o, in0=a, in1=b_, op=op), reads=rd, writes=wr)
                    TT("dve", v4(A1), PCv, v4(tabc[:]), ALU.mult, bC + bD + [b_tab], [b_tA])
                    TT("dve", v4(B1), PDv, v4(tabs[:]), ALU.mult, bC + bD + [b_tab], [b_tB])
                    TT("dve", A1, A1, B1, ALU.add, [b_tA, b_tB], [b_tA])
                    TT("dve", v4(C1), PDv, v4(tabc[:]), ALU.mult, bC + bD + [b_tab], [b_tC])
                    TT("dve", v4(D1), PCv, v4(tabs[:]), ALU.mult, bC + bD + [b_tab], [b_tD])
                    TT("dve", C1, C1, D1, ALU.subtract, [b_tC, b_tD], [b_tC])
                    S.c("pool", lambda e: e.tensor_copy(out=tA[:, :, 0], in_=sp_small[:, GCR, :]), reads=[b_gc], writes=[b_tA])
                    S.c("pool", lambda e: e.tensor_copy(out=tC[:, :, 0], in_=sp_small[:, GCI, :]), reads=[b_gc], writes=[b_tC])
                    fl = lambda a: a.rearrange("p q c -> p (q c)")
                    S.c("dve", lambda e: e.tensor_tensor_scan(out=fl(tB), data0=fl(rdec[:]), data1=fl(tA), initial=0.0, op0=ALU.mult, op1=ALU.add),
                        reads=[b_rdec, b_tA], writes=[b_tB])
                    S.c("dve", lambda e: e.tensor_tensor_scan(out=fl(tD), data0=fl(rdec[:]), data1=fl(tC), initial=0.0, op0=ALU.mult, op1=ALU.add),
                        reads=[b_rdec, b_tC], writes=[b_tD])
                    S.c("pool", lambda e: e.tensor_copy(out=sp_small[:, GCR, :], in_=tB[:, :, CH]), reads=[b_tB], writes=[b_gc])
                    S.c("pool", lambda e: e.tensor_copy(out=sp_small[:, GCI, :], in_=tD[:, :, CH]), reads=[b_tD], writes=[b_gc])
                    TT("dve", A1, B1, tabc[:], ALU.mult, [b_tB, b_tab], [b_tA])
                    TT("dve", C1, D1, tabs[:], ALU.mult, [b_tD, b_tab], [b_tC])
                    TT("dve", Hb[:, :, 0, 1:CH + 1], A1, C1, ALU.subtract, [b_tA, b_tC], [b_Hb])
                    if last_blk:
                        TT("pool", sp_small[:, HFR, :], tA[:, :, CH], tC[:, :, CH], ALU.subtract, [b_tA, b_tC], [b_hfin])
                    TT("dve", A1, B1, tabs[:], ALU.mult, [b_tB, b_tab], [b_tA])
                    TT("dve", C1, D1, tabc[:], ALU.mult, [b_tD, b_tab], [b_tC])
                    TT("dve", Hb[:, :, 1, 1:CH + 1], A1, C1, ALU.add, [b_tA, b_tC], [b_Hb])
                    if last_blk:
                        TT("pool", sp_small[:, HFI, :], tA[:, :, CH], tC[:, :, CH], ALU.add, [b_tA, b_tC], [b_hfin])
                        S.dma("sp", lambda e: e.dma_start(out=pre_d.rearrange("q p -> p q"), in_=sp_small[:, HFR, :], allow_slow_non_contiguous=True),
                              reads=[b_hfin], out=True)
                        S.dma("sp", lambda e: e.dma_start(out=pim_d.rearrange("q p -> p q"), in_=sp_small[:, HFI, :], allow_slow_non_contiguous=True),
                              reads=[b_hfin], out=True)
                if stage >= 4:
                    for kc in range(4):
                        pi = 2 + (kc % 2)
                        Y1, Y2 = pbank(pi, 0), pbank(pi, 1)

                        def ystage(e, kc=kc, Y1=Y1, Y2=Y2):
                            last = None
                            for jp in range(T8):
                                for j in range(jp + 1):
                                    last = e.matmul(Y1[:, jp:BT:T8], lhsT=Kblk[:, kc, jp - j, :], rhs=uT[:, kc, j:BT:T8], start=(j == 0), stop=(j == jp))
                            for jp in range(T8):
                                for q4 in range(4):
                                    q = q4 * 4 + kc
                                    o = Y2[32 * q4:32 * q4 + 32, jp:BT:T8]
                                    e.matmul(o, lhsT=Cst[:, q, jp + 1, 0, :], rhs=Hb[:, q, 0, 0:CH], start=True, stop=False, tile_position=(0, 32 * q4))
                                    last = e.matmul(o, lhsT=Cst[:, q, jp + 1, 1, :], rhs=Hb[:, q, 1, 0:CH], start=False, stop=True, tile_position=(0, 32 * q4))
                            return last
                        S.c("pe", ystage, reads=[b_Kblk, b_Cst, b_uT[kc], b_Hb, b_Hb0], writes=[bank[pi][0], bank[pi][1]])
                        s = kc % 2
                        S.c("act", lambda e, s=s, Y1=Y1: e.activation(out=ytmp[:, s, :], in_=Y1, func=AF.Copy), reads=[bank[pi][0]], writes=[b_ytmp[s]])
                        S.c("dve", lambda e, s=s, Y2=Y2: e.tensor_tensor(out=ytmp[:, s, :], in0=ytmp[:, s, :], in1=Y2, op=ALU.add),
                            reads=[b_ytmp[s], bank[pi][1]], writes=[b_ytmp[s]])
                        S.c("dve", lambda e, s=s, kc=kc: e.scalar_tensor_tensor(out=ytmp[:, s, :], in0=uT[:, kc, :], scalar=vecs[:, V_DS + kc:V_DS + kc + 1],
                                                                            in1=ytmp[:, s, :], op0=ALU.mult, op1=ALU.add),
                            reads=[b_ytmp[s], b_uT[kc], b_vecs], writes=[b_ytmp[s]])
                        S.c("act", lambda e, s=s, kc=kc: e.activation(out=yg[:, kc, :], in_=ytmp[:, s, :], func=AF.Gelu), reads=[b_ytmp[s]], writes=[b_yg[kc]])
                if stage >= 5:
                    S.c("pool", lambda e: e.tensor_copy(out=Hb[:, :, :, 0], in_=Hb[:, :, :, CH]), reads=[b_Hb], writes=[b_Hb0])
                    for m in range(4):
                        po, pbuf = next_bank()

                        def glu(e, m=m, po=po):
                            last = None
                            for k in range(4):
                                last = e.matmul(po, lhsT=w_glu_sb[:, k, m * 128:(m + 1) * 128], rhs=yg[:, k, :], start=(k == 0), stop=(k == 3))
                            return last
                        S.c("pe", glu, reads=[b_wglu] + b_yg, writes=[pbuf])
                        s = m % 2
                        S.c("act", lambda e, s=s, po=po: e.activation(out=ytmp[:, s, :], in_=po, func=AF.Sigmoid), reads=[pbuf], writes=[b_ytmp[s]])
                        S.c("dve", lambda e, s=s, m=m: e.tensor_tensor(out=hT[:, m, :], in0=yg[:, m, :], in1=ytmp[:, s, :], op=ALU.mult),
                            reads=[b_ytmp[s], b_yg[m]], writes=b_h[m])
                if stage < 5:
                    for m in range(4):
                        S.c("dve", lambda e, m=m: e.memset(hT[:, m, :], 0.0), writes=b_h[m])
            else:
                for m in range(4):
                    S.c("dve", lambda e, m=m: e.memset(hT[:, m, :], 0.0), writes=b_h[m])

            load_wout()
            S.c("dve", lambda e: e.memset(stat[:, 0:8], 0.0), writes=[b_stat])
            for t in range(4):
                pi = 2 + (t % 2)

                def oproj(e, t=t, pi=pi):
                    last = None
                    for h in range(2):
                        for k in range(8):
                            last = e.matmul(pbank(pi, h), lhsT=hT[:, k, t * 128:(t + 1) * 128], rhs=w_out_sb[:, k, h * 512:(h + 1) * 512],
                                            start=(k == 0), stop=(k == 7))
                    return last
                S.c("pe", oproj, reads=[b_wout] + [b_h[k][t] for k in range(8)], writes=[bank[pi][0], bank[pi][1]])
                S.c("dve", lambda e, t=t, pi=pi, xb=xb: e.tensor_tensor(out=xb[:, t, :], in0=xb[:, t, :], in1=PSn[pi][:, :], op=ALU.add),
                    reads=[bx[t], bank[pi][0], bank[pi][1]], writes=[bx[t]])
            norm_to_hT([(xb[:, t, :], bx[t], t * 128, 128) for t in range(4)], V_G2, premem=True)
            fence()
            if blk + 1 < nblk:
                for t in range(4):
                    S.dma("sp", lambda e, t=t, r1=r0 + BT, xn_=xn_: e.dma_start(out=xn_[:, t, :], in_=x_d[r1 + t * 128:r1 + (t + 1) * 128, :]), writes=[bxn[t]])
            all_h = flat(b_h)
            if blk > 0:
                cwv = vecs[:, V_CW:V_CW + 132].rearrange("p (c k) -> p c k", k=3)
                S.c("pool", lambda e: e.tensor_tensor(out=hfix[:], in0=hhalo[:], in1=cwv[:, :, 0:1].broadcast_to([128, 2 * NF, 2]), op=ALU.mult),
                    reads=[b_hhalo, b_vecs], writes=[b_hfix])
                S.c("pool", lambda e: e.tensor_tensor(out=hfix2[:], in0=hhalo[:, :, 1], in1=cwv[:, :, 1], op=ALU.mult),
                    reads=[b_hhalo, b_vecs], writes=[b_hfix2])
                S.c("pool", lambda e: e.tensor_tensor(out=hfix[:, :, 0], in0=hfix[:, :, 0], in1=hfix2[:], op=ALU.add),
                    reads=[b_hfix, b_hfix2], writes=[b_hfix])
            def ffn_tail(f):
                cs = f % 3
                S.c("act", lambda e, cs=cs: e.activation(out=cvt3[:, cs, 0, :], in_=cvt3[:, cs, 0, :], func=AF.Gelu), reads=[b_cvt3[cs][0]], writes=[b_cvt3[cs][0]])
                S.c("pool", lambda e, f=f, cs=cs: e.tensor_tensor(out=actT[:, f, :], in0=cvt3[:, cs, 0, :], in1=cvt3[:, cs, 1, :], op=ALU.mult),
                    reads=b_cvt3[cs], writes=[b_act[f]])

            for f in range(NF):
                slot = load_wup(f)
                pi = 1 + (f % 3)
                cs = f % 3

                def up(e, slot=slot, pi=pi):
                    last = None
                    for gv in range(2):
                        for k in range(8):
                            last = e.matmul(pbank(pi, gv), lhsT=w_up_ring[slot][:, k, gv * 128:(gv + 1) * 128], rhs=hT[:, k, :], start=(k == 0), stop=(k == 7))
                    return last
                S.c("pe", up, reads=[b_wup[slot]] + all_h, writes=[bank[pi][0], bank[pi][1]])
                if mrg:
                    sample_up(f, slot)
                for gv in range(2):
                    ch = f + gv * NF
                    cw2 = vecs[:, V_CW + ch * 3 + 2:V_CW + ch * 3 + 3]
                    S.c("act", lambda e, gv=gv, ch=ch, cw2=cw2, cs=cs, pi=pi: e.activation(out=cvt3[:, cs, gv, :], in_=pbank(pi, gv), func=AF.Identity, scale=cw2,
                                                                                     bias=vecs[:, V_CB + ch:V_CB + ch + 1]),
                        reads=[bank[pi][gv], b_vecs], writes=[b_cvt3[cs][gv]])
                if f > 0:
                    ffn_tail(f - 1)
                for gv in range(2):
                    ch = f + gv * NF
                    cw = lambda k, ch=ch: vecs[:, V_CW + ch * 3 + k:V_CW + ch * 3 + k + 1]
                    cc = cvt3[:, cs, gv, :]
                    pg = pbank(pi, gv)
                    bc = b_cvt3[cs][gv]
                    S.c("dve", lambda e, cc=cc, pg=pg, cw=cw: e.scalar_tensor_tensor(out=cc[:, 1:BT], in0=pg[:, 0:BT - 1], scalar=cw(1), in1=cc[:, 1:BT],
                                                                                 op0=ALU.mult, op1=ALU.add), reads=[bank[pi][gv], b_vecs, bc], writes=[bc])
                    S.c("dve", lambda e, cc=cc, pg=pg, cw=cw: e.scalar_tensor_tensor(out=cc[:, 2:BT], in0=pg[:, 0:BT - 2], scalar=cw(0), in1=cc[:, 2:BT],
                                                                                 op0=ALU.mult, op1=ALU.add), reads=[bank[pi][gv], b_vecs, bc], writes=[bc])
                    if blk > 0:
                        S.c("pool", lambda e, cc=cc, ch=ch: e.tensor_tensor(out=cc[:, 0:2], in0=cc[:, 0:2], in1=hfix[:, ch, :], op=ALU.add),
                            reads=[b_hfix, bc], writes=[bc])
                S.c("dve", lambda e, f=f, pi=pi: e.tensor_copy(out=hhalo[:, f:2 * NF:NF, :], in_=PSn[pi][:, :].rearrange("p (g n) -> p g n", g=2)[:, :, BT - 2:BT]),
                    reads=[bank[pi][0], bank[pi][1]], writes=[b_hhalo])
            ffn_tail(NF - 1)
            if last_blk:
                S.dma("sp", lambda e: [e.dma_start(out=pconv_d[r, :].rearrange("(c p) -> p c", p=128), in_=hhalo[:, :, r], allow_slow_non_contiguous=True)
                                        for r in range(2)], reads=[b_hhalo], out=True, ndma=2)
            def emit_down(m):
                slot = load_wdn(m)
                po, pbuf = next_bank()
                if mrg:
                    sample_down(m, slot)

                def down(e, slot=slot, po=po):
                    last = None
                    for k in range(NF):
                        last = e.matmul(po, lhsT=w_dn_ring[slot][:, k, :], rhs=actT[:, k, :], start=(k == 0), stop=(k == NF - 1))
                    return last
                S.c("pe", down, reads=[b_wdn[slot]] + b_act, writes=[pbuf])
                s = m % 2
                S.c("act", lambda e, s=s, po=po: e.activation(out=scr[:, s, 0:BT], in_=po, func=AF.Copy), reads=[pbuf], writes=[b_scr[s]])

            def emit_btr(m):
                s = m % 2
                hb = m % 2

                def btr(e, s=s, hb=hb):
                    last = None
                    for t in range(4):
                        last = e.transpose(out=PA[:, hb * 512 + t * 128:hb * 512 + (t + 1) * 128], in_=scr[:, s, t * 128:(t + 1) * 128], identity=ident[:])
                    return last
                S.c("pe", btr, reads=[b_scr[s], b_ident], writes=[bank[0][hb]])
                if mrg:
                    sample_btr(m)
                S.c("dve", lambda e, m=m, hb=hb, xb=xb: e.tensor_tensor(out=xb[:, :, m * 128:(m + 1) * 128], in0=xb[:, :, m * 128:(m + 1) * 128],
                                                             in1=PA[:, hb * 512:(hb + 1) * 512].rearrange("p (t n) -> p t n", t=4), op=ALU.add),
                    reads=bx + [bank[0][hb]], writes=bx)
            emit_down(0)
            if do_s5 and not last_blk:
                cWb = sp_small[:, CW, :].unsqueeze(2).broadcast_to([128, NQ, CH])
                sWb = sp_small[:, SW, :].unsqueeze(2).broadcast_to([128, NQ, CH])
                u1 = cvt3[:, 0, :, :].rearrange("p g n -> p (g n)").rearrange("p (q c) -> p q c", q=NQ)
                u2 = cvt3[:, 1, :, :].rearrange("p g n -> p (g n)").rearrange("p (q c) -> p q c", q=NQ)
                bu1, bu2 = b_cvt3[0], b_cvt3[1]
                TTp = lambda o, a_, b_, op, rd, wr: S.c("pool", lambda e: e.tensor_tensor(out=o, in0=a_, in1=b_, op=op), reads=rd, writes=wr)
                TTp(u1, tabc[:], sWb, ALU.mult, [b_tab, b_sps], bu1)
                TTp(u2, tabs[:], sWb, ALU.mult, [b_tab, b_sps], bu2)
                TTp(tabc[:], tabc[:], cWb, ALU.mult, [b_tab, b_sps], [b_tab])
                TTp(tabc[:], tabc[:], u2, ALU.subtract, [b_tab] + bu2, [b_tab])
                TTp(tabs[:], tabs[:], cWb, ALU.mult, [b_tab, b_sps], [b_tab])
                TTp(tabs[:], tabs[:], u1, ALU.add, [b_tab] + bu1, [b_tab])
            for m in range(8):
                if m + 1 < 8:
                    emit_down(m + 1)
                emit_btr(m)
            fence()
            if blk + 1 < nblk:
                norm_to_hT([(xn_[:, t, :], bxn[t], t * 128, 128) for t in range(4)], V_G1)
                if not (do_sample and blk + 1 == nblk - 1):
                    block_early(blk + 1)
            if mrg:
                sample_phase_C()
            rms_stats(4, [xb[:, t, :] for t in range(4)], bx)
            for t in range(4):
                s = t % 2
                S.c("dve", lambda e, t=t, s=s, xb=xb: e.scalar_tensor_tensor(out=scr[:, s, :], in0=xb[:, t, :], scalar=stat[:, 8 + t:9 + t], in1=g3b[:],
                                                                  op0=ALU.mult, op1=ALU.mult), reads=[bx[t], b_stat, b_g3b], writes=[b_scr[s]])
                S.dma("sp", lambda e, t=t, s=s, r0=r0: e.dma_start(out=y_d[r0 + t * 128:r0 + (t + 1) * 128, :], in_=scr[:, s, :]), reads=[b_scr[s]], out=True)

        S.emit(st)
        build_program.stats = S.stats
    return nc


_CACHE = {}


def _pack_inputs(inp):
    f32 = np.float32
    A = lambda a: np.ascontiguousarray(np.asarray(a, dtype=f32))
    vecs = np.zeros((128, NV), f32)
    vecs[:, V_G1:V_G1 + 8] = A(inp["norm_mix_g"])[0].reshape(8, 128).T
    vecs[:, V_G2:V_G2 + 8] = A(inp["norm_ffn_g"])[0].reshape(8, 128).T
    vecs[:, V_PS:V_PS + 4] = A(inp["pool_scale"])[0].reshape(4, 128).T
    vecs[:, V_DS:V_DS + 4] = A(inp["s5_d"])[0].reshape(4, 128).T
    vecs[:, V_CW:V_CW + 132] = A(inp["ffn_conv_w"])[0].reshape(3, 44, 128).transpose(2, 1, 0).reshape(128, 132)
    vecs[:, V_CB:V_CB + 44] = A(inp["ffn_conv_b"])[0].reshape(44, 128).T
    vecs[:, V_IC:V_IC + 16] = (1.0 / np.arange(1, 17, dtype=np.float64)).astype(f32)[None, :]
    vecs[:, V_IO:V_IO + 64] = np.arange(64, dtype=f32)[None, :]
    s5p = np.zeros((128, NSP), f32)
    perm = np.array([(qp % 4) * 4 + qp // 4 for qp in range(16)])
    lay = lambda a: A(a)[0].reshape(16, 2, 64)[perm].transpose(1, 2, 0).reshape(128, 16)
    s5p[:, SP_ARE:SP_ARE + 16] = lay(inp["s5_a_re"])
    s5p[:, SP_AIM:SP_AIM + 16] = lay(inp["s5_a_im"])
    s5p[:, SP_LDT:SP_LDT + 16] = np.broadcast_to(A(inp["s5_log_dt"])[0].reshape(16, 2, 1)[perm], (16, 2, 64)).transpose(1, 2, 0).reshape(128, 16)
    for off, key in ((SP_BRE, "s5_b_re"), (SP_BIM, "s5_b_im")):
        b = A(inp[key])[0].reshape(16, 2, 64, 16)[perm]
        o = np.zeros((2, 64, 16, 2, 16), f32)
        for g2 in range(2):
            o[g2, :, :, g2, :] = b[:, g2].transpose(1, 0, 2)
        s5p[:, off:off + 512] = o.reshape(128, 512)
    for off, key in ((SP_CRE, "s5_c_re"), (SP_CIM, "s5_c_im")):
        c = A(inp[key])[0].reshape(16, 2, 16, 64)[perm]
        o = np.zeros((2, 64, 16, 2, 16), f32)
        for g2 in range(2):
            o[g2, :, :, g2, :] = c[:, g2].transpose(2, 0, 1)
        s5p[:, off:off + 512] = o.reshape(128, 512)
    sel = np.zeros((128, 32), f32)
    for g, w in enumerate((2, 4, 8, 16)):
        for t in range(8):
            for r in range(16 - w, 15):
                sel[t * 15 + r, g * 8 + t] = 1.0
    shared = {
        "w_in": A(inp["w_in"])[0], "w_glu": A(inp["s5_w_glu"])[0], "pool_w": A(inp["pool_w"])[0].reshape(512, 128),
        "w_out": A(inp["w_out"])[0], "w_up": A(inp["ffn_w_up"])[0], "w_down": A(inp["ffn_w_down"])[0],
        "vecs": vecs, "g3b": np.ascontiguousarray(np.broadcast_to(A(inp["norm_final_g"])[None, :], (128, D))),
        "ident": np.eye(128, dtype=f32), "s5p": s5p, "sel": sel,
    }
    maps = []
    for i in range(NCORES):
        sl = slice(i * NS, (i + 1) * NS)
        m = dict(shared)
        m["x"] = A(inp["x_prompt"])[i]
        m["xs"] = A(inp["x_sample"])[sl, 0, :]
        m["s5re"] = A(inp["state_s5_re"])[0, sl].reshape(NS, 2048)
        m["s5im"] = A(inp["state_s5_im"])[0, sl].reshape(NS, 2048)
        m["pst"] = A(inp["state_pool"])[0, sl]
        m["cst"] = A(inp["state_ffn_conv"])[0, sl]
        maps.append(m)
    return maps


def kernel(**inputs):
    if "nc" not in _CACHE:
        _CACHE["nc"] = build_program()
    nc = _CACHE["nc"]
    maps = _pack_inputs(inputs)
    res = run_bass_kernel_spmd(nc, maps, core_ids=list(range(NCORES)))
    R = res.results
    f32 = np.float32
    y_prompt = np.stack([R[i]["y"] for i in range(NCORES)]).astype(f32)
    y_sample = np.concatenate([R[i]["ys"] for i in range(NCORES)])[:, None, :].astype(f32)
    inv = np.array([(q % 4) * 4 + q // 4 for q in range(16)])
    p_re = np.stack([R[i]["p_re"][inv].reshape(NG, NP_) for i in range(NCORES)])[None].astype(f32)
    p_im = np.stack([R[i]["p_im"][inv].reshape(NG, NP_) for i in range(NCORES)])[None].astype(f32)
    p_pool = np.stack([R[i]["p_pool"] for i in range(NCORES)])[None].astype(f32)
    p_conv = np.stack([R[i]["p_conv"] for i in range(NCORES)])[None].astype(f32)
    s_re = np.concatenate([R[i]["s_re"].reshape(NS, NG, NP_) for i in range(NCORES)])[None].astype(f32)
    s_im = np.concatenate([R[i]["s_im"].reshape(NS, NG, NP_) for i in range(NCORES)])[None].astype(f32)
    s_pool = np.concatenate([R[i]["s_pool"] for i in range(NCORES)])[None].astype(f32)
    s_conv = np.concatenate([R[i]["s_conv"] for i in range(NCORES)])[None].astype(f32)
    return (y_prompt, y_sample, p_re, p_im, p_pool, p_conv, s_re, s_im, s_pool, s_conv)
```
